# Optimizing a Trainium2 kernel written in Bass

```python
import jax, jax.numpy as jnp
from jax import lax
import numpy as np

D_MODEL = 2048
BATCH = 4
SEQ = 2048
DEPTH = 1
DEC_BATCH = 128
DEC_SEQ = 4
PAST_LEN = 16384
PAGE_SIZE = 128

N_RET_HEADS = 8
RET_DK = D_MODEL // N_RET_HEADS
RET_DV = 2 * RET_DK
D_RET_QK = N_RET_HEADS * RET_DK
D_RET_V = N_RET_HEADS * RET_DV
RET_CHUNK = 128
ROPE_BASE = 10000.0
D_CONV = D_MODEL
CONV_W = 3
D_FF = 5632
FFN_CONV_W = 3
EPS = 1e-6
D_IN = 2 * D_RET_QK + 2 * D_RET_V + 3 * D_CONV + 2 * D_MODEL

kernel_name = "retnet_shortconv_convffn_hybrid_step"


def rmsnorm(x, g):
    xf = x.astype(jnp.float32)
    y = xf * lax.rsqrt(jnp.mean(xf * xf, axis=-1, keepdims=True) + EPS)
    return (y * g.astype(jnp.float32)).astype(x.dtype)


def rotary(x, pos):
    half = x.shape[-1] // 2
    inv = ROPE_BASE ** (-jnp.arange(half, dtype=jnp.float32) / half)
    ang = pos.astype(jnp.float32)[:, None] * inv[None, :]
    cos = jnp.cos(ang)[None, :, None, :]
    sin = jnp.sin(ang)[None, :, None, :]
    xf = x.astype(jnp.float32)
    x1, x2 = xf[..., :half], xf[..., half:]
    return jnp.concatenate([x1 * cos - x2 * sin, x2 * cos + x1 * sin], axis=-1)


def causal_dwconv(u, buf, w):
    T = u.shape[1]
    width = w.shape[0]
    up = jnp.concatenate([buf.astype(u.dtype), u], axis=1)
    out = sum(w[i] * up[:, i:i + T] for i in range(width))
    return out, up[:, T:]


def retention_chunk(S, q, k, v, log_g):
    C = q.shape[2]
    i = jnp.arange(C, dtype=jnp.float32)
    diff = i[:, None] - i[None, :]
    causal = diff >= 0
    decay = jnp.where(causal[None], jnp.exp(log_g[:, None, None] * jnp.where(causal, diff, 0.0)[None]), 0.0)
    scores = jnp.einsum('bhid,bhjd->bhij', q, k) * decay[None]
    inner = jnp.einsum('bhij,bhje->bhie', scores, v)
    cross = jnp.einsum('bhid,bhde->bhie', q, S) * jnp.exp(log_g[:, None] * (i + 1.0))[None, :, :, None]
    k_dec = k * jnp.exp(log_g[:, None] * (C - 1.0 - i))[None, :, :, None]
    S_new = jnp.exp(log_g * C)[None, :, None, None] * S + jnp.einsum('bhjd,bhje->bhde', k_dec, v)
    return S_new, inner + cross


def retention(q, k, v, s0, chunk, log_g):
    B, H, T, _ = q.shape
    n = T // chunk

    def blocks(a):
        return jnp.moveaxis(a.reshape(B, H, n, chunk, a.shape[-1]), 2, 0)

    def step(S, qkv):
        return retention_chunk(S, qkv[0], qkv[1], qkv[2], log_g)

    s_new, o = lax.scan(step, s0, (blocks(q), blocks(k), blocks(v)))
    o = jnp.moveaxis(o, 0, 2).reshape(B, H, T, v.shape[-1])
    return o, s_new


def hybrid_layer(x, pos, s_ret, s_conv, s_ffn, chunk,
                 g_pre_mix, w_in, conv_w, p_ret, p_conv, w_o, g_post_mix,
                 g_pre_ffn, w_up, w_gate, ffn_conv_w, ffn_conv_b, w_down, g_post_ffn):
    B, T, _ = x.shape
    h = rmsnorm(x, g_pre_mix)
    z = h @ w_in
    cuts = np.cumsum([D_RET_QK, D_RET_QK, D_RET_V, D_RET_V, D_CONV, D_CONV, D_CONV, D_MODEL]).tolist()
    q, k, v, g_ret, gb, gc, hc, gate_a, gate_b = jnp.split(z, cuts, axis=-1)

    log_g = jnp.log1p(-jnp.exp2(-5.0 - jnp.arange(N_RET_HEADS, dtype=jnp.float32)))
    qr = rotary(q.reshape(B, T, N_RET_HEADS, RET_DK), pos)
    kr = rotary(k.reshape(B, T, N_RET_HEADS, RET_DK), pos) * (RET_DK ** -0.5)
    vh = v.reshape(B, T, N_RET_HEADS, RET_DV).astype(jnp.float32)
    o, s_ret_new = retention(qr.transpose(0, 2, 1, 3), kr.transpose(0, 2, 1, 3), vh.transpose(0, 2, 1, 3),
                             s_ret.astype(jnp.float32), chunk, log_g)
    o = o * lax.rsqrt(jnp.mean(o * o, axis=-1, keepdims=True) + EPS)
    o = o.transpose(0, 2, 1, 3).reshape(B, T, D_RET_V).astype(x.dtype)
    y_a = (jax.nn.silu(g_ret) * o) @ p_ret

    conv_out, s_conv_new = causal_dwconv(gc * hc, s_conv, conv_w)
    y_b = (gb * conv_out) @ p_conv

    m = jax.nn.sigmoid(gate_a) * y_a + jax.nn.sigmoid(gate_b) * y_b
    x = x + rmsnorm(m @ w_o, g_post_mix)

    h2 = rmsnorm(x, g_pre_ffn)
    u, s_ffn_new = causal_dwconv(h2 @ w_up, s_ffn, ffn_conv_w)
    f = (jax.nn.gelu(u + ffn_conv_b, approximate=True) * (h2 @ w_gate)) @ w_down
    x = x + rmsnorm(f, g_post_ffn)
    return x, s_ret_new.astype(s_ret.dtype), s_conv_new.astype(s_conv.dtype), s_ffn_new.astype(s_ffn.dtype)


def setup_inputs(seed: int = 0) -> dict:
    key = jax.random.key(seed)
    ks = jax.random.split(key, 20)
    f32 = jnp.float32

    def nrm(k, shape, scale):
        return jax.random.normal(k, shape, f32) * scale

    def gain(k, n):
        return 1.0 + 0.1 * jax.random.normal(k, (DEPTH, n), f32)

    return {
        "x_prompt": nrm(ks[0], (BATCH, SEQ, D_MODEL), 1.0),
        "x_sample": nrm(ks[1], (DEC_BATCH, DEC_SEQ, D_MODEL), 1.0),
        "state_ret": nrm(ks[2], (DEPTH, DEC_BATCH, N_RET_HEADS, RET_DK, RET_DV), 0.05),
        "state_conv": nrm(ks[3], (DEPTH, DEC_BATCH, CONV_W - 1, D_CONV), 1.0),
        "state_ffn": nrm(ks[4], (DEPTH, DEC_BATCH, FFN_CONV_W - 1, D_FF), 1.0),
        "g_pre_mix": gain(ks[5], D_MODEL),
        "w_in": nrm(ks[6], (DEPTH, D_MODEL, D_IN), D_MODEL ** -0.5),
        "conv_w": nrm(ks[7], (DEPTH, CONV_W, D_CONV), CONV_W ** -0.5),
        "p_ret": nrm(ks[8], (DEPTH, D_RET_V, D_MODEL), D_RET_V ** -0.5),
        "p_conv": nrm(ks[9], (DEPTH, D_CONV, D_MODEL), D_CONV ** -0.5),
        "w_o": nrm(ks[10], (DEPTH, D_MODEL, D_MODEL), D_MODEL ** -0.5),
        "g_post_mix": gain(ks[11], D_MODEL),
        "g_pre_ffn": gain(ks[12], D_MODEL),
        "w_up": nrm(ks[13], (DEPTH, D_MODEL, D_FF), D_MODEL ** -0.5),
        "w_gate": nrm(ks[14], (DEPTH, D_MODEL, D_FF), D_MODEL ** -0.5),
        "ffn_conv_w": nrm(ks[15], (DEPTH, FFN_CONV_W, D_FF), FFN_CONV_W ** -0.5),
        "ffn_conv_b": nrm(ks[16], (DEPTH, D_FF), 0.02),
        "w_down": nrm(ks[17], (DEPTH, D_FF, D_MODEL), D_FF ** -0.5),
        "g_post_ffn": gain(ks[18], D_MODEL),
    }


def reference(x_prompt, x_sample, state_ret, state_conv, state_ffn,
              g_pre_mix, w_in, conv_w, p_ret, p_conv, w_o, g_post_mix,
              g_pre_ffn, w_up, w_gate, ffn_conv_w, ffn_conv_b, w_down, g_post_ffn):
    pos_p = jnp.arange(SEQ, dtype=jnp.int32)
    pos_s = PAST_LEN + jnp.arange(DEC_SEQ, dtype=jnp.int32)
    yp, ys = x_prompt, x_sample
    rp, cp, fp, rs, cs, fs = [], [], [], [], [], []
    for l in range(DEPTH):
        w = (g_pre_mix[l], w_in[l], conv_w[l], p_ret[l], p_conv[l], w_o[l], g_post_mix[l],
             g_pre_ffn[l], w_up[l], w_gate[l], ffn_conv_w[l], ffn_conv_b[l], w_down[l], g_post_ffn[l])
        z_ret = jnp.zeros((BATCH, N_RET_HEADS, RET_DK, RET_DV), state_ret.dtype)
        z_conv = jnp.zeros((BATCH, CONV_W - 1, D_CONV), state_conv.dtype)
        z_ffn = jnp.zeros((BATCH, FFN_CONV_W - 1, D_FF), state_ffn.dtype)
        yp, a, b, c = hybrid_layer(yp, pos_p, z_ret, z_conv, z_ffn, RET_CHUNK, *w)
        ys, d, e, f = hybrid_layer(ys, pos_s, state_ret[l], state_conv[l], state_ffn[l], DEC_SEQ, *w)
        rp.append(a); cp.append(b); fp.append(c)
        rs.append(d); cs.append(e); fs.append(f)
    return (yp, ys, jnp.stack(rp), jnp.stack(cp), jnp.stack(fp), jnp.stack(rs), jnp.stack(cs), jnp.stack(fs))
```

```python
import math
import numpy as np
import concourse.bass as bass
import concourse.mybir as mybir
from concourse.bass_utils import run_bass_kernel_spmd

F32 = mybir.dt.float32
BF16 = mybir.dt.bfloat16
ALU = mybir.AluOpType
AF = mybir.ActivationFunctionType

D = 2048
NH = 8
DK = 256
DV = 512
DFF = 5632
NFF = DFF // 128
EPS = 1e-6
NCORES = 8
NPRE = 896
NMAIN = 1152
NSAMP = 64
NB = 16
TOFF = 1024.0
LOGG = [math.log1p(-2.0 ** (-5 - h)) for h in range(NH)]
C_Q, C_K, C_V, C_G, C_GB, C_GC, C_HC, C_GA, C_GBT = 0, 2048, 4096, 8192, 12288, 14336, 16384, 18432, 20480

PASSES = [
    dict(src="xp", row0=0, nch=7, kv=True, sample=False, t0=0, out0=None),
    dict(src="xm", row0=0, nch=5, kv=False, sample=False, t0=896, out0=0),
    dict(src="xm", row0=640, nch=4, kv=False, sample=True, t0=896 + 640, out0=640),
]


class _Op:
    __slots__ = ("eng", "fn", "deps", "signal", "ev", "dkey")


class Prog:
    def __init__(self, nc):
        self.nc = nc
        self.ops = []
        self.last_w = {}
        self.readers = {}
        self.dkeys = []

    def add(self, eng, fn, reads=(), writes=(), dkey=None):
        op = _Op()
        op.eng, op.fn, op.signal, op.ev, op.dkey = eng, fn, False, None, dkey
        deps = set()
        for r in reads:
            w = self.last_w.get(r)
            if w is not None:
                deps.add(w)
        for wr in writes:
            w = self.last_w.get(wr)
            if w is not None:
                deps.add(w)
            deps.update(self.readers.get(wr, ()))
        idx = len(self.ops)
        for r in reads:
            self.readers.setdefault(r, []).append(idx)
        for wr in writes:
            self.last_w[wr] = idx
            self.readers[wr] = []
        op.deps = deps
        self.ops.append(op)
        if dkey is not None and dkey not in self.dkeys:
            self.dkeys.append(dkey)
        return idx

    def pe(self, fn, reads=(), writes=()):
        return self.add("pe", fn, reads, writes)

    def act(self, fn, reads=(), writes=()):
        return self.add("act", fn, reads, writes)

    def dve(self, fn, reads=(), writes=()):
        return self.add("dve", fn, reads, writes)

    def pool(self, fn, reads=(), writes=()):
        return self.add("pool", fn, reads, writes)

    def dma(self, q, out, in_, key, reads=(), writes=(), accum=False):
        if accum:
            return self.add(q, lambda e: e.dma_start(out=out, in_=in_, accum_op=ALU.add), reads, writes, dkey=key)
        return self.add(q, lambda e: e.dma_start(out=out, in_=in_), reads, writes, dkey=key)

    def barrier(self):
        last = {}
        for i, op in enumerate(self.ops):
            if op.fn is not None:
                last[("e", op.eng) if op.dkey is None else ("d", op.dkey)] = i
        deps = set(last.values())
        for eng in ("pe", "act", "dve", "pool", "sp"):
            op = _Op()
            op.eng, op.fn, op.signal, op.ev, op.dkey = eng, None, False, None, None
            op.deps = set(deps)
            self.ops.append(op)
        self.last_w = {}
        self.readers = {}

    def emit(self):
        nc = self.nc
        engs = {"pe": nc.tensor, "act": nc.scalar, "dve": nc.vector, "pool": nc.gpsimd, "sp": nc.sync}
        for op in self.ops:
            for d in op.deps:
                if self.ops[d].dkey is None:
                    self.ops[d].signal = True
        esem = {e: nc.alloc_semaphore("es_" + e) for e in engs}
        dsem = {k: nc.alloc_semaphore("ds_%d" % i) for i, k in enumerate(self.dkeys)}
        cnt = {e: 0 for e in engs}
        dcnt = {k: 0 for k in self.dkeys}
        waited = {e: {} for e in engs}
        for op in self.ops:
            E = engs[op.eng]
            wl = {}
            for d in op.deps:
                p = self.ops[d]
                if p.fn is None:
                    continue
                if op.eng == "pe" and p.eng == "pe" and p.dkey is None:
                    continue
                name, sem, val = p.ev
                if wl.get(name, (None, 0))[1] < val:
                    wl[name] = (sem, val)
            for name, (sem, val) in wl.items():
                if waited[op.eng].get(name, 0) >= val:
                    continue
                E.wait_ge(sem, val)
                waited[op.eng][name] = val
            if op.fn is None:
                continue
            ins = op.fn(E)
            if op.dkey is not None:
                dcnt[op.dkey] += 16
                ins.then_inc(dsem[op.dkey], 16)
                op.ev = (("d", op.dkey), dsem[op.dkey], dcnt[op.dkey])
            elif op.signal:
                cnt[op.eng] += 1
                ins.then_inc(esem[op.eng], 1)
                op.ev = (("e", op.eng), esem[op.eng], cnt[op.eng])
        for k in self.dkeys:
            if dcnt[k]:
                nc.sync.wait_ge(dsem[k], dcnt[k])
        for e in engs:
            if cnt[e] and e != "sp":
                nc.sync.wait_ge(esem[e], cnt[e])


def build_program():
    nc = bass.Bass("TRN2", target_bir_lowering=False)
    P = Prog(nc)

    def din(name, shape):
        return nc.dram_tensor(name, list(shape), F32, kind="ExternalInput").ap()

    def dout(name, shape):
        return nc.dram_tensor(name, list(shape), F32, kind="ExternalOutput").ap()

    xm = din("xm", [NMAIN, D])
    xp = din("xp", [NPRE, D])
    xs = din("xs", [NSAMP, D])
    sret = din("sret", [NB, NH, DK, DV])
    sconv = din("sconv", [2 * NB, D])
    sffn = din("sffn", [2 * NB, DFF])
    w_in = din("w_in", [D, 22528])
    p_ret = din("p_ret", [4096, D])
    p_conv = din("p_conv", [D, D])
    w_o = din("w_o", [D, D])
    w_up = din("w_up", [D, DFF])
    w_gate = din("w_gate", [D, DFF])
    w_down = din("w_down", [DFF, D])
    gcol_d = din("gcol", [128, 2, 16])
    gpost_d = din("gpost", [2, 128, D])
    cw_d = din("cw", [128, 16, 3])
    fcw_d = din("fcw", [128, NFF, 3])
    fcb_d = din("fcb", [128, NFF])
    cq_d = din("cq", [2, NH, 128, 640])
    sq_d = din("sq", [2, NH, 128, 640])
    ck_d = din("ck", [3, 128, 7, 128])
    sk_d = din("sk", [3, 128, 7, 128])
    ksc_d = din("ksc", [128, 3, 7, NH])
    cmask_d = din("cmask", [128, 128])
    bmask_d = din("bmask", [64, 64])
    bm16_d = din("bm16", [128, NB, 64])
    bmT_d = din("bmT", [64, NB])
    idf_d = din("idf", [128, 128])

    ym = dout("ym", [NMAIN, D])
    ys = dout("ys", [NSAMP, D])
    retp = dout("retp", [NH, DK, DV])
    convp = dout("convp", [2, D])
    ffnp = dout("ffnp", [2, DFF])
    rets = dout("rets", [NB, NH, DK, DV])
    convs = dout("convs", [2 * NB, D])
    ffns = dout("ffns", [2 * NB, DFF])

    xmid_d = nc.dram_tensor("xmid_scr", [6, 128, D], F32, kind="Internal").ap()
    sbase_d = nc.dram_tensor("sbase_scr", [NH, 128, 2 * DV], F32, kind="Internal").ap()

    def sb(name, shape, dt):
        return nc.alloc_sbuf_tensor("s_" + name, list(shape), dt)

    HT = sb("HT", [128, 16, 640], BF16)
    R1 = sb("R1", [128, 10240], F32)
    R2 = sb("R2", [128, 5120], F32)
    R4 = sb("R4", [128, 18432], F32)
    WB = [sb("WB%d" % i, [128, 16, 512], BF16) for i in range(2)]
    ident = sb("ident", [128, 128], BF16)
    identf = sb("identf", [128, 128], F32)
    cmask = sb("cmask", [128, 128], F32)
    bmask = sb("bmask", [64, 64], F32)
    bm16 = sb("bm16", [128, NB, 64], BF16)
    bmT = sb("bmT", [64, NB], F32)
    gcol = sb("gcol", [128, 2, 16], F32)
    cw = sb("cw", [128, 16, 3], F32)
    fcw = sb("fcw", [128, NFF, 3], F32)
    fcb = sb("fcb", [128, NFF], F32)
    ksc = sb("ksc", [128, 3, 7, NH], F32)
    halo_c = sb("halo_c", [128, 16, 2], F32)
    halo_f = sb("halo_f", [128, NFF, 2], F32)
    halo_cs = sb("halo_cs", [128, 16, NB, 2], F32)
    halo_fs = sb("halo_fs", [128, NFF, NB, 2], F32)
    st = sb("st", [128, 64], F32)
    otm = sb("otm", [32, 512], F32)

    def _view(reg, off_b, shape, bf):
        n = int(np.prod(shape))
        if bf:
            v = reg[:, off_b // 4:(off_b + 2 * n + 3) // 4].bitcast(BF16)
        else:
            v = reg[:, off_b // 4:off_b // 4 + n]
        if len(shape) == 1:
            return v
        names = " ".join("d%d" % i for i in range(len(shape)))
        kw = {"d%d" % i: s for i, s in enumerate(shape)}
        return v.rearrange("p (%s) -> p %s" % (names, names), **kw)

    def r_bf(reg, off_b, shape):
        return _view(reg, off_b, shape, True)

    def r_f32(reg, off_b, shape):
        return _view(reg, off_b, shape, False)

    goT = r_bf(R1, 0, [32, 640])
    XF = r_f32(R1, 0, [5, 2048])
    hT0 = r_bf(R1, 0, [16, 896])
    gbcT = r_bf(R2, 0, [16, 640])
    Sf = r_f32(R2, 0, [4, 2, 512])
    Sbf = r_bf(R2, 16384, [2, 2, 512])
    GP = r_f32(R2, 0, [2048])
    mT = r_bf(R4, 0, [16, 640])
    aT = r_bf(R4, 0, [NFF, 640])
    junk = r_bf(R4, 40960, [2048])
    XT = [r_f32(R4, 57344, [2048]), r_f32(R4, 57344 + 8192, [2048]), r_f32(R4, 24576, [2048]), r_f32(R4, 32768, [2048])]
    off = {"o": 0}

    def tb(shape, bf):
        n = int(np.prod(shape)) * (2 if bf else 4)
        n = (n + 3) // 4 * 4
        v = _view(R4, off["o"], shape, bf)
        off["o"] += n
        return v

    qT = tb([2, 2, 640], True)
    kTM = tb([7, 2, 256], True)
    kT = tb([2, 2, 640], True)
    vS = tb([2, 512], True)
    sg = tb([5, 512], True)
    cqt = tb([640], False)
    sqt = tb([640], False)
    ckt = tb([7, 128], False)
    skt = tb([7, 128], False)
    Sb = tb([2, 2, 512], True)
    base = tb([2, 512], False)
    rt1 = tb([512], False)
    rt2 = tb([512], False)
    PT = tb([2, 128], True)
    gotm = tb([2, 512], True)
    qTs = tb([2, 2, 64], True)
    khat = tb([2, 256], True)
    vs = tb([2, 512], True)
    osum = tb([2, 512], False)
    sgs = tb([2, 512], True)
    khmb = tb([2, 256], True)
    rt3 = tb([512], False)
    rt4 = tb([512], False)
    junkB = tb([512], True)
    assert off["o"] <= 73728, off["o"]
    gcs = r_f32(R4, 20480, [4, 640])
    UPc = r_f32(R4, 20480 + 10240, [4, 740])
    cv = r_f32(R4, 20480 + 10240 + 11840, [4, 640])
    ya = r_f32(R4, 20480, [4, 640])
    yb = r_f32(R4, 20480 + 10240, [4, 640])
    sgm = r_f32(R4, 20480 + 20480, [4, 640])
    UPf = r_f32(R4, 56320, [2, 740])
    gaB = r_f32(R4, 56320 + 5920, [4, 640])
    assert 56320 + 5920 + 10240 <= 73728

    PS = [nc.alloc_psum_tensor("ps%d" % i, [128, 512], F32) for i in range(8)]

    def psb(i):
        return PS[i][:, :].bitcast(BF16).rearrange("p (a b) -> p a b", b=128)

    def cload(dst, src, key, q="sp"):
        P.dma(q, dst, src, key, writes=[("c", key)])

    cload(ident[:], idf_d[:, :], "c_id", q="pool")
    cload(identf[:], idf_d[:, :], "c_idf")
    cload(cmask[:], cmask_d[:, :], "c_cm")
    cload(bmask[:], bmask_d[:, :], "c_bm")
    cload(bm16[:], bm16_d[:, :, :], "c_bm16", q="pool")
    cload(bmT[:], bmT_d[:, :], "c_bmT")
    cload(gcol[:], gcol_d[:, :, :], "c_gcol")
    cload(cw[:], cw_d[:, :, :], "c_cw")
    cload(fcw[:], fcw_d[:, :, :], "c_fcw")
    cload(fcb[:], fcb_d[:, :], "c_fcb")
    cload(ksc[:], ksc_d[:, :, :, :], "c_ksc")
    P.dve(lambda e: e.memset(halo_c[:], 0.0), writes=[("halo_c",)])
    P.dve(lambda e: e.memset(halo_f[:], 0.0), writes=[("halo_f",)])
    tmpc = r_f32(R1, 0, [2048])
    tmpf = r_f32(R1, 8192, [DFF])
    P.dma("sp", tmpc[:32, :], sconv[:, :], "c_tc", writes=[("tmpc",)])
    P.dma("sp", tmpf[:32, :], sffn[:, :], "c_tf", writes=[("tmpf",)])
    P.barrier()

    def halo_init(tmp, dst, nf):
        for g0 in range(0, nf, 16):
            n = min(16, nf - g0)
            bank = (g0 // 16) % 2

            def f(e, g0=g0, n=n, bank=bank):
                ins = None
                for j in range(n):
                    ins = e.transpose(PS[bank][:, j * 32:(j + 1) * 32], tmp[:32, (g0 + j) * 128:(g0 + j + 1) * 128], identf[:32, :32])
                return ins

            P.pe(f, writes=[("ps", bank)])
            P.act(lambda e, g0=g0, n=n, bank=bank: e.activation(
                out=dst[:, g0:g0 + n, :, :].rearrange("p f b t -> p f (b t)"),
                in_=PS[bank][:, 0:n * 32].rearrange("p (f x) -> p f x", x=32), func=AF.Copy),
                reads=[("ps", bank)], writes=[("halo_s", g0)])

    halo_init(tmpc, halo_cs, 16)
    halo_init(tmpf, halo_fs, NFF)
    P.barrier()

    wstate = {"n": 0}

    def wload(w2d, r0, c0, nkc=16, ncol=512):
        slot = wstate["n"] % 2
        wstate["n"] += 1
        src = w2d[r0:r0 + nkc * 128, c0:c0 + ncol].rearrange("(kc p) n -> p kc n", p=128)
        P.dma("pool", WB[slot][:, 0:nkc, 0:ncol], src, ("wb", slot), writes=[("WB", slot)])
        return slot

    class WStream:
        def __init__(self, blocks):
            self.blocks = blocks
            self.issued = []

        def get(self, i):
            while len(self.issued) < min(len(self.blocks), i + 2):
                b = self.blocks[len(self.issued)]
                self.issued.append(wload(*b))
            return self.issued[i]

    st_n = {"n": 0}

    def stcol():
        i = st_n["n"] % 64
        st_n["n"] += 1
        return i

    def rstd_col(src_ap, rows, n, reads, junk_ap, junk_key):
        col = stcol()
        P.act(lambda e: e.activation(out=junk_ap, in_=src_ap, func=AF.Square, accum_out=st[:rows, col:col + 1]),
              reads=list(reads), writes=[("st", col), junk_key])
        P.act(lambda e: e.activation(out=st[:rows, col:col + 1], in_=st[:rows, col:col + 1], func=AF.Sqrt, bias=EPS, scale=1.0 / n),
              reads=[("st", col)], writes=[("st", col)])
        P.dve(lambda e: e.reciprocal(out=st[:rows, col:col + 1], in_=st[:rows, col:col + 1]), reads=[("st", col)], writes=[("st", col)])
        return col

    junks = [junk, r_bf(R4, 45056, [2048])]

    def norm_transpose(dst, gi, chunks, load_fn):
        for ci, (rows, col0) in enumerate(chunks):
            src, skey = load_fn(ci)
            jb = junks[ci % 2]
            jk = ("junk", ci % 2)
            col = rstd_col(src[:rows, :], rows, D, [skey], jb[:rows, :], jk)
            P.dve(lambda e, src=src, rows=rows, col=col, jb=jb: e.tensor_scalar(out=jb[:rows, :], in0=src[:rows, :],
                                                                                scalar1=st[:rows, col:col + 1], scalar2=None, op0=ALU.mult),
                  reads=[skey, ("st", col)], writes=[jk])
            for g4 in range(4):
                bank = 6 + (g4 % 2)
                pv = psb(bank)

                def tr(e, g4=g4, rows=rows, pv=pv, jb=jb):
                    ins = None
                    for j in range(4):
                        kc = g4 * 4 + j
                        ins = e.transpose(pv[:, j, 0:rows], jb[:rows, kc * 128:(kc + 1) * 128], ident[:rows, :rows])
                    return ins

                P.pe(tr, reads=[jk], writes=[("ps", bank)])
                P.dve(lambda e, g4=g4, rows=rows, col0=col0, pv=pv: e.tensor_tensor(
                    out=dst[:, g4 * 4:(g4 + 1) * 4, col0:col0 + rows], in0=pv[:, 0:4, 0:rows],
                    in1=gcol[:, gi, g4 * 4:(g4 + 1) * 4].unsqueeze(2).to_broadcast([128, 4, rows]), op=ALU.mult),
                    reads=[("ps", bank)], writes=[("hT", g4, ci)])

    def hT_reads(nchunks):
        return [("hT", g4, ci) for g4 in range(4) for ci in range(nchunks)]

    def proj_fm(ws, bi, fc, hsrc, hreads, col0, ncols, bank, nkc=16, kc0=0):
        slot = ws.get(bi)

        def f(e):
            ins = None
            for kc in range(nkc):
                ins = e.matmul(PS[bank][:, 0:ncols], lhsT=WB[slot][:, kc, fc * 128:(fc + 1) * 128],
                               rhs=hsrc[:, kc0 + kc, col0:col0 + ncols], start=(kc == 0), stop=(kc == nkc - 1))
            return ins

        P.pe(f, reads=[("WB", slot)] + list(hreads), writes=[("ps", bank)])

    def proj_tm(ws, bi, hsrc, hreads, col0, rows, bank, nkc=16, kc0=0, start=True, stop=True):
        slot = ws.get(bi)

        def f(e):
            ins = None
            for kc in range(nkc):
                ins = e.matmul(PS[bank][:rows, :], lhsT=hsrc[:, kc0 + kc, col0:col0 + rows],
                               rhs=WB[slot][:, kc, :], start=(start and kc == 0), stop=(stop and kc == nkc - 1),
                               skip_group_check=True)
            return ins

        P.pe(f, reads=[("WB", slot)] + list(hreads), writes=[("ps", bank)])

    ep_n = {"n": 0}
    late = []
    bg = []

    def flush_late(keep=0):
        while len(late) > keep:
            late.pop(0)()

    bgn = {"n": 0}

    def pump(n, banks=(7,)):
        for _ in range(n):
            if bg:
                bk = banks[bgn["n"] % len(banks)]
                bgn["n"] += 1
                bg.pop(0)(bk)

    def retention_epilogue(h, rows, col0, o_ap, o_key, sg_ap, sg_key):
        col = rstd_col(o_ap, rows, DV, [o_key], junkB[:rows, :], ("junkB",))
        gs = ep_n["n"] % 2
        ep_n["n"] += 1
        P.dve(lambda e: e.scalar_tensor_tensor(out=gotm[:rows, gs, :], in0=o_ap, scalar=st[:rows, col:col + 1],
                                               in1=sg_ap, op0=ALU.mult, op1=ALU.mult),
              reads=[o_key, ("st", col), sg_key], writes=[("gotm", gs)])

        def part2():
            pv = psb(6)

            def tr(e):
                ins = None
                for ec in range(4):
                    ins = e.transpose(pv[:, ec, 0:rows], gotm[:rows, gs, ec * 128:(ec + 1) * 128], ident[:rows, :rows])
                return ins

            P.pe(tr, reads=[("gotm", gs)], writes=[("ps", 6)])
            P.act(lambda e: e.activation(out=goT[:, h * 4:(h + 1) * 4, col0:col0 + rows], in_=pv[:, 0:4, 0:rows], func=AF.Copy),
                  reads=[("ps", 6)], writes=[("goT", h, col0)])

        late.append(part2)

    def sample_retention(h, hh, ci, TG, cx):
        g4 = math.exp(4.0 * LOGG[h])
        while bg:
            pump(1)
            flush_late()

        def sc_mm(e):
            ins = None
            for half in range(2):
                ins = e.matmul(PS[4][:64, 0:64], lhsT=kT[:, hh, half, TG:TG + 64], rhs=qT[:, hh, half, TG:TG + 64],
                               start=(half == 0), stop=(half == 1))
            return ins

        P.pe(sc_mm, reads=[("kT", hh, ci), ("qT", hh)], writes=[("ps", 4)])
        P.dve(lambda e: e.tensor_tensor(out=PT[:64, 0, 0:64], in0=PS[4][:64, 0:64], in1=bmask[:, :], op=ALU.mult),
              reads=[("ps", 4)], writes=[("PT", 0)])
        P.act(lambda e: e.mul(out=khat[:64, cx, :], in_=kTM[:64, ci, hh, :], mul=g4), reads=[("kTM", ci, hh)], writes=[("khat", cx)])
        P.act(lambda e: e.activation(out=qTs[:, cx], in_=qT[:, hh, :, TG:TG + 64], func=AF.Copy), reads=[("qT", hh)], writes=[("qTs", cx)])
        P.pe(lambda e: e.matmul(PS[7][:64, :], lhsT=PT[:64, 0, 0:64], rhs=vs[:64, cx, :], start=True, stop=True),
             reads=[("PT", 0), ("vs", cx)], writes=[("ps", 7)])
        P.act(lambda e: e.activation(out=osum[:64, cx, :], in_=PS[7][:64, :], func=AF.Copy), reads=[("ps", 7)], writes=[("osum", cx)])

        def load(b):
            P.dma("sp", Sf[:, b % 4], sret[b, h].rearrange("(a p) e -> p a e", p=128), ("sfi", b % 4), writes=[("Sf", b % 4)])

        for b in range(3):
            load(b)

        def cast(b):
            P.pool(lambda e: e.tensor_copy(out=Sbf[:, b % 2], in_=Sf[:, b % 4]), reads=[("Sf", b % 4)], writes=[("Sbf", b % 2)])

        cast(0)

        def mk_cross(b):
            def f(bk):
                s4, s2 = b % 4, b % 2
                P.act(lambda e: e.mul(out=khmb[:64, s2, :], in_=khat[:64, cx, :], mul=bmT[:, b:b + 1]),
                      reads=[("khat", cx)], writes=[("khmb", s2)])

                def mm(e):
                    ins = None
                    for half in range(2):
                        ins = e.matmul(PS[bk][:64, :], lhsT=qTs[:, cx, half, :], rhs=Sbf[:, s2, half, :], start=(half == 0), stop=(half == 1))
                    return ins

                P.pe(mm, reads=[("qTs", cx), ("Sbf", s2)], writes=[("ps", bk)])
                P.dve(lambda e: e.scalar_tensor_tensor(out=osum[:64, cx, :], in0=PS[bk][:64, :], scalar=bmT[:, b:b + 1], in1=osum[:64, cx, :],
                                                       op0=ALU.mult, op1=ALU.add),
                      reads=[("ps", bk), ("osum", cx)], writes=[("osum", cx)])
                if b + 1 < NB:
                    cast(b + 1)
            return f

        def mk_state(b, half):
            def f(bk):
                s4, s2 = b % 4, b % 2
                P.pe(lambda e: e.matmul(PS[bk][:, :], lhsT=khmb[:64, s2, half * 128:(half + 1) * 128], rhs=vs[:64, cx, :], start=True, stop=True),
                     reads=[("khmb", s2), ("vs", cx)], writes=[("ps", bk)])
                P.dve(lambda e: e.scalar_tensor_tensor(out=Sf[:, s4, half, :], in0=Sf[:, s4, half, :], scalar=g4, in1=PS[bk][:, :],
                                                       op0=ALU.mult, op1=ALU.add),
                      reads=[("ps", bk), ("Sf", s4)], writes=[("Sf", s4)])
                if half == 1:
                    P.dma("sp", rets[b, h].rearrange("(a p) e -> p a e", p=128), Sf[:, s4], ("sfo", s4), reads=[("Sf", s4)])
                    if b + 3 < NB:
                        load(b + 3)
            return f

        for b in range(NB):
            bg.append(mk_cross(b))
            bg.append(mk_state(b, 0))
            bg.append(mk_state(b, 1))

        def fin(bk):
            retention_epilogue(h, 64, TG, osum[:64, cx, :], ("osum", cx), sgs[:64, cx, :], ("sgs", cx))
        bg.append(fin)

    ostage = r_f32(R1, 0, [DFF])
    on = {"n": 0}

    def out_rows(src_fn, nf, rows, dst):
        k = on["n"]
        on["n"] += 1
        stg = ostage if k % 2 == 0 else r_f32(R1, 5632 * 4, [2048])
        for g0 in range(0, nf, 4):
            bank = (g0 // 4) % 4

            def f(e, g0=g0, bank=bank):
                ins = None
                for j in range(4):
                    ins = e.transpose(PS[bank][:rows, j * 128:(j + 1) * 128], src_fn(g0 + j), identf[:, :])
                return ins

            P.pe(f, writes=[("ps", bank)])
            P.act(lambda e, bank=bank, g0=g0, stg=stg: e.activation(out=stg[:rows, g0 * 128:(g0 + 4) * 128], in_=PS[bank][:rows, :], func=AF.Copy),
                  reads=[("ps", bank)], writes=[("ostg", k % 2)])
        P.dma("sp", dst[:, :], stg[:rows, 0:nf * 128], ("ostg_o", k % 2), reads=[("ostg", k % 2)])

    for pi, ps in enumerate(PASSES):
        nch, kv, has_s = ps["nch"], ps["kv"], ps["sample"]
        TG = nch * 128
        TT = TG + (NSAMP if has_s else 0)
        src = xm if ps["src"] == "xm" else xp
        hT = hT0 if kv else HT
        chunks = [(128, c * 128) for c in range(nch)] + ([(NSAMP, TG)] if has_s else [])
        nchk = len(chunks)
        tiles = []
        ntl = (TT + 511) // 512
        tw = ((TT + ntl - 1) // ntl + 31) // 32 * 32
        c0 = 0
        while c0 < TT:
            n = min(tw, TT - c0)
            segs = []
            if c0 < TG:
                segs.append((0, min(n, TG - c0), False, c0))
            if c0 + n > TG:
                a = max(c0, TG)
                segs.append((a - c0, c0 + n - a, True, a))
            tiles.append((c0, n, segs))
            c0 += n

        def xrows(ci, src=src, ps=ps, nch=nch):
            if ci < nch:
                return src[ps["row0"] + ci * 128: ps["row0"] + (ci + 1) * 128, :]
            return xs[:, :]

        def load_x(ci, chunks=chunks, xrows=xrows):
            slot = ci % 4
            rows = chunks[ci][0]
            P.dma("sp", XT[slot][:rows, :], xrows(ci), ("xt", slot), writes=[("XT", slot)])
            return XT[slot], ("XT", slot)

        blocks = []
        for hp in range(4):
            if not kv:
                blocks.append((w_in, 0, C_Q + 512 * hp))
            blocks.append((w_in, 0, C_K + 512 * hp))
            for hh in range(2):
                h = 2 * hp + hh
                if not kv:
                    blocks.append((w_in, 0, C_G + 512 * h))
                blocks.append((w_in, 0, C_V + 512 * h))
        if not kv:
            for fg in range(4):
                blocks += [(w_in, 0, C_GC + 512 * fg), (w_in, 0, C_HC + 512 * fg), (w_in, 0, C_GB + 512 * fg)]
            for fg in range(4):
                blocks += [(p_ret, 0, 512 * fg), (p_ret, 2048, 512 * fg), (w_in, 0, C_GA + 512 * fg), (p_conv, 0, 512 * fg),
                           (w_in, 0, C_GBT + 512 * fg)]
            for nb in range(4):
                blocks += [(w_o, 0, 512 * nb)]
            for fg in range(11):
                blocks += [(w_up, 0, 512 * fg), (w_gate, 0, 512 * fg)]
            for nb in range(4):
                for sub in range(3):
                    blocks += [(w_down, sub * 2048, 512 * nb, 16 if sub < 2 else 12)]
        ws = WStream(blocks)
        bi = 0
        ws.get(0)

        norm_transpose(hT, 0, chunks, load_x)
        P.barrier()
        HR = []
        P.dma("sp", ckt[:, 0:7, :], ck_d[pi], "ckt", writes=[("ckt",)])
        P.dma("sp", skt[:, 0:7, :], sk_d[pi], "skt", writes=[("skt",)])
        NPUMP = 4

        for hp in range(4):
            if not kv:
                qn = 0
                for hh in range(2):
                    h = 2 * hp + hh
                    P.dma("sp", cqt[:, 0:640], cq_d[pi - 1, h], "cqt", writes=[("cqt",)])
                    P.dma("sp", sqt[:, 0:640], sq_d[pi - 1, h], "sqt", writes=[("sqt",)])
                    for ti, (t0, tn, _s) in enumerate(tiles):
                        b0_, b1_ = (0, 1) if qn % 2 == 0 else (2, 3)
                        ta, tb2 = (rt1, rt2) if qn % 2 == 0 else (rt3, rt4)
                        qn += 1
                        proj_fm(ws, bi, 2 * hh, hT, HR, t0, tn, b0_)
                        proj_fm(ws, bi, 2 * hh + 1, hT, HR, t0, tn, b1_)
                        flush_late()
                        pump(5, (7, 6))
                        x1, x2 = PS[b0_][:, 0:tn], PS[b1_][:, 0:tn]
                        cs, sn = cqt[:, t0:t0 + tn], sqt[:, t0:t0 + tn]
                        ka, kb = ("rt", id(ta)), ("rt", id(tb2))
                        P.dve(lambda e, x1=x1, cs=cs, tn=tn, ta=ta: e.tensor_tensor(out=ta[:, 0:tn], in0=x1, in1=cs, op=ALU.mult),
                              reads=[("ps", b0_), ("cqt",)], writes=[ka])
                        P.dve(lambda e, x2=x2, sn=sn, tn=tn, tb2=tb2: e.tensor_tensor(out=tb2[:, 0:tn], in0=x2, in1=sn, op=ALU.mult),
                              reads=[("ps", b1_), ("sqt",)], writes=[kb])
                        P.pool(lambda e, hh=hh, t0=t0, tn=tn, ta=ta, tb2=tb2: e.tensor_tensor(out=qT[:, hh, 0, t0:t0 + tn], in0=ta[:, 0:tn],
                                                                                           in1=tb2[:, 0:tn], op=ALU.subtract),
                               reads=[ka, kb], writes=[("qT", hh)])
                        P.dve(lambda e, x2=x2, cs=cs, tn=tn, ta=ta: e.tensor_tensor(out=ta[:, 0:tn], in0=x2, in1=cs, op=ALU.mult),
                              reads=[("ps", b1_), ("cqt",)], writes=[ka])
                        P.dve(lambda e, x1=x1, sn=sn, tn=tn, tb2=tb2: e.tensor_tensor(out=tb2[:, 0:tn], in0=x1, in1=sn, op=ALU.mult),
                              reads=[("ps", b0_), ("sqt",)], writes=[kb])
                        P.pool(lambda e, hh=hh, t0=t0, tn=tn, ta=ta, tb2=tb2: e.tensor_tensor(out=qT[:, hh, 1, t0:t0 + tn], in0=ta[:, 0:tn],
                                                                                           in1=tb2[:, 0:tn], op=ALU.add),
                               reads=[ka, kb], writes=[("qT", hh)])
                bi += 1
            for ci, (rows, col0) in enumerate(chunks):
                bank = 2 + (ci % 2)
                proj_tm(ws, bi, hT, HR, col0, rows, bank)
                flush_late()
                pump(5, (7, 6))
                for hh in range(2):
                    h = 2 * hp + hh
                    x1 = PS[bank][:rows, hh * 256: hh * 256 + 128]
                    x2 = PS[bank][:rows, hh * 256 + 128: hh * 256 + 256]
                    cs, sn = ckt[:rows, ci, :], skt[:rows, ci, :]
                    sc = ksc[:rows, pi, ci, h:h + 1]
                    rk = [("ps", bank), ("ckt",), ("skt",)]
                    ta, tb2 = (rt1, rt2) if hh == 0 else (rt3, rt4)
                    ka, kb = ("rt", id(ta)), ("rt", id(tb2))

                    def stt(e, o_, a_, b_, sc=sc):
                        return e.scalar_tensor_tensor(out=o_, in0=a_, scalar=sc, in1=b_, op0=ALU.mult, op1=ALU.mult)

                    P.dve(lambda e, x1=x1, cs=cs, rows=rows, stt=stt, ta=ta: stt(e, ta[:rows, 0:128], x1, cs), reads=rk, writes=[ka])
                    P.dve(lambda e, x2=x2, sn=sn, rows=rows, stt=stt, tb2=tb2: stt(e, tb2[:rows, 0:128], x2, sn), reads=rk, writes=[kb])
                    P.dve(lambda e, x2=x2, cs=cs, rows=rows, stt=stt, ta=ta: stt(e, ta[:rows, 128:256], x2, cs), reads=rk, writes=[ka])
                    P.dve(lambda e, x1=x1, sn=sn, rows=rows, stt=stt, tb2=tb2: stt(e, tb2[:rows, 128:256], x1, sn), reads=rk, writes=[kb])
                    P.pool(lambda e, rows=rows, ci=ci, hh=hh, ta=ta, tb2=tb2: e.tensor_tensor(out=kTM[:rows, ci, hh, 0:128], in0=ta[:rows, 0:128],
                                                                                           in1=tb2[:rows, 0:128], op=ALU.subtract),
                           reads=[ka, kb], writes=[("kTM", ci, hh)])
                    P.pool(lambda e, rows=rows, ci=ci, hh=hh, ta=ta, tb2=tb2: e.tensor_tensor(out=kTM[:rows, ci, hh, 128:256], in0=ta[:rows, 128:256],
                                                                                           in1=tb2[:rows, 128:256], op=ALU.add),
                           reads=[ka, kb], writes=[("kTM", ci, hh)])
                    if not kv:
                        def part2(rows=rows, ci=ci, hh=hh, col0=col0):
                            pv = psb(6)

                            def trk(e):
                                ins = None
                                for half in range(2):
                                    ins = e.transpose(pv[:, half, 0:rows], kTM[:rows, ci, hh, half * 128:(half + 1) * 128], ident[:rows, :rows])
                                return ins

                            P.pe(trk, reads=[("kTM", ci, hh)], writes=[("ps", 6)])
                            P.act(lambda e: e.activation(out=kT[:, hh, :, col0:col0 + rows], in_=pv[:, 0:2, 0:rows], func=AF.Copy),
                                  reads=[("ps", 6)], writes=[("kT", hh, ci)])

                        late.append(part2)
            bi += 1
            for hh in range(2):
                h = 2 * hp + hh
                cx = h % 2
                if not kv:
                    for ci, (rows, col0) in enumerate(chunks):
                        bank = 2 + (ci % 2)
                        proj_tm(ws, bi, hT, HR, col0, rows, bank)
                        flush_late()
                        pump(6, (7, 6))
                        if ci < nch:
                            P.act(lambda e, rows=rows, ci=ci, bank=bank: e.activation(out=sg[:rows, ci, :], in_=PS[bank][:rows, :], func=AF.Silu),
                                  reads=[("ps", bank)], writes=[("sg", ci)])
                        else:
                            P.act(lambda e, rows=rows, cx=cx, bank=bank: e.activation(out=sgs[:rows, cx, :], in_=PS[bank][:rows, :], func=AF.Silu),
                                  reads=[("ps", bank)], writes=[("sgs", cx)])
                    bi += 1
                if pi == 0:
                    P.dve(lambda e: e.memset(base[:], 0.0), writes=[("base",)])
                else:
                    P.dma("sp", base[:].rearrange("p a b -> p (a b)"), sbase_d[h], "base", writes=[("base",)])
                if not kv:
                    P.act(lambda e: e.activation(out=Sb[:, 0], in_=base[:], func=AF.Copy), reads=[("base",)], writes=[("Sb", 0)])
                sbs = 0

                def v_proj(ci, hh=hh, cx=cx):
                    rows, col0 = chunks[ci]
                    bank = 2 + (ci % 2)
                    proj_tm(ws, bi, hT, HR, col0, rows, bank)
                    if ci >= nch:
                        P.act(lambda e: e.activation(out=vs[:rows, cx, :], in_=PS[bank][:rows, :], func=AF.Copy),
                              reads=[("ps", bank)], writes=[("vs", cx)])
                    else:
                        P.act(lambda e: e.activation(out=vS[:rows, ci % 2, :], in_=PS[bank][:rows, :], func=AF.Copy),
                              reads=[("ps", bank)], writes=[("vS", ci % 2)])

                def scores(ci, hh=hh):
                    col0 = chunks[ci][1]
                    pslot = ci % 2

                    def sc_mm(e):
                        ins = None
                        for half in range(2):
                            ins = e.matmul(PS[4][:, 0:128], lhsT=kT[:, hh, half, col0:col0 + 128], rhs=qT[:, hh, half, col0:col0 + 128],
                                           start=(half == 0), stop=(half == 1))
                        return ins

                    P.pe(sc_mm, reads=[("kT", hh, ci), ("qT", hh)], writes=[("ps", 4)])
                    P.dve(lambda e: e.tensor_tensor(out=PT[:, pslot, :], in0=PS[4][:, 0:128], in1=cmask[:, :], op=ALU.mult),
                          reads=[("ps", 4)], writes=[("PT", pslot)])

                v_proj(0)
                if not kv:
                    scores(0)
                for ci in range(nch):
                    rows, col0 = chunks[ci]
                    vslot = ci % 2
                    if ci + 1 < nchk:
                        v_proj(ci + 1)
                    flush_late(1)
                    pump(2)
                    if not kv:
                        def o_mm(e, hh=hh, col0=col0, vslot=vslot, sbs=sbs, ci=ci):
                            e.matmul(PS[5][:, :], lhsT=PT[:, ci % 2, :], rhs=vS[:, vslot, :], start=True, stop=False)
                            ins = None
                            for half in range(2):
                                ins = e.matmul(PS[5][:, :], lhsT=qT[:, hh, half, col0:col0 + 128], rhs=Sb[:, sbs, half, :],
                                               start=False, stop=(half == 1))
                            return ins

                        P.pe(o_mm, reads=[("PT", ci % 2), ("vS", vslot), ("qT", hh), ("Sb", sbs)], writes=[("ps", 5)])

                    def s_mm(e, ci=ci, hh=hh, vslot=vslot):
                        ins = None
                        for half in range(2):
                            ins = e.matmul(PS[half][:, :], lhsT=kTM[:, ci, hh, half * 128:(half + 1) * 128], rhs=vS[:, vslot, :],
                                           start=(ci == 0), stop=True, skip_group_check=True)
                        return ins

                    P.pe(s_mm, reads=[("kTM", ci, hh), ("vS", vslot)], writes=[("ps", 0), ("ps", 1)])
                    if not kv and ci + 1 < nch:
                        scores(ci + 1)
                    last_prompt = (ci == nch - 1)
                    if not kv and not last_prompt:
                        nsb = 1 - sbs
                        for half in range(2):
                            P.dve(lambda e, half=half, nsb=nsb: e.tensor_tensor(out=Sb[:, nsb, half, :], in0=PS[half][:, :], in1=base[:, half, :],
                                                                                 op=ALU.add),
                                  reads=[("ps", half), ("base",)], writes=[("Sb", nsb)])
                        sbs = nsb
                    if last_prompt:
                        for half in range(2):
                            P.dve(lambda e, half=half: e.tensor_tensor(out=base[:, half, :], in0=PS[half][:, :], in1=base[:, half, :], op=ALU.add),
                                  reads=[("ps", half), ("base",)], writes=[("base",)])
                        if pi < 2:
                            P.dma("sp", sbase_d[h], base[:].rearrange("p a b -> p (a b)"), "base_o", reads=[("base",)])
                        else:
                            fs = math.exp(LOGG[h] * (2047.0 - TOFF))
                            P.act(lambda e, fs=fs: e.mul(out=base[:], in_=base[:], mul=fs), reads=[("base",)],
                                  writes=[("base",)])
                            P.dma("sp", retp[h].rearrange("(a p) e -> p a e", p=128), base[:], "base_o", reads=[("base",)])
                    if not kv:
                        retention_epilogue(h, 128, col0, PS[5][:, :], ("ps", 5), sg[:, ci, :], ("sg", ci))
                if has_s:
                    sample_retention(h, hh, nch, TG, cx)
                bi += 1
        flush_late()
        while bg:
            pump(1)
            flush_late()
        P.barrier()
        if kv:
            continue

        def conv3(dstP, dstS, up, upkey, dkey, f, wt, bias, TG=TG, has_s=has_s):
            upP = up[:, 0:2 + TG]
            segs = [(dstP, lambda k, upP=upP, TG=TG: upP[:, k:k + TG])]
            if has_s:
                upS = up[:, 642:738].rearrange("p (b t) -> p b t", t=6)
                segs.append((dstS, lambda k, upS=upS: upS[:, :, k:k + 4]))
            for (dd, sl) in segs:
                if bias is None:
                    P.dve(lambda e, dd=dd, sl=sl: e.tensor_scalar(out=dd, in0=sl(0), scalar1=wt[:, f, 0:1], scalar2=None, op0=ALU.mult),
                           reads=[upkey], writes=[dkey])
                else:
                    P.dve(lambda e, dd=dd, sl=sl: e.tensor_scalar(out=dd, in0=sl(0), scalar1=wt[:, f, 0:1], scalar2=bias, op0=ALU.mult,
                                                                   op1=ALU.add),
                           reads=[upkey], writes=[dkey])
                for k in (1, 2):
                    P.dve(lambda e, dd=dd, sl=sl, k=k: e.scalar_tensor_tensor(out=dd, in0=sl(k), scalar=wt[:, f, k:k + 1], in1=dd,
                                                                               op0=ALU.mult, op1=ALU.add),
                           reads=[upkey, dkey], writes=[dkey])

        for fg in range(4):
            for fc in range(4):
                for ti, (t0, tn, _s) in enumerate(tiles):
                    bank = (fc * 2 + ti) % 4
                    proj_fm(ws, bi, fc, hT, HR, t0, tn, bank)
                    P.act(lambda e, fc=fc, t0=t0, tn=tn, bank=bank: e.activation(out=gcs[:, fc, t0:t0 + tn], in_=PS[bank][:, 0:tn], func=AF.Copy),
                          reads=[("ps", bank)], writes=[("gcs", fc)])
            bi += 1
            for fc in range(4):
                f = fg * 4 + fc
                up = UPc[:, fc, :]
                upkey = ("upc", fc)
                P.dve(lambda e, up=up, f=f: e.tensor_copy(out=up[:, 0:2], in_=halo_c[:, f, :]), reads=[("halo_c", f)], writes=[upkey])
                if has_s:
                    P.dve(lambda e, up=up, f=f: e.tensor_copy(out=up[:, 642:738].rearrange("p (b t) -> p b t", t=6)[:, :, 0:2],
                                                              in_=halo_cs[:, f, :, :]), reads=[("halo_cs", f)], writes=[upkey])
                for ti, (t0, tn, segs) in enumerate(tiles):
                    bank = (fc * 2 + ti) % 4
                    proj_fm(ws, bi, fc, hT, HR, t0, tn, bank)
                    for (so, sn, is_s, sc0) in segs:
                        if not is_s:
                            P.dve(lambda e, up=up, fc=fc, so=so, sn=sn, sc0=sc0, bank=bank: e.tensor_tensor(
                                out=up[:, 2 + sc0:2 + sc0 + sn], in0=PS[bank][:, so:so + sn], in1=gcs[:, fc, sc0:sc0 + sn], op=ALU.mult),
                                reads=[("ps", bank), ("gcs", fc)], writes=[upkey])
                        else:
                            P.dve(lambda e, up=up, fc=fc, so=so, sn=sn, sc0=sc0, bank=bank: e.tensor_tensor(
                                out=up[:, 642:738].rearrange("p (b t) -> p b t", t=6)[:, :, 2:6],
                                in0=PS[bank][:, so:so + sn].rearrange("p (b t) -> p b t", t=4),
                                in1=gcs[:, fc, sc0:sc0 + sn].rearrange("p (b t) -> p b t", t=4), op=ALU.mult),
                                reads=[("ps", bank), ("gcs", fc)], writes=[upkey])
                P.dve(lambda e, up=up, f=f, TG=TG: e.tensor_copy(out=halo_c[:, f, :], in_=up[:, TG:TG + 2]), reads=[upkey],
                      writes=[("halo_c", f)])
                if has_s:
                    P.dve(lambda e, up=up, f=f: e.tensor_copy(out=halo_cs[:, f, :, :],
                                                              in_=up[:, 642:738].rearrange("p (b t) -> p b t", t=6)[:, :, 4:6]),
                          reads=[upkey], writes=[("halo_cs", f)])
                conv3(cv[:, fc, 0:TG], cv[:, fc, TG:TG + 64].rearrange("p (b t) -> p b t", t=4) if has_s else None, up, upkey, ("cv", fc), f, cw, None)
            bi += 1
            for fc in range(4):
                f = fg * 4 + fc
                for ti, (t0, tn, _s) in enumerate(tiles):
                    bank = (fc * 2 + ti) % 4
                    proj_fm(ws, bi, fc, hT, HR, t0, tn, bank)
                    P.dve(lambda e, f=f, fc=fc, t0=t0, tn=tn, bank=bank: e.tensor_tensor(
                        out=gbcT[:, f, t0:t0 + tn], in0=PS[bank][:, 0:tn], in1=cv[:, fc, t0:t0 + tn], op=ALU.mult),
                        reads=[("ps", bank), ("cv", fc)], writes=[("gbcT", f)])
            bi += 1
        P.barrier()

        for fg in range(4):
            for (kc0, first) in ((0, True), (16, False)):
                for fc in range(4):
                    for ti, (t0, tn, _s) in enumerate(tiles):
                        bank = (fc * 2 + ti) % 4
                        proj_fm(ws, bi, fc, goT, [], t0, tn, bank, kc0=kc0)
                        if first:
                            P.act(lambda e, fc=fc, t0=t0, tn=tn, bank=bank: e.activation(out=ya[:, fc, t0:t0 + tn], in_=PS[bank][:, 0:tn],
                                                                                        func=AF.Copy),
                                  reads=[("ps", bank)], writes=[("ya", fc)])
                        else:
                            P.dve(lambda e, fc=fc, t0=t0, tn=tn, bank=bank: e.tensor_tensor(out=ya[:, fc, t0:t0 + tn], in0=PS[bank][:, 0:tn],
                                                                                            in1=ya[:, fc, t0:t0 + tn], op=ALU.add),
                                  reads=[("ps", bank), ("ya", fc)], writes=[("ya", fc)])
                bi += 1
            for fc in range(4):
                for ti, (t0, tn, _s) in enumerate(tiles):
                    bank = (fc * 2 + ti) % 4
                    proj_fm(ws, bi, fc, hT, [], t0, tn, bank)
                    P.act(lambda e, fc=fc, t0=t0, tn=tn, bank=bank: e.activation(out=sgm[:, fc, t0:t0 + tn], in_=PS[bank][:, 0:tn],
                                                                                func=AF.Sigmoid),
                          reads=[("ps", bank)], writes=[("sgm", fc)])
                    P.pool(lambda e, fc=fc, t0=t0, tn=tn: e.tensor_tensor(out=ya[:, fc, t0:t0 + tn], in0=ya[:, fc, t0:t0 + tn],
                                                                          in1=sgm[:, fc, t0:t0 + tn], op=ALU.mult),
                           reads=[("ya", fc), ("sgm", fc)], writes=[("ya", fc)])
            bi += 1
            for fc in range(4):
                for ti, (t0, tn, _s) in enumerate(tiles):
                    bank = (fc * 2 + ti) % 4
                    proj_fm(ws, bi, fc, gbcT, [], t0, tn, bank)
                    P.act(lambda e, fc=fc, t0=t0, tn=tn, bank=bank: e.activation(out=yb[:, fc, t0:t0 + tn], in_=PS[bank][:, 0:tn], func=AF.Copy),
                          reads=[("ps", bank)], writes=[("yb", fc)])
            bi += 1
            for fc in range(4):
                f = fg * 4 + fc
                for ti, (t0, tn, _s) in enumerate(tiles):
                    bank = (fc * 2 + ti) % 4
                    proj_fm(ws, bi, fc, hT, [], t0, tn, bank)
                    P.act(lambda e, fc=fc, t0=t0, tn=tn, bank=bank: e.activation(out=sgm[:, fc, t0:t0 + tn], in_=PS[bank][:, 0:tn],
                                                                                func=AF.Sigmoid),
                          reads=[("ps", bank)], writes=[("sgm", fc)])
                    P.pool(lambda e, fc=fc, t0=t0, tn=tn: e.tensor_tensor(out=yb[:, fc, t0:t0 + tn], in0=yb[:, fc, t0:t0 + tn],
                                                                          in1=sgm[:, fc, t0:t0 + tn], op=ALU.mult),
                           reads=[("yb", fc), ("sgm", fc)], writes=[("yb", fc)])
                    P.pool(lambda e, f=f, fc=fc, t0=t0, tn=tn: e.tensor_tensor(out=mT[:, f, t0:t0 + tn], in0=ya[:, fc, t0:t0 + tn],
                                                                               in1=yb[:, fc, t0:t0 + tn], op=ALU.add),
                           reads=[("ya", fc), ("yb", fc)], writes=[("mT", f)])
            bi += 1
        P.barrier()

        P.dma("sp", GP[:, :], gpost_d[0], "gp", writes=[("GP",)])
        for nb in range(4):
            for ci, (rows, col0) in enumerate(chunks):
                bank = ci % 4
                proj_tm(ws, bi, mT, [], col0, rows, bank)
                P.act(lambda e, rows=rows, ci=ci, nb=nb, bank=bank: e.activation(out=XF[:rows, ci, nb * 512:(nb + 1) * 512], in_=PS[bank][:rows, :],
                                                                                func=AF.Copy),
                      reads=[("ps", bank)], writes=[("XF", ci)])
            bi += 1
        for ci, (rows, col0) in enumerate(chunks):
            col = rstd_col(XF[:rows, ci, :], rows, D, [("XF", ci)], junks[ci % 2][:rows, :], ("junk", ci % 2))
            P.dve(lambda e, rows=rows, ci=ci, col=col: e.scalar_tensor_tensor(out=XF[:rows, ci, :], in0=XF[:rows, ci, :],
                                                                             scalar=st[:rows, col:col + 1], in1=GP[:rows, :],
                                                                             op0=ALU.mult, op1=ALU.mult),
                  reads=[("XF", ci), ("st", col), ("GP",)], writes=[("XF", ci)])
            P.dma("pool", XF[:rows, ci, :], xrows(ci), ("xacc", ci), reads=[("XF", ci)], writes=[("XF", ci)], accum=True)
            P.dma("sp", xmid_d[ci, 0:rows, :], XF[:rows, ci, :], ("xmo", ci), reads=[("XF", ci)])
        norm_transpose(HT, 1, chunks, lambda ci: (XF[:, ci, :], ("XF", ci)))
        P.barrier()

        for fg in range(11):
            for fc in range(4):
                f = fg * 4 + fc
                up = UPf[:, fc % 2, :]
                upkey = ("upf", fc % 2)
                P.dve(lambda e, up=up, f=f: e.tensor_copy(out=up[:, 0:2], in_=halo_f[:, f, :]), reads=[("halo_f", f)], writes=[upkey])
                if has_s:
                    P.dve(lambda e, up=up, f=f: e.tensor_copy(out=up[:, 642:738].rearrange("p (b t) -> p b t", t=6)[:, :, 0:2],
                                                              in_=halo_fs[:, f, :, :]), reads=[("halo_fs", f)], writes=[upkey])
                for ti, (t0, tn, segs) in enumerate(tiles):
                    bank = (fc * 2 + ti) % 4
                    proj_fm(ws, bi, fc, HT, [], t0, tn, bank)
                    for (so, sn, is_s, sc0) in segs:
                        if not is_s:
                            P.act(lambda e, up=up, so=so, sn=sn, sc0=sc0, bank=bank: e.activation(out=up[:, 2 + sc0:2 + sc0 + sn],
                                                                                                 in_=PS[bank][:, so:so + sn], func=AF.Copy),
                                  reads=[("ps", bank)], writes=[upkey])
                        else:
                            P.act(lambda e, up=up, so=so, sn=sn, bank=bank: e.activation(
                                out=up[:, 642:738].rearrange("p (b t) -> p b t", t=6)[:, :, 2:6],
                                in_=PS[bank][:, so:so + sn].rearrange("p (b t) -> p b t", t=4), func=AF.Copy),
                                reads=[("ps", bank)], writes=[upkey])
                P.dve(lambda e, up=up, f=f, TG=TG: e.tensor_copy(out=halo_f[:, f, :], in_=up[:, TG:TG + 2]), reads=[upkey],
                      writes=[("halo_f", f)])
                if has_s:
                    P.dve(lambda e, up=up, f=f: e.tensor_copy(out=halo_fs[:, f, :, :],
                                                              in_=up[:, 642:738].rearrange("p (b t) -> p b t", t=6)[:, :, 4:6]),
                          reads=[upkey], writes=[("halo_fs", f)])
                conv3(gaB[:, fc, 0:TG], gaB[:, fc, TG:TG + 64].rearrange("p (b t) -> p b t", t=4) if has_s else None, up, upkey,
                      ("ga", fc), f, fcw, fcb[:, f:f + 1])
                P.act(lambda e, fc=fc, TT=TT: e.activation(out=gaB[:, fc, 0:TT], in_=gaB[:, fc, 0:TT], func=AF.Gelu_apprx_tanh),
                      reads=[("ga", fc)], writes=[("ga", fc)])
            bi += 1
            for fc in range(4):
                f = fg * 4 + fc
                for ti, (t0, tn, _s) in enumerate(tiles):
                    bank = (fc * 2 + ti) % 4
                    proj_fm(ws, bi, fc, HT, [], t0, tn, bank)
                    P.dve(lambda e, f=f, fc=fc, t0=t0, tn=tn, bank=bank: e.tensor_tensor(
                        out=aT[:, f, t0:t0 + tn], in0=PS[bank][:, 0:tn], in1=gaB[:, fc, t0:t0 + tn], op=ALU.mult),
                        reads=[("ps", bank), ("ga", fc)], writes=[("aT", f)])
            bi += 1
        P.barrier()

        for nb in range(4):
            for sub in range(3):
                nkc = 16 if sub < 2 else 12
                for ci, (rows, col0) in enumerate(chunks):
                    proj_tm(ws, bi, aT, [], col0, rows, ci, nkc=nkc, kc0=sub * 16, start=(sub == 0), stop=(sub == 2))
                    if sub == 2:
                        P.act(lambda e, rows=rows, ci=ci, nb=nb: e.activation(out=XF[:rows, ci, nb * 512:(nb + 1) * 512], in_=PS[ci][:rows, :],
                                                                             func=AF.Copy),
                              reads=[("ps", ci)], writes=[("XF", ci)])
                bi += 1
        P.barrier()
        P.dma("sp", GP[:, :], gpost_d[1], "gp", writes=[("GP",)])
        for ci, (rows, col0) in enumerate(chunks):
            col = rstd_col(XF[:rows, ci, :], rows, D, [("XF", ci)], junks[ci % 2][:rows, :], ("junk", ci % 2))
            P.dve(lambda e, rows=rows, ci=ci, col=col: e.scalar_tensor_tensor(out=XF[:rows, ci, :], in0=XF[:rows, ci, :],
                                                                             scalar=st[:rows, col:col + 1], in1=GP[:rows, :],
                                                                             op0=ALU.mult, op1=ALU.mult),
                  reads=[("XF", ci), ("st", col), ("GP",)], writes=[("XF", ci)])
            P.dma("pool", XF[:rows, ci, :], xmid_d[ci, 0:rows, :], ("xacc", ci), reads=[("XF", ci)], writes=[("XF", ci)], accum=True)
            if ci < nch:
                dst = ym[ps["out0"] + ci * 128: ps["out0"] + (ci + 1) * 128, :]
            else:
                dst = ys[:, :]
            P.dma("sp", dst, XF[:rows, ci, :], ("yo", ci), reads=[("XF", ci)])
        P.barrier()

    out_rows(lambda f: halo_f[:, f, :], NFF, 2, ffnp)
    out_rows(lambda f: halo_c[:, f, :], 16, 2, convp)
    out_rows(lambda f: halo_fs[:, f, :, :].rearrange("p b t -> p (b t)"), NFF, 32, ffns)
    out_rows(lambda f: halo_cs[:, f, :, :].rearrange("p b t -> p (b t)"), 16, 32, convs)

    P.emit()
    return nc


def _tables(core):
    half = core % 2
    f32 = np.float32
    inv = (f32(10000.0) ** (-(np.arange(128, dtype=f32) / f32(128)))).astype(f32)

    def cs(pos):
        ang = (np.asarray(pos, dtype=f32)[:, None] * inv[None, :]).astype(f32)
        return np.cos(ang).astype(np.float64), np.sin(ang).astype(np.float64)

    logg = np.array(LOGG, dtype=np.float64)
    m = np.arange(NMAIN)
    pos_main = np.maximum(half * 1024 - 128 + m, 0)
    t_main = 896 + m
    pos_pre = np.arange(NPRE)
    t_pre = np.arange(NPRE)
    tau = np.arange(NSAMP) % 4
    pos_s = 16384 + tau
    cm, sm = cs(pos_main)
    cp, sp_ = cs(pos_pre)
    c_s, s_s = cs(pos_s)

    cq = np.zeros((2, NH, 128, 640), np.float64)
    sq = np.zeros((2, NH, 128, 640), np.float64)
    ck = np.zeros((3, 128, 7, 128), np.float64)
    sk = np.zeros((3, 128, 7, 128), np.float64)
    ksc = np.zeros((128, 3, 7, NH), np.float64)
    ck[0] = cp.reshape(7, 128, 128).transpose(1, 0, 2)
    sk[0] = sp_.reshape(7, 128, 128).transpose(1, 0, 2)
    for h in range(NH):
        ksc[:, 0, :, h] = (np.exp(-logg[h] * (t_pre - TOFF)) / 16.0).reshape(7, 128).T
    for p, (row0, nch) in ((1, (0, 5)), (2, (640, 4))):
        rows = slice(row0, row0 + nch * 128)
        ck[p, :, :nch] = cm[rows].reshape(nch, 128, 128).transpose(1, 0, 2)
        sk[p, :, :nch] = sm[rows].reshape(nch, 128, 128).transpose(1, 0, 2)
        for h in range(NH):
            dq = np.exp(logg[h] * (t_main[rows] - TOFF))
            cq[p - 1, h, :, :nch * 128] = (cm[rows] * dq[:, None]).T
            sq[p - 1, h, :, :nch * 128] = (sm[rows] * dq[:, None]).T
            ksc[:, p, :nch, h] = (np.exp(-logg[h] * (t_main[rows] - TOFF)) / 16.0).reshape(nch, 128).T
    ck[2, :64, 4] = c_s
    sk[2, :64, 4] = s_s
    for h in range(NH):
        dq = np.exp(logg[h] * (tau + 1.0))
        cq[1, h, :, 512:576] = (c_s * dq[:, None]).T
        sq[1, h, :, 512:576] = (s_s * dq[:, None]).T
        ksc[:64, 2, 4, h] = np.exp(-logg[h] * (tau + 1.0)) / 16.0
    i = np.arange(128)
    cmask = (i[None, :] >= i[:, None]).astype(f32)
    j64 = np.arange(64)
    bmask = ((j64[None, :] // 4 == j64[:, None] // 4) & (j64[None, :] >= j64[:, None])).astype(f32)
    bm16 = np.broadcast_to((j64[None, :] // 4 == np.arange(NB)[:, None]).astype(f32)[None], (128, NB, 64)).copy()
    bmT = (j64[:, None] // 4 == np.arange(NB)[None, :]).astype(f32)
    return dict(cq=cq.astype(f32), sq=sq.astype(f32), ck=ck.astype(f32), sk=sk.astype(f32), ksc=ksc.astype(f32),
                cmask=cmask, bmask=bmask, bm16=bm16, bmT=bmT, idf=np.eye(128, dtype=f32))


def _in_map(core, x_prompt, x_sample, state_ret, state_conv, state_ffn, g_pre_mix, w_in, conv_w, p_ret, p_conv, w_o, g_post_mix,
            g_pre_ffn, w_up, w_gate, ffn_conv_w, ffn_conv_b, w_down, g_post_ffn):
    f32 = np.float32
    b, half = core // 2, core % 2
    xm = np.zeros((NMAIN, D), f32)
    xp = np.zeros((NPRE, D), f32)
    if half == 0:
        xm[128:] = x_prompt[b, 0:1024]
    else:
        xm[:] = x_prompt[b, 896:2048]
        xp[:] = x_prompt[b, 0:896]
    sl = slice(core * NB, (core + 1) * NB)
    mp = dict(
        xm=xm, xp=xp, xs=np.ascontiguousarray(x_sample[sl].reshape(NSAMP, D)),
        sret=np.ascontiguousarray(state_ret[0, sl]),
        sconv=np.ascontiguousarray(state_conv[0, sl].reshape(2 * NB, D)),
        sffn=np.ascontiguousarray(state_ffn[0, sl].reshape(2 * NB, DFF)),
        w_in=w_in[0], p_ret=p_ret[0], p_conv=p_conv[0], w_o=w_o[0], w_up=w_up[0], w_gate=w_gate[0], w_down=w_down[0],
        gcol=np.ascontiguousarray(np.stack([g_pre_mix[0].reshape(16, 128).T, g_pre_ffn[0].reshape(16, 128).T], axis=1)),
        gpost=np.ascontiguousarray(np.stack([np.broadcast_to(g_post_mix[0], (128, D)), np.broadcast_to(g_post_ffn[0], (128, D))])),
        cw=np.ascontiguousarray(conv_w[0].T.reshape(16, 128, 3).transpose(1, 0, 2)),
        fcw=np.ascontiguousarray(ffn_conv_w[0].T.reshape(NFF, 128, 3).transpose(1, 0, 2)),
        fcb=np.ascontiguousarray(ffn_conv_b[0].reshape(NFF, 128).T),
    )
    mp.update(_tables(core))
    return {k: np.ascontiguousarray(v, dtype=f32) for k, v in mp.items()}


_NC_CACHE = {}


def _run(inputs, cores):
    if "nc" not in _NC_CACHE:
        _NC_CACHE["nc"] = build_program()
    nc = _NC_CACHE["nc"]
    inputs = {k: np.asarray(v) for k, v in inputs.items()}
    in_maps = [_in_map(c, **inputs) for c in cores]
    res = run_bass_kernel_spmd(nc, in_maps, core_ids=list(range(len(cores))))
    return res.results


def kernel(**inputs):
    f32 = np.float32
    results = _run(inputs, list(range(NCORES)))
    B, S = 4, 2048
    y_prompt = np.zeros((B, S, D), f32)
    y_sample = np.zeros((128, 4, D), f32)
    ret_prompt = np.zeros((1, B, NH, DK, DV), f32)
    conv_prompt = np.zeros((1, B, 2, D), f32)
    ffn_prompt = np.zeros((1, B, 2, DFF), f32)
    ret_sample = np.zeros((1, 128, NH, DK, DV), f32)
    conv_sample = np.zeros((1, 128, 2, D), f32)
    ffn_sample = np.zeros((1, 128, 2, DFF), f32)
    for c, r in enumerate(results):
        b, half = c // 2, c % 2
        y_prompt[b, half * 1024:(half + 1) * 1024] = r["ym"][128:]
        sl = slice(c * NB, (c + 1) * NB)
        y_sample[sl] = r["ys"].reshape(NB, 4, D)
        ret_sample[0, sl] = r["rets"]
        conv_sample[0, sl] = r["convs"].reshape(NB, 2, D)
        ffn_sample[0, sl] = r["ffns"].reshape(NB, 2, DFF)
        if half == 1:
            ret_prompt[0, b] = r["retp"]
            conv_prompt[0, b] = r["convp"]
            ffn_prompt[0, b] = r["ffnp"]
    return (y_prompt, y_sample, ret_prompt, conv_prompt, ffn_prompt, ret_sample, conv_sample, ffn_sample)
```

```python
import math
import numpy as np
import concourse.bass as bass
import concourse.mybir as mybir
from concourse.bass_utils import run_bass_kernel_spmd

F32 = mybir.dt.float32
BF16 = mybir.dt.bfloat16
ALU = mybir.AluOpType
AF = mybir.ActivationFunctionType

D = 2048
NH = 8
DK = 256
DV = 512
DFF = 5632
NFF = DFF // 128
EPS = 1e-6
NCORES = 8
NPRE = 896
NMAIN = 1152
NSAMP = 64
NB = 16
TOFF = 1024.0
LOGG = [math.log1p(-2.0 ** (-5 - h)) for h in range(NH)]
C_Q, C_K, C_V, C_G, C_GB, C_GC, C_HC, C_GA, C_GBT = 0, 2048, 4096, 8192, 12288, 14336, 16384, 18432, 20480

PASSES = [
    dict(src="xp", row0=0, nch=7, kv=True, sample=False, t0=0, out0=None),
    dict(src="xm", row0=0, nch=5, kv=False, sample=False, t0=896, out0=0),
    dict(src="xm", row0=640, nch=4, kv=False, sample=True, t0=896 + 640, out0=640),
]


class _Op:
    __slots__ = ("eng", "fn", "deps", "signal", "ev", "dkey")


class Prog:
    def __init__(self, nc):
        self.nc = nc
        self.ops = []
        self.last_w = {}
        self.readers = {}
        self.dkeys = []

    def add(self, eng, fn, reads=(), writes=(), dkey=None):
        op = _Op()
        op.eng, op.fn, op.signal, op.ev, op.dkey = eng, fn, False, None, dkey
        deps = set()
        for r in reads:
            w = self.last_w.get(r)
            if w is not None:
                deps.add(w)
        for wr in writes:
            w = self.last_w.get(wr)
            if w is not None:
                deps.add(w)
            deps.update(self.readers.get(wr, ()))
        idx = len(self.ops)
        for r in reads:
            self.readers.setdefault(r, []).append(idx)
        for wr in writes:
            self.last_w[wr] = idx
            self.readers[wr] = []
        op.deps = deps
        self.ops.append(op)
        if dkey is not None and dkey not in self.dkeys:
            self.dkeys.append(dkey)
        return idx

    def pe(self, fn, reads=(), writes=()):
        return self.add("pe", fn, reads, writes)

    def act(self, fn, reads=(), writes=()):
        return self.add("act", fn, reads, writes)

    def dve(self, fn, reads=(), writes=()):
        return self.add("dve", fn, reads, writes)

    def pool(self, fn, reads=(), writes=()):
        return self.add("pool", fn, reads, writes)

    def dma(self, q, out, in_, key, reads=(), writes=(), accum=False):
        if accum:
            return self.add(q, lambda e: e.dma_start(out=out, in_=in_, accum_op=ALU.add), reads, writes, dkey=key)
        return self.add(q, lambda e: e.dma_start(out=out, in_=in_), reads, writes, dkey=key)

    def barrier(self):
        last = {}
        for i, op in enumerate(self.ops):
            if op.fn is not None:
                last[("e", op.eng) if op.dkey is None else ("d", op.dkey)] = i
        deps = set(last.values())
        for eng in ("pe", "act", "dve", "pool", "sp"):
            op = _Op()
            op.eng, op.fn, op.signal, op.ev, op.dkey = eng, None, False, None, None
            op.deps = set(deps)
            self.ops.append(op)
        self.last_w = {}
        self.readers = {}

    def emit(self):
        nc = self.nc
        engs = {"pe": nc.tensor, "act": nc.scalar, "dve": nc.vector, "pool": nc.gpsimd, "sp": nc.sync}
        for op in self.ops:
            for d in op.deps:
                if self.ops[d].dkey is None:
                    self.ops[d].signal = True
        esem = {e: nc.alloc_semaphore("es_" + e) for e in engs}
        dsem = {k: nc.alloc_semaphore("ds_%d" % i) for i, k in enumerate(self.dkeys)}
        cnt = {e: 0 for e in engs}
        dcnt = {k: 0 for k in self.dkeys}
        waited = {e: {} for e in engs}
        for op in self.ops:
            E = engs[op.eng]
            wl = {}
            for d in op.deps:
                p = self.ops[d]
                if p.fn is None:
                    continue
                if op.eng == "pe" and p.eng == "pe" and p.dkey is None:
                    continue
                name, sem, val = p.ev
                if wl.get(name, (None, 0))[1] < val:
                    wl[name] = (sem, val)
            for name, (sem, val) in wl.items():
                if waited[op.eng].get(name, 0) >= val:
                    continue
                E.wait_ge(sem, val)
                waited[op.eng][name] = val
            if op.fn is None:
                continue
            ins = op.fn(E)
            if op.dkey is not None:
                dcnt[op.dkey] += 16
                ins.then_inc(dsem[op.dkey], 16)
                op.ev = (("d", op.dkey), dsem[op.dkey], dcnt[op.dkey])
            elif op.signal:
                cnt[op.eng] += 1
                ins.then_inc(esem[op.eng], 1)
                op.ev = (("e", op.eng), esem[op.eng], cnt[op.eng])
        for k in self.dkeys:
            if dcnt[k]:
                nc.sync.wait_ge(dsem[k], dcnt[k])
        for e in engs:
            if cnt[e] and e != "sp":
                nc.sync.wait_ge(esem[e], cnt[e])


def build_program():
    nc = bass.Bass("TRN2", target_bir_lowering=False)
    P = Prog(nc)

    def din(name, shape):
        return nc.dram_tensor(name, list(shape), F32, kind="ExternalInput").ap()

    def dout(name, shape):
        return nc.dram_tensor(name, list(shape), F32, kind="ExternalOutput").ap()

    xm = din("xm", [NMAIN, D])
    xp = din("xp", [NPRE, D])
    xs = din("xs", [NSAMP, D])
    sret = din("sret", [NB, NH, DK, DV])
    sconv = din("sconv", [2 * NB, D])
    sffn = din("sffn", [2 * NB, DFF])
    w_in = din("w_in", [D, 22528])
    p_ret = din("p_ret", [4096, D])
    p_conv = din("p_conv", [D, D])
    w_o = din("w_o", [D, D])
    w_up = din("w_up", [D, DFF])
    w_gate = din("w_gate", [D, DFF])
    w_down = din("w_down", [DFF, D])
    gcol_d = din("gcol", [128, 2, 16])
    gpost_d = din("gpost", [2, 128, D])
    cw_d = din("cw", [128, 16, 3])
    fcw_d = din("fcw", [128, NFF, 3])
    fcb_d = din("fcb", [128, NFF])
    cq_d = din("cq", [2, NH, 128, 640])
    sq_d = din("sq", [2, NH, 128, 640])
    ck_d = din("ck", [3, 128, 7, 128])
    sk_d = din("sk", [3, 128, 7, 128])
    ksc_d = din("ksc", [128, 3, 7, NH])
    cmask_d = din("cmask", [128, 128])
    bmask_d = din("bmask", [64, 64])
    bm16_d = din("bm16", [128, NB, 64])
    bmT_d = din("bmT", [64, NB])
    idf_d = din("idf", [128, 128])

    ym = dout("ym", [NMAIN, D])
    ys = dout("ys", [NSAMP, D])
    retp = dout("retp", [NH, DK, DV])
    convp = dout("convp", [2, D])
    ffnp = dout("ffnp", [2, DFF])
    rets = dout("rets", [NB, NH, DK, DV])
    convs = dout("convs", [2 * NB, D])
    ffns = dout("ffns", [2 * NB, DFF])

    xmid_d = nc.dram_tensor("xmid_scr", [6, 128, D], F32, kind="Internal").ap()
    sbase_d = nc.dram_tensor("sbase_scr", [NH, 128, 2 * DV], F32, kind="Internal").ap()

    def sb(name, shape, dt):
        return nc.alloc_sbuf_tensor("s_" + name, list(shape), dt)

    HT = sb("HT", [128, 16, 640], BF16)
    R1 = sb("R1", [128, 10240], F32)
    R2 = sb("R2", [128, 5120], F32)
    R4 = sb("R4", [128, 16384], F32)
    WB = [sb("WB%d" % i, [128, 16, 512], BF16) for i in range(3)]
    ident = sb("ident", [128, 128], BF16)
    identf = sb("identf", [128, 128], F32)
    cmask = sb("cmask", [128, 128], F32)
    bmask = sb("bmask", [64, 64], F32)
    bm16 = sb("bm16", [128, NB, 64], BF16)
    bmT = sb("bmT", [64, NB], F32)
    gcol = sb("gcol", [128, 2, 16], F32)
    cw = sb("cw", [128, 16, 3], F32)
    fcw = sb("fcw", [128, NFF, 3], F32)
    fcb = sb("fcb", [128, NFF], F32)
    ksc = sb("ksc", [128, 3, 7, NH], F32)
    halo_c = sb("halo_c", [128, 16, 2], F32)
    halo_f = sb("halo_f", [128, NFF, 2], F32)
    halo_cs = sb("halo_cs", [128, 16, NB, 2], F32)
    halo_fs = sb("halo_fs", [128, NFF, NB, 2], F32)
    st = sb("st", [128, 64], F32)
    otm = sb("otm", [32, 512], F32)

    def _view(reg, off_b, shape, bf):
        n = int(np.prod(shape))
        if bf:
            v = reg[:, off_b // 4:(off_b + 2 * n + 3) // 4].bitcast(BF16)
        else:
            v = reg[:, off_b // 4:off_b // 4 + n]
        if len(shape) == 1:
            return v
        names = " ".join("d%d" % i for i in range(len(shape)))
        kw = {"d%d" % i: s for i, s in enumerate(shape)}
        return v.rearrange("p (%s) -> p %s" % (names, names), **kw)

    def r_bf(reg, off_b, shape):
        return _view(reg, off_b, shape, True)

    def r_f32(reg, off_b, shape):
        return _view(reg, off_b, shape, False)

    goT = r_bf(R1, 0, [32, 640])
    XF = r_f32(R1, 0, [5, 2048])
    hT0 = r_bf(R1, 0, [16, 896])
    gbcT = r_bf(R2, 0, [16, 640])
    Sf = r_f32(R2, 0, [4, 2, 512])
    Sbf = r_bf(R2, 16384, [2, 2, 512])
    GP = r_f32(R2, 0, [2048])
    mT = r_bf(R4, 0, [16, 640])
    aT = r_bf(R4, 0, [NFF, 640])
    junk = r_bf(R4, 40960, [2048])
    XT = [r_f32(R4, 49152, [2048]), r_f32(R4, 57344, [2048]), r_f32(R4, 24576, [2048]), r_f32(R4, 32768, [2048])]
    dumpA = r_bf(R4, 20480, [2048])
    XTh = [r_f32(R2, 0, [2048]), r_f32(R2, 8192, [2048])]
    junkh = r_bf(R2, 16384, [2048])
    off = {"o": 0}

    def tb(shape, bf):
        n = int(np.prod(shape)) * (2 if bf else 4)
        n = (n + 3) // 4 * 4
        v = _view(R4, off["o"], shape, bf)
        off["o"] += n
        return v

    qT = tb([2, 2, 640], True)
    kTM = tb([7, 2, 256], True)
    kT = tb([2, 2, 640], True)
    vS = tb([2, 512], True)
    sg = tb([5, 512], True)
    cqt = tb([640], False)
    sqt = tb([640], False)
    ckt = tb([7, 128], False)
    skt = tb([7, 128], False)
    Sb = tb([2, 2, 512], True)
    base = tb([2, 512], False)
    rt1 = tb([320], False)
    rt2 = tb([320], False)
    PT = tb([2, 128], True)
    gotm = tb([2, 512], True)
    qTs = tb([2, 2, 64], True)
    khat = tb([2, 256], True)
    vs = tb([2, 512], True)
    osum = tb([2, 512], False)
    sgs = tb([2, 512], True)
    khmb = tb([2, 256], True)
    rt3 = tb([320], False)
    rt4 = tb([320], False)
    junkB = tb([512], True)
    assert off["o"] <= 65536, off["o"]
    gcs = r_f32(R4, 20480, [4, 640])
    UPc = r_f32(R4, 20480 + 10240, [4, 740])
    cv = r_f32(R4, 20480 + 10240 + 11840, [4, 640])
    ya = r_f32(R4, 20480, [4, 640])
    yb = r_f32(R4, 20480 + 10240, [4, 640])
    sgm = r_f32(R4, 20480 + 20480, [4, 640])
    UPf = r_f32(R2, 0, [2, 740])
    gaB = r_f32(R2, 5920, [4, 640])

    PS = [nc.alloc_psum_tensor("ps%d" % i, [128, 512], F32) for i in range(8)]

    def psb(i):
        return PS[i][:, :].bitcast(BF16).rearrange("p (a b) -> p a b", b=128)

    def cload(dst, src, key, q="sp"):
        P.dma(q, dst, src, key, writes=[("c", key)])

    cload(ident[:], idf_d[:, :], "c_id", q="pool")
    cload(identf[:], idf_d[:, :], "c_idf")
    cload(cmask[:], cmask_d[:, :], "c_cm")
    cload(bmask[:], bmask_d[:, :], "c_bm")
    cload(bm16[:], bm16_d[:, :, :], "c_bm16", q="pool")
    cload(bmT[:], bmT_d[:, :], "c_bmT")
    cload(gcol[:], gcol_d[:, :, :], "c_gcol")
    cload(cw[:], cw_d[:, :, :], "c_cw")
    cload(fcw[:], fcw_d[:, :, :], "c_fcw")
    cload(fcb[:], fcb_d[:, :], "c_fcb")
    cload(ksc[:], ksc_d[:, :, :, :], "c_ksc")
    P.dve(lambda e: e.memset(halo_c[:], 0.0), writes=[("halo_c",)])
    P.dve(lambda e: e.memset(halo_f[:], 0.0), writes=[("halo_f",)])
    tmpc = r_f32(R1, 0, [2048])
    tmpf = r_f32(R1, 8192, [DFF])
    P.dma("sp", tmpc[:32, :], sconv[:, :], "c_tc", writes=[("tmpc",)])
    P.dma("sp", tmpf[:32, :], sffn[:, :], "c_tf", writes=[("tmpf",)])
    P.barrier()

    def halo_init(tmp, dst, nf):
        for g0 in range(0, nf, 16):
            n = min(16, nf - g0)
            bank = (g0 // 16) % 2

            def f(e, g0=g0, n=n, bank=bank):
                ins = None
                for j in range(n):
                    ins = e.transpose(PS[bank][:, j * 32:(j + 1) * 32], tmp[:32, (g0 + j) * 128:(g0 + j + 1) * 128], identf[:32, :32])
                return ins

            P.pe(f, writes=[("ps", bank)])
            P.act(lambda e, g0=g0, n=n, bank=bank: e.activation(
                out=dst[:, g0:g0 + n, :, :].rearrange("p f b t -> p f (b t)"),
                in_=PS[bank][:, 0:n * 32].rearrange("p (f x) -> p f x", x=32), func=AF.Copy),
                reads=[("ps", bank)], writes=[("halo_s", g0)])

    halo_init(tmpc, halo_cs, 16)
    halo_init(tmpf, halo_fs, NFF)
    P.barrier()

    wstate = {"n": 0}

    def wload(w2d, r0, c0, nkc=16, ncol=512):
        slot = wstate["n"] % 3
        wstate["n"] += 1
        src = w2d[r0:r0 + nkc * 128, c0:c0 + ncol].rearrange("(kc p) n -> p kc n", p=128)
        P.dma("pool", WB[slot][:, 0:nkc, 0:ncol], src, ("wb", slot), writes=[("WB", slot)])
        return slot

    class WStream:
        def __init__(self, blocks):
            self.blocks = blocks
            self.issued = []

        def get(self, i):
            while len(self.issued) < min(len(self.blocks), i + 3):
                b = self.blocks[len(self.issued)]
                self.issued.append(wload(*b))
            return self.issued[i]

    st_n = {"n": 0}

    def stcol():
        i = st_n["n"] % 64
        st_n["n"] += 1
        return i

    def rstd_col(src_ap, rows, n, reads, junk_ap, junk_key):
        col = stcol()
        P.act(lambda e: e.activation(out=junk_ap, in_=src_ap, func=AF.Square, accum_out=st[:rows, col:col + 1]),
              reads=list(reads), writes=[("st", col)] + ([junk_key] if junk_key is not None else []))
        P.act(lambda e: e.activation(out=st[:rows, col:col + 1], in_=st[:rows, col:col + 1], func=AF.Sqrt, bias=EPS, scale=1.0 / n),
              reads=[("st", col)], writes=[("st", col)])
        P.dve(lambda e: e.reciprocal(out=st[:rows, col:col + 1], in_=st[:rows, col:col + 1]), reads=[("st", col)], writes=[("st", col)])
        return col

    junks = [junk, r_bf(R4, 45056, [2048])]
    FG = dict(junks=junks, jkeys=[("junk", 0), ("junk", 1)], dump=dumpA)
    BGB = dict(junks=[junkh], jkeys=[("junkh",)], dump=None)

    def norm_stages(dst, gi, chunks, load_fn, bufs):
        out = []
        nj = len(bufs["junks"])
        for ci, (rows, col0) in enumerate(chunks):
            jb = bufs["junks"][ci % nj]
            jk = bufs["jkeys"][ci % nj]

            def s1(ci=ci, rows=rows, jb=jb, jk=jk):
                src, skey = load_fn(ci)
                if bufs["dump"] is not None:
                    col = rstd_col(src[:rows, :], rows, D, [skey], bufs["dump"][:rows, :], None)
                else:
                    col = rstd_col(src[:rows, :], rows, D, [skey], jb[:rows, :], jk)
                P.dve(lambda e: e.tensor_scalar(out=jb[:rows, :], in0=src[:rows, :], scalar1=st[:rows, col:col + 1], scalar2=None,
                                                op0=ALU.mult),
                      reads=[skey, ("st", col)], writes=[jk])

            def s2(ci=ci, rows=rows, col0=col0, jb=jb, jk=jk):
                for g4 in range(4):
                    bank = 6 + (g4 % 2)
                    pv = psb(bank)

                    def tr(e, g4=g4, pv=pv):
                        ins = None
                        for j in range(4):
                            kc = g4 * 4 + j
                            ins = e.transpose(pv[:, j, 0:rows], jb[:rows, kc * 128:(kc + 1) * 128], ident[:rows, :rows])
                        return ins

                    P.pe(tr, reads=[jk], writes=[("ps", bank)])
                    P.dve(lambda e, g4=g4, pv=pv: e.tensor_tensor(
                        out=dst[:, g4 * 4:(g4 + 1) * 4, col0:col0 + rows], in0=pv[:, 0:4, 0:rows],
                        in1=gcol[:, gi, g4 * 4:(g4 + 1) * 4].unsqueeze(2).to_broadcast([128, 4, rows]), op=ALU.mult),
                        reads=[("ps", bank)], writes=[("hT", g4, ci)])

            out.append((s1, s2))
        return out

    def norm_transpose(dst, gi, chunks, load_fn):
        stg = norm_stages(dst, gi, chunks, load_fn, FG)
        stg[0][0]()
        for ci in range(len(stg)):
            if ci + 1 < len(stg):
                stg[ci + 1][0]()
            stg[ci][1]()

    bgA = []

    pa = {"n": 0}

    def pumpA(n=1, every=1):
        pa["n"] += 1
        if pa["n"] % every != 0:
            return
        for _ in range(n):
            if bgA:
                bgA.pop(0)()

    def hT_reads(nchunks):
        return [("hT", g4, ci) for g4 in range(4) for ci in range(nchunks)]

    def proj_fm(ws, bi, fc, hsrc, hreads, col0, ncols, bank, nkc=16, kc0=0, mid=None):
        slot = ws.get(bi)
        parts = [(0, nkc)] if mid is None else [(0, nkc // 2), (nkc // 2, nkc)]
        for pi_, (ka, kb) in enumerate(parts):
            def f(e, ka=ka, kb=kb):
                ins = None
                for kc in range(ka, kb):
                    ins = e.matmul(PS[bank][:, 0:ncols], lhsT=WB[slot][:, kc, fc * 128:(fc + 1) * 128],
                                   rhs=hsrc[:, kc0 + kc, col0:col0 + ncols], start=(kc == 0), stop=(kc == nkc - 1), skip_group_check=True)
                return ins

            P.pe(f, reads=[("WB", slot)] + list(hreads), writes=[("ps", bank)])
            if mid is not None and pi_ == 0:
                mid()

    def proj_tm(ws, bi, hsrc, hreads, col0, rows, bank, nkc=16, kc0=0, start=True, stop=True, mid=None):
        slot = ws.get(bi)
        parts = [(0, nkc)] if mid is None else [(0, nkc // 2), (nkc // 2, nkc)]
        for pi_, (ka, kb) in enumerate(parts):
            def f(e, ka=ka, kb=kb):
                ins = None
                for kc in range(ka, kb):
                    ins = e.matmul(PS[bank][:rows, :], lhsT=hsrc[:, kc0 + kc, col0:col0 + rows],
                                   rhs=WB[slot][:, kc, :], start=(start and kc == 0), stop=(stop and kc == nkc - 1),
                                   skip_group_check=True)
                return ins

            P.pe(f, reads=[("WB", slot)] + list(hreads), writes=[("ps", bank)])
            if mid is not None and pi_ == 0:
                mid()

    ep_n = {"n": 0}
    late = []
    bg = []

    def flush_late(keep=0):
        while len(late) > keep:
            late.pop(0)()

    bgn = {"n": 0}

    def pump(n, banks=(7,)):
        for _ in range(n):
            if bg:
                bk = banks[bgn["n"] % len(banks)]
                bgn["n"] += 1
                bg.pop(0)(bk)

    def retention_epilogue(h, rows, col0, o_ap, o_key, sg_ap, sg_key):
        col = rstd_col(o_ap, rows, DV, [o_key], junkB[:rows, :], ("junkB",))
        gs = ep_n["n"] % 2
        ep_n["n"] += 1
        P.dve(lambda e: e.scalar_tensor_tensor(out=gotm[:rows, gs, :], in0=o_ap, scalar=st[:rows, col:col + 1],
                                               in1=sg_ap, op0=ALU.mult, op1=ALU.mult),
              reads=[o_key, ("st", col), sg_key], writes=[("gotm", gs)])

        def part2():
            pv = psb(6)

            def tr(e):
                ins = None
                for ec in range(4):
                    ins = e.transpose(pv[:, ec, 0:rows], gotm[:rows, gs, ec * 128:(ec + 1) * 128], ident[:rows, :rows])
                return ins

            P.pe(tr, reads=[("gotm", gs)], writes=[("ps", 6)])
            P.act(lambda e: e.activation(out=goT[:, h * 4:(h + 1) * 4, col0:col0 + rows], in_=pv[:, 0:4, 0:rows], func=AF.Copy),
                  reads=[("ps", 6)], writes=[("goT", h, col0)])

        late.append(part2)

    def sample_retention(h, hh, ci, TG, cx):
        g4 = math.exp(4.0 * LOGG[h])
        while bg:
            pump(1)
            flush_late()

        def sc_mm(e):
            ins = None
            for half in range(2):
                ins = e.matmul(PS[4][:64, 0:64], lhsT=kT[:, hh, half, TG:TG + 64], rhs=qT[:, hh, half, TG:TG + 64],
                               start=(half == 0), stop=(half == 1))
            return ins

        P.pe(sc_mm, reads=[("kT", hh, ci), ("qT", hh)], writes=[("ps", 4)])
        P.dve(lambda e: e.tensor_tensor(out=PT[:64, 0, 0:64], in0=PS[4][:64, 0:64], in1=bmask[:, :], op=ALU.mult),
              reads=[("ps", 4)], writes=[("PT", 0)])
        P.act(lambda e: e.mul(out=khat[:64, cx, :], in_=kTM[:64, ci, hh, :], mul=g4), reads=[("kTM", ci, hh)], writes=[("khat", cx)])
        P.act(lambda e: e.activation(out=qTs[:, cx], in_=qT[:, hh, :, TG:TG + 64], func=AF.Copy), reads=[("qT", hh)], writes=[("qTs", cx)])
        P.pe(lambda e: e.matmul(PS[7][:64, :], lhsT=PT[:64, 0, 0:64], rhs=vs[:64, cx, :], start=True, stop=True),
             reads=[("PT", 0), ("vs", cx)], writes=[("ps", 7)])
        P.act(lambda e: e.activation(out=osum[:64, cx, :], in_=PS[7][:64, :], func=AF.Copy), reads=[("ps", 7)], writes=[("osum", cx)])

        def load(b):
            P.dma("sp", Sf[:, b % 4], sret[b, h].rearrange("(a p) e -> p a e", p=128), ("sfi", b % 4), writes=[("Sf", b % 4)])

        for b in range(3):
            load(b)

        def cast(b):
            P.act(lambda e: e.activation(out=Sbf[:, b % 2], in_=Sf[:, b % 4], func=AF.Copy), reads=[("Sf", b % 4)], writes=[("Sbf", b % 2)])

        cast(0)

        def mk_cross(b):
            def f(bk):
                s4, s2 = b % 4, b % 2
                P.act(lambda e: e.mul(out=khmb[:64, s2, :], in_=khat[:64, cx, :], mul=bmT[:, b:b + 1]),
                      reads=[("khat", cx)], writes=[("khmb", s2)])

                def mm(e):
                    ins = None
                    for half in range(2):
                        ins = e.matmul(PS[bk][:64, :], lhsT=qTs[:, cx, half, :], rhs=Sbf[:, s2, half, :], start=(half == 0), stop=(half == 1))
                    return ins

                P.pe(mm, reads=[("qTs", cx), ("Sbf", s2)], writes=[("ps", bk)])
                P.dve(lambda e: e.scalar_tensor_tensor(out=osum[:64, cx, :], in0=PS[bk][:64, :], scalar=bmT[:, b:b + 1], in1=osum[:64, cx, :],
                                                       op0=ALU.mult, op1=ALU.add),
                      reads=[("ps", bk), ("osum", cx)], writes=[("osum", cx)])
                if b + 1 < NB:
                    cast(b + 1)
            return f

        def mk_state(b, half):
            def f(bk):
                s4, s2 = b % 4, b % 2
                P.pe(lambda e: e.matmul(PS[bk][:, :], lhsT=khmb[:64, s2, half * 128:(half + 1) * 128], rhs=vs[:64, cx, :], start=True, stop=True),
                     reads=[("khmb", s2), ("vs", cx)], writes=[("ps", bk)])
                P.dve(lambda e: e.scalar_tensor_tensor(out=Sf[:, s4, half, :], in0=Sf[:, s4, half, :], scalar=g4, in1=PS[bk][:, :],
                                                       op0=ALU.mult, op1=ALU.add),
                      reads=[("ps", bk), ("Sf", s4)], writes=[("Sf", s4)])
                if half == 1:
                    P.dma("sp", rets[b, h].rearrange("(a p) e -> p a e", p=128), Sf[:, s4], ("sfo", s4), reads=[("Sf", s4)])
                    if b + 3 < NB:
                        load(b + 3)
            return f

        for b in range(NB):
            bg.append(mk_cross(b))
            bg.append(mk_state(b, 0))
            bg.append(mk_state(b, 1))

        def fin(bk):
            retention_epilogue(h, 64, TG, osum[:64, cx, :], ("osum", cx), sgs[:64, cx, :], ("sgs", cx))
        bg.append(fin)

    ostage = r_f32(R4, 0, [DFF])
    on = {"n": 0}

    def out_rows(src_fn, nf, rows, dst):
        k = on["n"]
        on["n"] += 1
        stg = ostage if k % 2 == 0 else r_f32(R2, 8192, [2048])
        for g0 in range(0, nf, 4):
            bank = (g0 // 4) % 4

            def f(e, g0=g0, bank=bank):
                ins = None
                for j in range(4):
                    ins = e.transpose(PS[bank][:rows, j * 128:(j + 1) * 128], src_fn(g0 + j), identf[:, :])
                return ins

            P.pe(f, writes=[("ps", bank)])
            P.act(lambda e, bank=bank, g0=g0, stg=stg: e.activation(out=stg[:rows, g0 * 128:(g0 + 4) * 128], in_=PS[bank][:rows, :], func=AF.Copy),
                  reads=[("ps", bank)], writes=[("ostg", k % 2)])
        P.dma("sp", dst[:, :], stg[:rows, 0:nf * 128], ("ostg_o", k % 2), reads=[("ostg", k % 2)])

    def enqueue_phaseA(p):
        ps_ = PASSES[p]
        nch_ = ps_["nch"]
        src_ = xm if ps_["src"] == "xm" else xp
        chunks_ = [(128, c * 128) for c in range(nch_)] + ([(NSAMP, nch_ * 128)] if ps_["sample"] else [])

        def load(ci):
            slot = ci % 2
            rows = chunks_[ci][0]
            srcrows = src_[ps_["row0"] + ci * 128: ps_["row0"] + (ci + 1) * 128, :] if ci < nch_ else xs[:, :]
            P.dma("sp", XTh[slot][:rows, :], srcrows, ("xth", slot), writes=[("XTh", slot)])
            return XTh[slot], ("XTh", slot)

        for (s1, s2) in norm_stages(HT, 0, chunks_, load, BGB):
            bgA.append(s1)
            bgA.append(s2)

    blocks = []
    for ps_ in PASSES:
        kv_ = ps_["kv"]
        for hp in range(4):
            if not kv_:
                blocks.append((w_in, 0, C_Q + 512 * hp))
            blocks.append((w_in, 0, C_K + 512 * hp))
            for hh in range(2):
                h = 2 * hp + hh
                if not kv_:
                    blocks.append((w_in, 0, C_G + 512 * h))
                blocks.append((w_in, 0, C_V + 512 * h))
        if not kv_:
            for fg in range(4):
                blocks += [(w_in, 0, C_GC + 512 * fg), (w_in, 0, C_HC + 512 * fg), (w_in, 0, C_GB + 512 * fg)]
            for fg in range(4):
                blocks += [(p_ret, 0, 512 * fg), (p_ret, 2048, 512 * fg), (w_in, 0, C_GA + 512 * fg), (p_conv, 0, 512 * fg),
                           (w_in, 0, C_GBT + 512 * fg)]
            for nb in range(4):
                blocks += [(w_o, 0, 512 * nb)]
            for fg in range(11):
                blocks += [(w_up, 0, 512 * fg), (w_gate, 0, 512 * fg)]
            for nb in range(4):
                for sub in range(3):
                    blocks += [(w_down, sub * 2048, 512 * nb, 16 if sub < 2 else 12)]
    ws = WStream(blocks)
    bi = 0
    ws.get(0)

    for pi, ps in enumerate(PASSES):
        nch, kv, has_s = ps["nch"], ps["kv"], ps["sample"]
        TG = nch * 128
        TT = TG + (NSAMP if has_s else 0)
        src = xm if ps["src"] == "xm" else xp
        hT = hT0 if kv else HT
        chunks = [(128, c * 128) for c in range(nch)] + ([(NSAMP, TG)] if has_s else [])
        nchk = len(chunks)
        tiles = []
        ntl = (TT + 511) // 512
        tw = ((TT + ntl - 1) // ntl + 31) // 32 * 32
        c0 = 0
        while c0 < TT:
            n = min(tw, TT - c0)
            segs = []
            if c0 < TG:
                segs.append((0, min(n, TG - c0), False, c0))
            if c0 + n > TG:
                a = max(c0, TG)
                segs.append((a - c0, c0 + n - a, True, a))
            tiles.append((c0, n, segs))
            c0 += n

        def xrows(ci, src=src, ps=ps, nch=nch):
            if ci < nch:
                return src[ps["row0"] + ci * 128: ps["row0"] + (ci + 1) * 128, :]
            return xs[:, :]

        def load_x(ci, chunks=chunks, xrows=xrows):
            slot = ci % 4
            rows = chunks[ci][0]
            P.dma("sp", XT[slot][:rows, :], xrows(ci), ("xt", slot), writes=[("XT", slot)])
            return XT[slot], ("XT", slot)

        if pi == 0:
            norm_transpose(hT, 0, chunks, load_x)
            P.barrier()
            enqueue_phaseA(1)
        HR = []
        P.dma("sp", ckt[:, 0:7, :], ck_d[pi], "ckt", writes=[("ckt",)])
        P.dma("sp", skt[:, 0:7, :], sk_d[pi], "skt", writes=[("skt",)])
        NPUMP = 4

        for hp in range(4):
            if not kv:
                qn = 0
                for hh in range(2):
                    h = 2 * hp + hh
                    P.dma("sp", cqt[:, 0:640], cq_d[pi - 1, h], "cqt", writes=[("cqt",)])
                    P.dma("sp", sqt[:, 0:640], sq_d[pi - 1, h], "sqt", writes=[("sqt",)])
                    for ti, (t0, tn, _s) in enumerate(tiles):
                        b0_, b1_ = (0, 1) if qn % 2 == 0 else (2, 3)
                        ta, tb2 = (rt1, rt2) if qn % 2 == 0 else (rt3, rt4)
                        qn += 1
                        proj_fm(ws, bi, 2 * hh, hT, HR, t0, tn, b0_, mid=lambda: pump(1, (7, 6)))
                        pump(1, (7, 6))
                        proj_fm(ws, bi, 2 * hh + 1, hT, HR, t0, tn, b1_, mid=lambda: pump(1, (7, 6)))
                        flush_late()
                        pump(1, (7, 6))
                        x1, x2 = PS[b0_][:, 0:tn], PS[b1_][:, 0:tn]
                        cs, sn = cqt[:, t0:t0 + tn], sqt[:, t0:t0 + tn]
                        ka, kb = ("rt", id(ta)), ("rt", id(tb2))
                        P.dve(lambda e, x1=x1, cs=cs, tn=tn, ta=ta: e.tensor_tensor(out=ta[:, 0:tn], in0=x1, in1=cs, op=ALU.mult),
                              reads=[("ps", b0_), ("cqt",)], writes=[ka])
                        P.dve(lambda e, x2=x2, sn=sn, tn=tn, tb2=tb2: e.tensor_tensor(out=tb2[:, 0:tn], in0=x2, in1=sn, op=ALU.mult),
                              reads=[("ps", b1_), ("sqt",)], writes=[kb])
                        P.pool(lambda e, hh=hh, t0=t0, tn=tn, ta=ta, tb2=tb2: e.tensor_tensor(out=qT[:, hh, 0, t0:t0 + tn], in0=ta[:, 0:tn],
                                                                                           in1=tb2[:, 0:tn], op=ALU.subtract),
                               reads=[ka, kb], writes=[("qT", hh)])
                        P.dve(lambda e, x2=x2, cs=cs, tn=tn, ta=ta: e.tensor_tensor(out=ta[:, 0:tn], in0=x2, in1=cs, op=ALU.mult),
                              reads=[("ps", b1_), ("cqt",)], writes=[ka])
                        P.dve(lambda e, x1=x1, sn=sn, tn=tn, tb2=tb2: e.tensor_tensor(out=tb2[:, 0:tn], in0=x1, in1=sn, op=ALU.mult),
                              reads=[("ps", b0_), ("sqt",)], writes=[kb])
                        P.pool(lambda e, hh=hh, t0=t0, tn=tn, ta=ta, tb2=tb2: e.tensor_tensor(out=qT[:, hh, 1, t0:t0 + tn], in0=ta[:, 0:tn],
                                                                                           in1=tb2[:, 0:tn], op=ALU.add),
                               reads=[ka, kb], writes=[("qT", hh)])
                bi += 1
            for ci, (rows, col0) in enumerate(chunks):
                bank = 2 + (ci % 2)
                proj_tm(ws, bi, hT, HR, col0, rows, bank, mid=lambda: pump(2, (7, 6)))
                flush_late()
                pump(2, (7, 6))
                if kv:
                    pumpA(1, every=4)
                for hh in range(2):
                    h = 2 * hp + hh
                    x1 = PS[bank][:rows, hh * 256: hh * 256 + 128]
                    x2 = PS[bank][:rows, hh * 256 + 128: hh * 256 + 256]
                    cs, sn = ckt[:rows, ci, :], skt[:rows, ci, :]
                    sc = ksc[:rows, pi, ci, h:h + 1]
                    rk = [("ps", bank), ("ckt",), ("skt",)]
                    ta, tb2 = (rt1, rt2) if hh == 0 else (rt3, rt4)
                    ka, kb = ("rt", id(ta)), ("rt", id(tb2))

                    def stt(e, o_, a_, b_, sc=sc):
                        return e.scalar_tensor_tensor(out=o_, in0=a_, scalar=sc, in1=b_, op0=ALU.mult, op1=ALU.mult)

                    P.dve(lambda e, x1=x1, cs=cs, rows=rows, stt=stt, ta=ta: stt(e, ta[:rows, 0:128], x1, cs), reads=rk, writes=[ka])
                    P.dve(lambda e, x2=x2, sn=sn, rows=rows, stt=stt, tb2=tb2: stt(e, tb2[:rows, 0:128], x2, sn), reads=rk, writes=[kb])
                    P.dve(lambda e, x2=x2, cs=cs, rows=rows, stt=stt, ta=ta: stt(e, ta[:rows, 128:256], x2, cs), reads=rk, writes=[ka])
                    P.dve(lambda e, x1=x1, sn=sn, rows=rows, stt=stt, tb2=tb2: stt(e, tb2[:rows, 128:256], x1, sn), reads=rk, writes=[kb])
                    P.pool(lambda e, rows=rows, ci=ci, hh=hh, ta=ta, tb2=tb2: e.tensor_tensor(out=kTM[:rows, ci, hh, 0:128], in0=ta[:rows, 0:128],
                                                                                           in1=tb2[:rows, 0:128], op=ALU.subtract),
                           reads=[ka, kb], writes=[("kTM", ci, hh)])
                    P.pool(lambda e, rows=rows, ci=ci, hh=hh, ta=ta, tb2=tb2: e.tensor_tensor(out=kTM[:rows, ci, hh, 128:256], in0=ta[:rows, 128:256],
                                                                                           in1=tb2[:rows, 128:256], op=ALU.add),
                           reads=[ka, kb], writes=[("kTM", ci, hh)])
                    if not kv:
                        def part2(rows=rows, ci=ci, hh=hh, col0=col0):
                            pv = psb(6)

                            def trk(e):
                                ins = None
                                for half in range(2):
                                    ins = e.transpose(pv[:, half, 0:rows], kTM[:rows, ci, hh, half * 128:(half + 1) * 128], ident[:rows, :rows])
                                return ins

                            P.pe(trk, reads=[("kTM", ci, hh)], writes=[("ps", 6)])
                            P.act(lambda e: e.activation(out=kT[:, hh, :, col0:col0 + rows], in_=pv[:, 0:2, 0:rows], func=AF.Copy),
                                  reads=[("ps", 6)], writes=[("kT", hh, ci)])

                        late.append(part2)
            bi += 1
            for hh in range(2):
                h = 2 * hp + hh
                cx = h % 2
                if not kv:
                    for ci, (rows, col0) in enumerate(chunks):
                        bank = 2 + (ci % 2)
                        proj_tm(ws, bi, hT, HR, col0, rows, bank, mid=lambda: pump(2, (7, 6)))
                        flush_late()
                        pump(2, (7, 6))
                        if ci < nch:
                            P.act(lambda e, rows=rows, ci=ci, bank=bank: e.activation(out=sg[:rows, ci, :], in_=PS[bank][:rows, :], func=AF.Silu),
                                  reads=[("ps", bank)], writes=[("sg", ci)])
                        else:
                            P.act(lambda e, rows=rows, cx=cx, bank=bank: e.activation(out=sgs[:rows, cx, :], in_=PS[bank][:rows, :], func=AF.Silu),
                                  reads=[("ps", bank)], writes=[("sgs", cx)])
                    bi += 1
                if pi == 0:
                    P.dve(lambda e: e.memset(base[:], 0.0), writes=[("base",)])
                else:
                    P.dma("sp", base[:].rearrange("p a b -> p (a b)"), sbase_d[h], "base", writes=[("base",)])
                if not kv:
                    P.act(lambda e: e.activation(out=Sb[:, 0], in_=base[:], func=AF.Copy), reads=[("base",)], writes=[("Sb", 0)])
                sbs = 0

                def v_proj(ci, hh=hh, cx=cx):
                    rows, col0 = chunks[ci]
                    bank = 2 + (ci % 2)
                    proj_tm(ws, bi, hT, HR, col0, rows, bank, mid=lambda: pump(1))
                    if ci >= nch:
                        P.act(lambda e: e.activation(out=vs[:rows, cx, :], in_=PS[bank][:rows, :], func=AF.Copy),
                              reads=[("ps", bank)], writes=[("vs", cx)])
                    else:
                        P.act(lambda e: e.activation(out=vS[:rows, ci % 2, :], in_=PS[bank][:rows, :], func=AF.Copy),
                              reads=[("ps", bank)], writes=[("vS", ci % 2)])

                def scores(ci, hh=hh):
                    col0 = chunks[ci][1]
                    pslot = ci % 2

                    def sc_mm(e):
                        ins = None
                        for half in range(2):
                            ins = e.matmul(PS[4][:, 0:128], lhsT=kT[:, hh, half, col0:col0 + 128], rhs=qT[:, hh, half, col0:col0 + 128],
                                           start=(half == 0), stop=(half == 1))
                        return ins

                    P.pe(sc_mm, reads=[("kT", hh, ci), ("qT", hh)], writes=[("ps", 4)])
                    P.dve(lambda e: e.tensor_tensor(out=PT[:, pslot, :], in0=PS[4][:, 0:128], in1=cmask[:, :], op=ALU.mult),
                          reads=[("ps", 4)], writes=[("PT", pslot)])

                v_proj(0)
                if not kv:
                    scores(0)
                for ci in range(nch):
                    rows, col0 = chunks[ci]
                    vslot = ci % 2
                    if ci + 1 < nchk:
                        v_proj(ci + 1)
                    flush_late(1)
                    pump(1)
                    if kv:
                        pumpA(1, every=4)
                    if not kv:
                        def o_mm(e, hh=hh, col0=col0, vslot=vslot, sbs=sbs, ci=ci):
                            e.matmul(PS[5][:, :], lhsT=PT[:, ci % 2, :], rhs=vS[:, vslot, :], start=True, stop=False)
                            ins = None
                            for half in range(2):
                                ins = e.matmul(PS[5][:, :], lhsT=qT[:, hh, half, col0:col0 + 128], rhs=Sb[:, sbs, half, :],
                                               start=False, stop=(half == 1))
                            return ins

                        P.pe(o_mm, reads=[("PT", ci % 2), ("vS", vslot), ("qT", hh), ("Sb", sbs)], writes=[("ps", 5)])

                    def s_mm(e, ci=ci, hh=hh, vslot=vslot):
                        ins = None
                        for half in range(2):
                            ins = e.matmul(PS[half][:, :], lhsT=kTM[:, ci, hh, half * 128:(half + 1) * 128], rhs=vS[:, vslot, :],
                                           start=(ci == 0), stop=True, skip_group_check=True)
                        return ins

                    P.pe(s_mm, reads=[("kTM", ci, hh), ("vS", vslot)], writes=[("ps", 0), ("ps", 1)])
                    pump(1)
                    if not kv and ci + 1 < nch:
                        scores(ci + 1)
                    last_prompt = (ci == nch - 1)
                    if not kv and not last_prompt:
                        nsb = 1 - sbs
                        for half in range(2):
                            P.dve(lambda e, half=half, nsb=nsb: e.tensor_tensor(out=Sb[:, nsb, half, :], in0=PS[half][:, :], in1=base[:, half, :],
                                                                                 op=ALU.add),
                                  reads=[("ps", half), ("base",)], writes=[("Sb", nsb)])
                        sbs = nsb
                    if last_prompt:
                        for half in range(2):
                            P.dve(lambda e, half=half: e.tensor_tensor(out=base[:, half, :], in0=PS[half][:, :], in1=base[:, half, :], op=ALU.add),
                                  reads=[("ps", half), ("base",)], writes=[("base",)])
                        if pi < 2:
                            P.dma("sp", sbase_d[h], base[:].rearrange("p a b -> p (a b)"), "base_o", reads=[("base",)])
                        else:
                            fs = math.exp(LOGG[h] * (2047.0 - TOFF))
                            P.act(lambda e, fs=fs: e.mul(out=base[:], in_=base[:], mul=fs), reads=[("base",)],
                                  writes=[("base",)])
                            P.dma("sp", retp[h].rearrange("(a p) e -> p a e", p=128), base[:], "base_o", reads=[("base",)])
                    if not kv:
                        retention_epilogue(h, 128, col0, PS[5][:, :], ("ps", 5), sg[:, ci, :], ("sg", ci))
                if has_s:
                    sample_retention(h, hh, nch, TG, cx)
                bi += 1
        flush_late()
        while bg:
            pump(1)
            flush_late()
        pumpA(1000, every=1)
        P.barrier()
        if kv:
            continue

        def conv3(dstP, dstS, up, upkey, dkey, f, wt, bias, TG=TG, has_s=has_s):
            upP = up[:, 0:2 + TG]
            segs = [(dstP, lambda k, upP=upP, TG=TG: upP[:, k:k + TG])]
            if has_s:
                upS = up[:, 642:738].rearrange("p (b t) -> p b t", t=6)
                segs.append((dstS, lambda k, upS=upS: upS[:, :, k:k + 4]))
            for (dd, sl) in segs:
                if bias is None:
                    P.dve(lambda e, dd=dd, sl=sl: e.tensor_scalar(out=dd, in0=sl(0), scalar1=wt[:, f, 0:1], scalar2=None, op0=ALU.mult),
                           reads=[upkey], writes=[dkey])
                else:
                    P.dve(lambda e, dd=dd, sl=sl: e.tensor_scalar(out=dd, in0=sl(0), scalar1=wt[:, f, 0:1], scalar2=bias, op0=ALU.mult,
                                                                   op1=ALU.add),
                           reads=[upkey], writes=[dkey])
                for k in (1, 2):
                    P.dve(lambda e, dd=dd, sl=sl, k=k: e.scalar_tensor_tensor(out=dd, in0=sl(k), scalar=wt[:, f, k:k + 1], in1=dd,
                                                                               op0=ALU.mult, op1=ALU.add),
                           reads=[upkey, dkey], writes=[dkey])

        for fg in range(4):
            for fc in range(4):
                for ti, (t0, tn, _s) in enumerate(tiles):
                    bank = (fc * 2 + ti) % 4
                    proj_fm(ws, bi, fc, hT, HR, t0, tn, bank)
                    P.act(lambda e, fc=fc, t0=t0, tn=tn, bank=bank: e.activation(out=gcs[:, fc, t0:t0 + tn], in_=PS[bank][:, 0:tn], func=AF.Copy),
                          reads=[("ps", bank)], writes=[("gcs", fc)])
            bi += 1
            for fc in range(4):
                f = fg * 4 + fc
                up = UPc[:, fc, :]
                upkey = ("upc", fc)
                P.dve(lambda e, up=up, f=f: e.tensor_copy(out=up[:, 0:2], in_=halo_c[:, f, :]), reads=[("halo_c", f)], writes=[upkey])
                if has_s:
                    P.dve(lambda e, up=up, f=f: e.tensor_copy(out=up[:, 642:738].rearrange("p (b t) -> p b t", t=6)[:, :, 0:2],
                                                              in_=halo_cs[:, f, :, :]), reads=[("halo_cs", f)], writes=[upkey])
                for ti, (t0, tn, segs) in enumerate(tiles):
                    bank = (fc * 2 + ti) % 4
                    proj_fm(ws, bi, fc, hT, HR, t0, tn, bank)
                    for (so, sn, is_s, sc0) in segs:
                        if not is_s:
                            P.dve(lambda e, up=up, fc=fc, so=so, sn=sn, sc0=sc0, bank=bank: e.tensor_tensor(
                                out=up[:, 2 + sc0:2 + sc0 + sn], in0=PS[bank][:, so:so + sn], in1=gcs[:, fc, sc0:sc0 + sn], op=ALU.mult),
                                reads=[("ps", bank), ("gcs", fc)], writes=[upkey])
                        else:
                            P.dve(lambda e, up=up, fc=fc, so=so, sn=sn, sc0=sc0, bank=bank: e.tensor_tensor(
                                out=up[:, 642:738].rearrange("p (b t) -> p b t", t=6)[:, :, 2:6],
                                in0=PS[bank][:, so:so + sn].rearrange("p (b t) -> p b t", t=4),
                                in1=gcs[:, fc, sc0:sc0 + sn].rearrange("p (b t) -> p b t", t=4), op=ALU.mult),
                                reads=[("ps", bank), ("gcs", fc)], writes=[upkey])
                P.dve(lambda e, up=up, f=f, TG=TG: e.tensor_copy(out=halo_c[:, f, :], in_=up[:, TG:TG + 2]), reads=[upkey],
                      writes=[("halo_c", f)])
                if has_s:
                    P.dve(lambda e, up=up, f=f: e.tensor_copy(out=halo_cs[:, f, :, :],
                                                              in_=up[:, 642:738].rearrange("p (b t) -> p b t", t=6)[:, :, 4:6]),
                          reads=[upkey], writes=[("halo_cs", f)])
                conv3(cv[:, fc, 0:TG], cv[:, fc, TG:TG + 64].rearrange("p (b t) -> p b t", t=4) if has_s else None, up, upkey, ("cv", fc), f, cw, None)
            bi += 1
            for fc in range(4):
                f = fg * 4 + fc
                for ti, (t0, tn, _s) in enumerate(tiles):
                    bank = (fc * 2 + ti) % 4
                    proj_fm(ws, bi, fc, hT, HR, t0, tn, bank)
                    P.dve(lambda e, f=f, fc=fc, t0=t0, tn=tn, bank=bank: e.tensor_tensor(
                        out=gbcT[:, f, t0:t0 + tn], in0=PS[bank][:, 0:tn], in1=cv[:, fc, t0:t0 + tn], op=ALU.mult),
                        reads=[("ps", bank), ("cv", fc)], writes=[("gbcT", f)])
            bi += 1
        P.barrier()

        for fg in range(4):
            for (kc0, first) in ((0, True), (16, False)):
                for fc in range(4):
                    for ti, (t0, tn, _s) in enumerate(tiles):
                        bank = (fc * 2 + ti) % 4
                        proj_fm(ws, bi, fc, goT, [], t0, tn, bank, kc0=kc0)
                        if first:
                            P.act(lambda e, fc=fc, t0=t0, tn=tn, bank=bank: e.activation(out=ya[:, fc, t0:t0 + tn], in_=PS[bank][:, 0:tn],
                                                                                        func=AF.Copy),
                                  reads=[("ps", bank)], writes=[("ya", fc)])
                        else:
                            P.dve(lambda e, fc=fc, t0=t0, tn=tn, bank=bank: e.tensor_tensor(out=ya[:, fc, t0:t0 + tn], in0=PS[bank][:, 0:tn],
                                                                                            in1=ya[:, fc, t0:t0 + tn], op=ALU.add),
                                  reads=[("ps", bank), ("ya", fc)], writes=[("ya", fc)])
                bi += 1
            for fc in range(4):
                for ti, (t0, tn, _s) in enumerate(tiles):
                    bank = (fc * 2 + ti) % 4
                    proj_fm(ws, bi, fc, hT, [], t0, tn, bank)
                    P.act(lambda e, fc=fc, t0=t0, tn=tn, bank=bank: e.activation(out=sgm[:, fc, t0:t0 + tn], in_=PS[bank][:, 0:tn],
                                                                                func=AF.Sigmoid),
                          reads=[("ps", bank)], writes=[("sgm", fc)])
                    P.pool(lambda e, fc=fc, t0=t0, tn=tn: e.tensor_tensor(out=ya[:, fc, t0:t0 + tn], in0=ya[:, fc, t0:t0 + tn],
                                                                          in1=sgm[:, fc, t0:t0 + tn], op=ALU.mult),
                           reads=[("ya", fc), ("sgm", fc)], writes=[("ya", fc)])
            bi += 1
            for fc in range(4):
                for ti, (t0, tn, _s) in enumerate(tiles):
                    bank = (fc * 2 + ti) % 4
                    proj_fm(ws, bi, fc, gbcT, [], t0, tn, bank)
                    P.act(lambda e, fc=fc, t0=t0, tn=tn, bank=bank: e.activation(out=yb[:, fc, t0:t0 + tn], in_=PS[bank][:, 0:tn], func=AF.Copy),
                          reads=[("ps", bank)], writes=[("yb", fc)])
            bi += 1
            for fc in range(4):
                f = fg * 4 + fc
                for ti, (t0, tn, _s) in enumerate(tiles):
                    bank = (fc * 2 + ti) % 4
                    proj_fm(ws, bi, fc, hT, [], t0, tn, bank)
                    P.act(lambda e, fc=fc, t0=t0, tn=tn, bank=bank: e.activation(out=sgm[:, fc, t0:t0 + tn], in_=PS[bank][:, 0:tn],
                                                                                func=AF.Sigmoid),
                          reads=[("ps", bank)], writes=[("sgm", fc)])
                    P.pool(lambda e, fc=fc, t0=t0, tn=tn: e.tensor_tensor(out=yb[:, fc, t0:t0 + tn], in0=yb[:, fc, t0:t0 + tn],
                                                                          in1=sgm[:, fc, t0:t0 + tn], op=ALU.mult),
                           reads=[("yb", fc), ("sgm", fc)], writes=[("yb", fc)])
                    P.pool(lambda e, f=f, fc=fc, t0=t0, tn=tn: e.tensor_tensor(out=mT[:, f, t0:t0 + tn], in0=ya[:, fc, t0:t0 + tn],
                                                                               in1=yb[:, fc, t0:t0 + tn], op=ALU.add),
                           reads=[("ya", fc), ("yb", fc)], writes=[("mT", f)])
            bi += 1
        P.barrier()

        P.dma("sp", GP[:, :], gpost_d[0], "gp", writes=[("GP",)])
        for nb in range(4):
            for ci, (rows, col0) in enumerate(chunks):
                bank = ci % 4
                proj_tm(ws, bi, mT, [], col0, rows, bank)
                P.act(lambda e, rows=rows, ci=ci, nb=nb, bank=bank: e.activation(out=XF[:rows, ci, nb * 512:(nb + 1) * 512], in_=PS[bank][:rows, :],
                                                                                func=AF.Copy),
                      reads=[("ps", bank)], writes=[("XF", ci)])
            bi += 1
        for ci, (rows, col0) in enumerate(chunks):
            col = rstd_col(XF[:rows, ci, :], rows, D, [("XF", ci)], dumpA[:rows, :], None)
            P.dve(lambda e, rows=rows, ci=ci, col=col: e.scalar_tensor_tensor(out=XF[:rows, ci, :], in0=XF[:rows, ci, :],
                                                                             scalar=st[:rows, col:col + 1], in1=GP[:rows, :],
                                                                             op0=ALU.mult, op1=ALU.mult),
                  reads=[("XF", ci), ("st", col), ("GP",)], writes=[("XF", ci)])
            P.dma("pool", XF[:rows, ci, :], xrows(ci), ("xacc", ci), reads=[("XF", ci)], writes=[("XF", ci)], accum=True)
            P.dma("sp", xmid_d[ci, 0:rows, :], XF[:rows, ci, :], ("xmo", ci), reads=[("XF", ci)])
        norm_transpose(HT, 1, chunks, lambda ci: (XF[:, ci, :], ("XF", ci)))
        P.barrier()

        for fg in range(11):
            for fc in range(4):
                f = fg * 4 + fc
                up = UPf[:, fc % 2, :]
                upkey = ("upf", fc % 2)
                P.dve(lambda e, up=up, f=f: e.tensor_copy(out=up[:, 0:2], in_=halo_f[:, f, :]), reads=[("halo_f", f)], writes=[upkey])
                if has_s:
                    P.dve(lambda e, up=up, f=f: e.tensor_copy(out=up[:, 642:738].rearrange("p (b t) -> p b t", t=6)[:, :, 0:2],
                                                              in_=halo_fs[:, f, :, :]), reads=[("halo_fs", f)], writes=[upkey])
                for ti, (t0, tn, segs) in enumerate(tiles):
                    bank = (fc * 2 + ti) % 4
                    proj_fm(ws, bi, fc, HT, [], t0, tn, bank)
                    for (so, sn, is_s, sc0) in segs:
                        if not is_s:
                            P.act(lambda e, up=up, so=so, sn=sn, sc0=sc0, bank=bank: e.activation(out=up[:, 2 + sc0:2 + sc0 + sn],
                                                                                                 in_=PS[bank][:, so:so + sn], func=AF.Copy),
                                  reads=[("ps", bank)], writes=[upkey])
                        else:
                            P.act(lambda e, up=up, so=so, sn=sn, bank=bank: e.activation(
                                out=up[:, 642:738].rearrange("p (b t) -> p b t", t=6)[:, :, 2:6],
                                in_=PS[bank][:, so:so + sn].rearrange("p (b t) -> p b t", t=4), func=AF.Copy),
                                reads=[("ps", bank)], writes=[upkey])
                P.dve(lambda e, up=up, f=f, TG=TG: e.tensor_copy(out=halo_f[:, f, :], in_=up[:, TG:TG + 2]), reads=[upkey],
                      writes=[("halo_f", f)])
                if has_s:
                    P.dve(lambda e, up=up, f=f: e.tensor_copy(out=halo_fs[:, f, :, :],
                                                              in_=up[:, 642:738].rearrange("p (b t) -> p b t", t=6)[:, :, 4:6]),
                          reads=[upkey], writes=[("halo_fs", f)])
                conv3(gaB[:, fc, 0:TG], gaB[:, fc, TG:TG + 64].rearrange("p (b t) -> p b t", t=4) if has_s else None, up, upkey,
                      ("ga", fc), f, fcw, fcb[:, f:f + 1])
                P.act(lambda e, fc=fc, TT=TT: e.activation(out=gaB[:, fc, 0:TT], in_=gaB[:, fc, 0:TT], func=AF.Gelu_apprx_tanh),
                      reads=[("ga", fc)], writes=[("ga", fc)])
            bi += 1
            for fc in range(4):
                f = fg * 4 + fc
                for ti, (t0, tn, _s) in enumerate(tiles):
                    bank = (fc * 2 + ti) % 4
                    proj_fm(ws, bi, fc, HT, [], t0, tn, bank)
                    P.dve(lambda e, f=f, fc=fc, t0=t0, tn=tn, bank=bank: e.tensor_tensor(
                        out=aT[:, f, t0:t0 + tn], in0=PS[bank][:, 0:tn], in1=gaB[:, fc, t0:t0 + tn], op=ALU.mult),
                        reads=[("ps", bank), ("ga", fc)], writes=[("aT", f)])
            bi += 1
        P.barrier()

        if pi + 1 < len(PASSES):
            enqueue_phaseA(pi + 1)
        e2n = 0
        for nb in range(4):
            for sub in range(3):
                nkc = 16 if sub < 2 else 12
                for ci, (rows, col0) in enumerate(chunks):
                    proj_tm(ws, bi, aT, [], col0, rows, ci, nkc=nkc, kc0=sub * 16, start=(sub == 0), stop=(sub == 2))
                    pumpA(1, every=5)
                    if sub == 2:
                        P.act(lambda e, rows=rows, ci=ci, nb=nb: e.activation(out=XF[:rows, ci, nb * 512:(nb + 1) * 512], in_=PS[ci][:rows, :],
                                                                             func=AF.Copy),
                              reads=[("ps", ci)], writes=[("XF", ci)])
                bi += 1
        pumpA(1000, every=1)
        P.barrier()
        if pi == len(PASSES) - 1:
            out_rows(lambda f: halo_f[:, f, :], NFF, 2, ffnp)
            out_rows(lambda f: halo_c[:, f, :], 16, 2, convp)
            out_rows(lambda f: halo_fs[:, f, :, :].rearrange("p b t -> p (b t)"), NFF, 32, ffns)
            out_rows(lambda f: halo_cs[:, f, :, :].rearrange("p b t -> p (b t)"), 16, 32, convs)
        P.dma("sp", GP[:, :], gpost_d[1], "gp", writes=[("GP",)])
        def ld_xmid(ci):
            P.dma("sp", XT[ci % 4][:chunks[ci][0], :], xmid_d[ci, 0:chunks[ci][0], :], ("xt", ci % 4), writes=[("XT", ci % 4)])

        for ci in range(min(4, nchk)):
            ld_xmid(ci)
        for ci, (rows, col0) in enumerate(chunks):
            slot = ci % 4
            col = rstd_col(XF[:rows, ci, :], rows, D, [("XF", ci)], junks[1][:rows, :], None)
            P.dve(lambda e, rows=rows, ci=ci, col=col: e.scalar_tensor_tensor(out=XF[:rows, ci, :], in0=XF[:rows, ci, :],
                                                                             scalar=st[:rows, col:col + 1], in1=GP[:rows, :],
                                                                             op0=ALU.mult, op1=ALU.mult),
                  reads=[("XF", ci), ("st", col), ("GP",)], writes=[("XF", ci)])
            P.dve(lambda e, rows=rows, ci=ci, slot=slot: e.tensor_tensor(out=XF[:rows, ci, :], in0=XF[:rows, ci, :], in1=XT[slot][:rows, :],
                                                                         op=ALU.add),
                  reads=[("XF", ci), ("XT", slot)], writes=[("XF", ci)])
            if ci + 4 < nchk:
                ld_xmid(ci + 4)
            if ci < nch:
                dst = ym[ps["out0"] + ci * 128: ps["out0"] + (ci + 1) * 128, :]
            else:
                dst = ys[:, :]
            P.dma("sp", dst, XF[:rows, ci, :], ("yo", ci), reads=[("XF", ci)])
        P.barrier()

    P.emit()
    return nc


def _tables(core):
    half = core % 2
    f32 = np.float32
    inv = (f32(10000.0) ** (-(np.arange(128, dtype=f32) / f32(128)))).astype(f32)

    def cs(pos):
        ang = (np.asarray(pos, dtype=f32)[:, None] * inv[None, :]).astype(f32)
        return np.cos(ang).astype(np.float64), np.sin(ang).astype(np.float64)

    logg = np.array(LOGG, dtype=np.float64)
    m = np.arange(NMAIN)
    pos_main = np.maximum(half * 1024 - 128 + m, 0)
    t_main = 896 + m
    pos_pre = np.arange(NPRE)
    t_pre = np.arange(NPRE)
    tau = np.arange(NSAMP) % 4
    pos_s = 16384 + tau
    cm, sm = cs(pos_main)
    cp, sp_ = cs(pos_pre)
    c_s, s_s = cs(pos_s)

    cq = np.zeros((2, NH, 128, 640), np.float64)
    sq = np.zeros((2, NH, 128, 640), np.float64)
    ck = np.zeros((3, 128, 7, 128), np.float64)
    sk = np.zeros((3, 128, 7, 128), np.float64)
    ksc = np.zeros((128, 3, 7, NH), np.float64)
    ck[0] = cp.reshape(7, 128, 128).transpose(1, 0, 2)
    sk[0] = sp_.reshape(7, 128, 128).transpose(1, 0, 2)
    for h in range(NH):
        ksc[:, 0, :, h] = (np.exp(-logg[h] * (t_pre - TOFF)) / 16.0).reshape(7, 128).T
    for p, (row0, nch) in ((1, (0, 5)), (2, (640, 4))):
        rows = slice(row0, row0 + nch * 128)
        ck[p, :, :nch] = cm[rows].reshape(nch, 128, 128).transpose(1, 0, 2)
        sk[p, :, :nch] = sm[rows].reshape(nch, 128, 128).transpose(1, 0, 2)
        for h in range(NH):
            dq = np.exp(logg[h] * (t_main[rows] - TOFF))
            cq[p - 1, h, :, :nch * 128] = (cm[rows] * dq[:, None]).T
            sq[p - 1, h, :, :nch * 128] = (sm[rows] * dq[:, None]).T
            ksc[:, p, :nch, h] = (np.exp(-logg[h] * (t_main[rows] - TOFF)) / 16.0).reshape(nch, 128).T
    ck[2, :64, 4] = c_s
    sk[2, :64, 4] = s_s
    for h in range(NH):
        dq = np.exp(logg[h] * (tau + 1.0))
        cq[1, h, :, 512:576] = (c_s * dq[:, None]).T
        sq[1, h, :, 512:576] = (s_s * dq[:, None]).T
        ksc[:64, 2, 4, h] = np.exp(-logg[h] * (tau + 1.0)) / 16.0
    i = np.arange(128)
    cmask = (i[None, :] >= i[:, None]).astype(f32)
    j64 = np.arange(64)
    bmask = ((j64[None, :] // 4 == j64[:, None] // 4) & (j64[None, :] >= j64[:, None])).astype(f32)
    bm16 = np.broadcast_to((j64[None, :] // 4 == np.arange(NB)[:, None]).astype(f32)[None], (128, NB, 64)).copy()
    bmT = (j64[:, None] // 4 == np.arange(NB)[None, :]).astype(f32)
    return dict(cq=cq.astype(f32), sq=sq.astype(f32), ck=ck.astype(f32), sk=sk.astype(f32), ksc=ksc.astype(f32),
                cmask=cmask, bmask=bmask, bm16=bm16, bmT=bmT, idf=np.eye(128, dtype=f32))


def _in_map(core, x_prompt, x_sample, state_ret, state_conv, state_ffn, g_pre_mix, w_in, conv_w, p_ret, p_conv, w_o, g_post_mix,
            g_pre_ffn, w_up, w_gate, ffn_conv_w, ffn_conv_b, w_down, g_post_ffn):
    f32 = np.float32
    b, half = core // 2, core % 2
    xm = np.zeros((NMAIN, D), f32)
    xp = np.zeros((NPRE, D), f32)
    if half == 0:
        xm[128:] = x_prompt[b, 0:1024]
    else:
        xm[:] = x_prompt[b, 896:2048]
        xp[:] = x_prompt[b, 0:896]
    sl = slice(core * NB, (core + 1) * NB)
    mp = dict(
        xm=xm, xp=xp, xs=np.ascontiguousarray(x_sample[sl].reshape(NSAMP, D)),
        sret=np.ascontiguousarray(state_ret[0, sl]),
        sconv=np.ascontiguousarray(state_conv[0, sl].reshape(2 * NB, D)),
        sffn=np.ascontiguousarray(state_ffn[0, sl].reshape(2 * NB, DFF)),
        w_in=w_in[0], p_ret=p_ret[0], p_conv=p_conv[0], w_o=w_o[0], w_up=w_up[0], w_gate=w_gate[0], w_down=w_down[0],
        gcol=np.ascontiguousarray(np.stack([g_pre_mix[0].reshape(16, 128).T, g_pre_ffn[0].reshape(16, 128).T], axis=1)),
        gpost=np.ascontiguousarray(np.stack([np.broadcast_to(g_post_mix[0], (128, D)), np.broadcast_to(g_post_ffn[0], (128, D))])),
        cw=np.ascontiguousarray(conv_w[0].T.reshape(16, 128, 3).transpose(1, 0, 2)),
        fcw=np.ascontiguousarray(ffn_conv_w[0].T.reshape(NFF, 128, 3).transpose(1, 0, 2)),
        fcb=np.ascontiguousarray(ffn_conv_b[0].reshape(NFF, 128).T),
    )
    mp.update(_tables(core))
    return {k: np.ascontiguousarray(v, dtype=f32) for k, v in mp.items()}


_NC_CACHE = {}


def _run(inputs, cores):
    if "nc" not in _NC_CACHE:
        _NC_CACHE["nc"] = build_program()
    nc = _NC_CACHE["nc"]
    inputs = {k: np.asarray(v) for k, v in inputs.items()}
    in_maps = [_in_map(c, **inputs) for c in cores]
    res = run_bass_kernel_spmd(nc, in_maps, core_ids=list(range(len(cores))))
    return res.results


def kernel(**inputs):
    f32 = np.float32
    results = _run(inputs, list(range(NCORES)))
    B, S = 4, 2048
    y_prompt = np.zeros((B, S, D), f32)
    y_sample = np.zeros((128, 4, D), f32)
    ret_prompt = np.zeros((1, B, NH, DK, DV), f32)
    conv_prompt = np.zeros((1, B, 2, D), f32)
    ffn_prompt = np.zeros((1, B, 2, DFF), f32)
    ret_sample = np.zeros((1, 128, NH, DK, DV), f32)
    conv_sample = np.zeros((1, 128, 2, D), f32)
    ffn_sample = np.zeros((1, 128, 2, DFF), f32)
    for c, r in enumerate(results):
        b, half = c // 2, c % 2
        y_prompt[b, half * 1024:(half + 1) * 1024] = r["ym"][128:]
        sl = slice(c * NB, (c + 1) * NB)
        y_sample[sl] = r["ys"].reshape(NB, 4, D)
        ret_sample[0, sl] = r["rets"]
        conv_sample[0, sl] = r["convs"].reshape(NB, 2, D)
        ffn_sample[0, sl] = r["ffns"].reshape(NB, 2, DFF)
        if half == 1:
            ret_prompt[0, b] = r["retp"]
            conv_prompt[0, b] = r["convp"]
            ffn_prompt[0, b] = r["ffnp"]
    return (y_prompt, y_sample, ret_prompt, conv_prompt, ffn_prompt, ret_sample, conv_sample, ffn_sample)
```

```python
import math
import numpy as np
import concourse.bass as bass
import concourse.mybir as mybir
from concourse.bass_utils import run_bass_kernel_spmd

F32 = mybir.dt.float32
BF16 = mybir.dt.bfloat16
ALU = mybir.AluOpType
AF = mybir.ActivationFunctionType

D = 2048
NH = 8
DK = 256
DV = 512
DFF = 5632
NFF = DFF // 128
EPS = 1e-6
NCORES = 8
NPRE = 896
NMAIN = 1152
NSAMP = 64
NB = 16
TOFF = 1024.0
LOGG = [math.log1p(-2.0 ** (-5 - h)) for h in range(NH)]
C_Q, C_K, C_V, C_G, C_GB, C_GC, C_HC, C_GA, C_GBT = 0, 2048, 4096, 8192, 12288, 14336, 16384, 18432, 20480

PASSES = [
    dict(src="xp", row0=0, nch=7, kv=True, sample=False, t0=0, out0=None),
    dict(src="xm", row0=0, nch=5, kv=False, sample=False, t0=896, out0=0),
    dict(src="xm", row0=640, nch=4, kv=False, sample=True, t0=896 + 640, out0=640),
]


class _Op:
    __slots__ = ("eng", "fn", "deps", "signal", "ev", "dkey")


class Prog:
    def __init__(self, nc):
        self.nc = nc
        self.ops = []
        self.last_w = {}
        self.readers = {}
        self.dkeys = []

    def add(self, eng, fn, reads=(), writes=(), dkey=None):
        op = _Op()
        op.eng, op.fn, op.signal, op.ev, op.dkey = eng, fn, False, None, dkey
        deps = set()
        for r in reads:
            w = self.last_w.get(r)
            if w is not None:
                deps.add(w)
        for wr in writes:
            w = self.last_w.get(wr)
            if w is not None:
                deps.add(w)
            deps.update(self.readers.get(wr, ()))
        idx = len(self.ops)
        for r in reads:
            self.readers.setdefault(r, []).append(idx)
        for wr in writes:
            self.last_w[wr] = idx
            self.readers[wr] = []
        op.deps = deps
        self.ops.append(op)
        if dkey is not None and dkey not in self.dkeys:
            self.dkeys.append(dkey)
        return idx

    def pe(self, fn, reads=(), writes=()):
        return self.add("pe", fn, reads, writes)

    def act(self, fn, reads=(), writes=()):
        return self.add("act", fn, reads, writes)

    def dve(self, fn, reads=(), writes=()):
        return self.add("dve", fn, reads, writes)

    def pool(self, fn, reads=(), writes=()):
        return self.add("pool", fn, reads, writes)

    def dma(self, q, out, in_, key, reads=(), writes=(), accum=False):
        if accum:
            return self.add(q, lambda e: e.dma_start(out=out, in_=in_, accum_op=ALU.add), reads, writes, dkey=key)
        return self.add(q, lambda e: e.dma_start(out=out, in_=in_), reads, writes, dkey=key)

    def barrier(self):
        last = {}
        for i, op in enumerate(self.ops):
            if op.fn is not None:
                last[("e", op.eng) if op.dkey is None else ("d", op.dkey)] = i
        deps = set(last.values())
        for eng in ("pe", "act", "dve", "pool", "sp"):
            op = _Op()
            op.eng, op.fn, op.signal, op.ev, op.dkey = eng, None, False, None, None
            op.deps = set(deps)
            self.ops.append(op)
        self.last_w = {}
        self.readers = {}

    def emit(self):
        nc = self.nc
        engs = {"pe": nc.tensor, "act": nc.scalar, "dve": nc.vector, "pool": nc.gpsimd, "sp": nc.sync}
        for op in self.ops:
            for d in op.deps:
                if self.ops[d].dkey is None:
                    self.ops[d].signal = True
        esem = {e: nc.alloc_semaphore("es_" + e) for e in engs}
        dsem = {k: nc.alloc_semaphore("ds_%d" % i) for i, k in enumerate(self.dkeys)}
        cnt = {e: 0 for e in engs}
        dcnt = {k: 0 for k in self.dkeys}
        waited = {e: {} for e in engs}
        for op in self.ops:
            E = engs[op.eng]
            wl = {}
            for d in op.deps:
                p = self.ops[d]
                if p.fn is None:
                    continue
                if op.eng == "pe" and p.eng == "pe" and p.dkey is None:
                    continue
                name, sem, val = p.ev
                if wl.get(name, (None, 0))[1] < val:
                    wl[name] = (sem, val)
            for name, (sem, val) in wl.items():
                if waited[op.eng].get(name, 0) >= val:
                    continue
                E.wait_ge(sem, val)
                waited[op.eng][name] = val
            if op.fn is None:
                continue
            ins = op.fn(E)
            if op.dkey is not None:
                dcnt[op.dkey] += 16
                ins.then_inc(dsem[op.dkey], 16)
                op.ev = (("d", op.dkey), dsem[op.dkey], dcnt[op.dkey])
            elif op.signal:
                cnt[op.eng] += 1
                ins.then_inc(esem[op.eng], 1)
                op.ev = (("e", op.eng), esem[op.eng], cnt[op.eng])
        for k in self.dkeys:
            if dcnt[k]:
                nc.sync.wait_ge(dsem[k], dcnt[k])
        for e in engs:
            if cnt[e] and e != "sp":
                nc.sync.wait_ge(esem[e], cnt[e])


def build_program():
    nc = bass.Bass("TRN2", target_bir_lowering=False)
    P = Prog(nc)

    def din(name, shape):
        return nc.dram_tensor(name, list(shape), F32, kind="ExternalInput").ap()

    def dout(name, shape):
        return nc.dram_tensor(name, list(shape), F32, kind="ExternalOutput").ap()

    xm = din("xm", [NMAIN, D])
    xp = din("xp", [NPRE, D])
    xs = din("xs", [NSAMP, D])
    sret = din("sret", [NB, NH, DK, DV])
    sconv = din("sconv", [2 * NB, D])
    sffn = din("sffn", [2 * NB, DFF])
    w_in = din("w_in", [D, 22528])
    p_ret = din("p_ret", [4096, D])
    p_conv = din("p_conv", [D, D])
    w_o = din("w_o", [D, D])
    w_up = din("w_up", [D, DFF])
    w_gate = din("w_gate", [D, DFF])
    w_down = din("w_down", [DFF, D])
    gcol_d = din("gcol", [128, 2, 16])
    gpost_d = din("gpost", [2, 128, D])
    cw_d = din("cw", [128, 16, 3])
    fcw_d = din("fcw", [128, NFF, 3])
    fcb_d = din("fcb", [128, NFF])
    cq_d = din("cq", [2, NH, 128, 640])
    sq_d = din("sq", [2, NH, 128, 640])
    ck_d = din("ck", [3, 128, 7, 128])
    sk_d = din("sk", [3, 128, 7, 128])
    ksc_d = din("ksc", [128, 3, 7, NH])
    cmask_d = din("cmask", [128, 128])
    bmask_d = din("bmask", [64, 64])
    bm16_d = din("bm16", [128, NB, 64])
    bmT_d = din("bmT", [64, NB])
    idf_d = din("idf", [128, 128])

    ym = dout("ym", [NMAIN, D])
    ys = dout("ys", [NSAMP, D])
    retp = dout("retp", [NH, DK, DV])
    convp = dout("convp", [2, D])
    ffnp = dout("ffnp", [2, DFF])
    rets = dout("rets", [NB, NH, DK, DV])
    convs = dout("convs", [2 * NB, D])
    ffns = dout("ffns", [2 * NB, DFF])

    xmid_d = nc.dram_tensor("xmid_scr", [6, 128, D], F32, kind="Internal").ap()
    sbase_d = nc.dram_tensor("sbase_scr", [NH, 128, 2 * DV], F32, kind="Internal").ap()

    def sb(name, shape, dt):
        return nc.alloc_sbuf_tensor("s_" + name, list(shape), dt)

    HT = sb("HT", [128, 16, 640], BF16)
    R1 = sb("R1", [128, 10240], F32)
    R2 = sb("R2", [128, 5120], F32)
    R4 = sb("R4", [128, 16384], F32)
    WB = [sb("WB%d" % i, [128, 16, 512], BF16) for i in range(3)]
    ident = sb("ident", [128, 128], BF16)
    identf = sb("identf", [128, 128], F32)
    cmask = sb("cmask", [128, 128], F32)
    bmask = sb("bmask", [64, 64], F32)
    bm16 = sb("bm16", [128, NB, 64], BF16)
    bmT = sb("bmT", [64, NB], F32)
    gcol = sb("gcol", [128, 2, 16], F32)
    cw = sb("cw", [128, 16, 3], F32)
    fcw = sb("fcw", [128, NFF, 3], F32)
    fcb = sb("fcb", [128, NFF], F32)
    ksc = sb("ksc", [128, 3, 7, NH], F32)
    halo_c = sb("halo_c", [128, 16, 2], F32)
    halo_f = sb("halo_f", [128, NFF, 2], F32)
    halo_cs = sb("halo_cs", [128, 16, NB, 2], F32)
    halo_fs = sb("halo_fs", [128, NFF, NB, 2], F32)
    st = sb("st", [128, 64], F32)
    otm = sb("otm", [32, 512], F32)

    def _view(reg, off_b, shape, bf):
        n = int(np.prod(shape))
        if bf:
            v = reg[:, off_b // 4:(off_b + 2 * n + 3) // 4].bitcast(BF16)
        else:
            v = reg[:, off_b // 4:off_b // 4 + n]
        if len(shape) == 1:
            return v
        names = " ".join("d%d" % i for i in range(len(shape)))
        kw = {"d%d" % i: s for i, s in enumerate(shape)}
        return v.rearrange("p (%s) -> p %s" % (names, names), **kw)

    def r_bf(reg, off_b, shape):
        return _view(reg, off_b, shape, True)

    def r_f32(reg, off_b, shape):
        return _view(reg, off_b, shape, False)

    goT = r_bf(R1, 0, [32, 640])
    XF = r_f32(R1, 0, [5, 2048])
    hT0 = r_bf(R1, 0, [16, 896])
    gbcT = r_bf(R2, 0, [16, 640])
    Sf = r_f32(R2, 0, [4, 2, 512])
    Sbf = r_bf(R2, 16384, [2, 2, 512])
    GP = r_f32(R2, 0, [2048])
    mT = r_bf(R4, 0, [16, 640])
    aT = r_bf(R4, 0, [NFF, 640])
    junk = r_bf(R4, 40960, [2048])
    XT = [r_f32(R4, 49152, [2048]), r_f32(R4, 57344, [2048]), r_f32(R4, 24576, [2048]), r_f32(R4, 32768, [2048])]
    dumpA = r_bf(R4, 20480, [2048])
    XTh = [r_f32(R2, 0, [2048]), r_f32(R2, 8192, [2048])]
    junkh = r_bf(R2, 16384, [2048])
    off = {"o": 0}

    def tb(shape, bf):
        n = int(np.prod(shape)) * (2 if bf else 4)
        n = (n + 3) // 4 * 4
        v = _view(R4, off["o"], shape, bf)
        off["o"] += n
        return v

    qT = tb([2, 2, 640], True)
    kTM = tb([7, 2, 256], True)
    kT = tb([2, 2, 640], True)
    vS = tb([2, 512], True)
    sg = tb([5, 512], True)
    cqt = tb([640], False)
    sqt = tb([640], False)
    ckt = tb([7, 128], False)
    skt = tb([7, 128], False)
    Sb = tb([2, 2, 512], True)
    base = tb([2, 512], False)
    rt1 = tb([320], False)
    rt2 = tb([320], False)
    PT = tb([2, 128], True)
    gotm = tb([2, 512], True)
    qTs = tb([2, 2, 64], True)
    khat = tb([2, 256], True)
    vs = tb([2, 512], True)
    osum = tb([2, 512], False)
    sgs = tb([2, 512], True)
    khmb = tb([2, 256], True)
    rt3 = tb([320], False)
    rt4 = tb([320], False)
    junkB = tb([512], True)
    assert off["o"] <= 65536, off["o"]
    gcs = r_f32(R4, 20480, [4, 640])
    UPc = r_f32(R4, 20480 + 10240, [4, 740])
    cv = r_f32(R4, 20480 + 10240 + 11840, [4, 640])
    ya = r_f32(R4, 20480, [4, 640])
    yb = r_f32(R4, 20480 + 10240, [4, 640])
    sgm = r_f32(R4, 20480 + 20480, [4, 640])
    UPf = r_f32(R2, 0, [2, 740])
    gaB = r_f32(R2, 5920, [4, 640])

    PS = [nc.alloc_psum_tensor("ps%d" % i, [128, 512], F32) for i in range(8)]

    def psb(i):
        return PS[i][:, :].bitcast(BF16).rearrange("p (a b) -> p a b", b=128)

    def cload(dst, src, key, q="sp"):
        P.dma(q, dst, src, key, writes=[("c", key)])

    cload(ident[:], idf_d[:, :], "c_id", q="pool")
    cload(identf[:], idf_d[:, :], "c_idf")
    cload(cmask[:], cmask_d[:, :], "c_cm")
    cload(bmask[:], bmask_d[:, :], "c_bm")
    cload(bm16[:], bm16_d[:, :, :], "c_bm16", q="pool")
    cload(bmT[:], bmT_d[:, :], "c_bmT")
    cload(gcol[:], gcol_d[:, :, :], "c_gcol")
    cload(cw[:], cw_d[:, :, :], "c_cw")
    cload(fcw[:], fcw_d[:, :, :], "c_fcw")
    cload(fcb[:], fcb_d[:, :], "c_fcb")
    cload(ksc[:], ksc_d[:, :, :, :], "c_ksc")
    P.dve(lambda e: e.memset(halo_c[:], 0.0), writes=[("halo_c",)])
    P.dve(lambda e: e.memset(halo_f[:], 0.0), writes=[("halo_f",)])
    tmpc = r_f32(R1, 0, [2048])
    tmpf = r_f32(R1, 8192, [DFF])
    P.dma("sp", tmpc[:32, :], sconv[:, :], "c_tc", writes=[("tmpc",)])
    P.dma("sp", tmpf[:32, :], sffn[:, :], "c_tf", writes=[("tmpf",)])
    P.barrier()

    def halo_init(tmp, dst, nf):
        for g0 in range(0, nf, 16):
            n = min(16, nf - g0)
            bank = (g0 // 16) % 2

            def f(e, g0=g0, n=n, bank=bank):
                ins = None
                for j in range(n):
                    ins = e.transpose(PS[bank][:, j * 32:(j + 1) * 32], tmp[:32, (g0 + j) * 128:(g0 + j + 1) * 128], identf[:32, :32])
                return ins

            P.pe(f, writes=[("ps", bank)])
            P.act(lambda e, g0=g0, n=n, bank=bank: e.activation(
                out=dst[:, g0:g0 + n, :, :].rearrange("p f b t -> p f (b t)"),
                in_=PS[bank][:, 0:n * 32].rearrange("p (f x) -> p f x", x=32), func=AF.Copy),
                reads=[("ps", bank)], writes=[("halo_s", g0)])

    halo_init(tmpc, halo_cs, 16)
    halo_init(tmpf, halo_fs, NFF)
    P.barrier()

    wstate = {"n": 0}

    def wload(w2d, r0, c0, nkc=16, ncol=512):
        slot = wstate["n"] % 3
        wstate["n"] += 1
        src = w2d[r0:r0 + nkc * 128, c0:c0 + ncol].rearrange("(kc p) n -> p kc n", p=128)
        P.dma("pool", WB[slot][:, 0:nkc, 0:ncol], src, ("wb", slot), writes=[("WB", slot)])
        return slot

    class WStream:
        def __init__(self, blocks):
            self.blocks = blocks
            self.issued = []

        def get(self, i):
            while len(self.issued) < min(len(self.blocks), i + 3):
                b = self.blocks[len(self.issued)]
                self.issued.append(wload(*b))
            return self.issued[i]

    st_n = {"n": 0}

    def stcol():
        i = st_n["n"] % 64
        st_n["n"] += 1
        return i

    def rstd_col(src_ap, rows, n, reads, junk_ap, junk_key):
        col = stcol()
        P.act(lambda e: e.activation(out=junk_ap, in_=src_ap, func=AF.Square, accum_out=st[:rows, col:col + 1]),
              reads=list(reads), writes=[("st", col)] + ([junk_key] if junk_key is not None else []))
        P.act(lambda e: e.activation(out=st[:rows, col:col + 1], in_=st[:rows, col:col + 1], func=AF.Sqrt, bias=EPS, scale=1.0 / n),
              reads=[("st", col)], writes=[("st", col)])
        P.dve(lambda e: e.reciprocal(out=st[:rows, col:col + 1], in_=st[:rows, col:col + 1]), reads=[("st", col)], writes=[("st", col)])
        return col

    junks = [junk, r_bf(R4, 45056, [2048])]
    FG = dict(junks=junks, jkeys=[("junk", 0), ("junk", 1)], dump=dumpA)
    BGB = dict(junks=[junkh], jkeys=[("junkh",)], dump=None)

    def norm_stages(dst, gi, chunks, load_fn, bufs):
        out = []
        nj = len(bufs["junks"])
        for ci, (rows, col0) in enumerate(chunks):
            jb = bufs["junks"][ci % nj]
            jk = bufs["jkeys"][ci % nj]

            def s1(ci=ci, rows=rows, jb=jb, jk=jk):
                src, skey = load_fn(ci)
                if bufs["dump"] is not None:
                    col = rstd_col(src[:rows, :], rows, D, [skey], bufs["dump"][:rows, :], None)
                else:
                    col = rstd_col(src[:rows, :], rows, D, [skey], jb[:rows, :], jk)
                P.dve(lambda e: e.tensor_scalar(out=jb[:rows, :], in0=src[:rows, :], scalar1=st[:rows, col:col + 1], scalar2=None,
                                                op0=ALU.mult),
                      reads=[skey, ("st", col)], writes=[jk])

            def s2(ci=ci, rows=rows, col0=col0, jb=jb, jk=jk):
                for g4 in range(4):
                    bank = 6 + (g4 % 2)
                    pv = psb(bank)

                    def tr(e, g4=g4, pv=pv):
                        ins = None
                        for j in range(4):
                            kc = g4 * 4 + j
                            ins = e.transpose(pv[:, j, 0:rows], jb[:rows, kc * 128:(kc + 1) * 128], ident[:rows, :rows])
                        return ins

                    P.pe(tr, reads=[jk], writes=[("ps", bank)])
                    P.dve(lambda e, g4=g4, pv=pv: e.tensor_tensor(
                        out=dst[:, g4 * 4:(g4 + 1) * 4, col0:col0 + rows], in0=pv[:, 0:4, 0:rows],
                        in1=gcol[:, gi, g4 * 4:(g4 + 1) * 4].unsqueeze(2).to_broadcast([128, 4, rows]), op=ALU.mult),
                        reads=[("ps", bank)], writes=[("hT", g4, ci)])

            out.append((s1, s2))
        return out

    def norm_transpose(dst, gi, chunks, load_fn):
        stg = norm_stages(dst, gi, chunks, load_fn, FG)
        stg[0][0]()
        for ci in range(len(stg)):
            if ci + 1 < len(stg):
                stg[ci + 1][0]()
            stg[ci][1]()

    bgA = []

    pa = {"n": 0}

    def pumpA(n=1, every=1):
        pa["n"] += 1
        if pa["n"] % every != 0:
            return
        for _ in range(n):
            if bgA:
                bgA.pop(0)()

    def hT_reads(nchunks):
        return [("hT", g4, ci) for g4 in range(4) for ci in range(nchunks)]

    def proj_fm(ws, bi, fc, hsrc, hreads, col0, ncols, bank, nkc=16, kc0=0, mid=None):
        slot = ws.get(bi)
        parts = [(0, nkc)] if mid is None else [(0, nkc // 2), (nkc // 2, nkc)]
        for pi_, (ka, kb) in enumerate(parts):
            def f(e, ka=ka, kb=kb):
                ins = None
                for kc in range(ka, kb):
                    ins = e.matmul(PS[bank][:, 0:ncols], lhsT=WB[slot][:, kc, fc * 128:(fc + 1) * 128],
                                   rhs=hsrc[:, kc0 + kc, col0:col0 + ncols], start=(kc == 0), stop=(kc == nkc - 1), skip_group_check=True)
                return ins

            P.pe(f, reads=[("WB", slot)] + list(hreads), writes=[("ps", bank)])
            if mid is not None and pi_ == 0:
                mid()

    def proj_tm(ws, bi, hsrc, hreads, col0, rows, bank, nkc=16, kc0=0, start=True, stop=True, mid=None):
        slot = ws.get(bi)
        parts = [(0, nkc)] if mid is None else [(0, nkc // 2), (nkc // 2, nkc)]
        for pi_, (ka, kb) in enumerate(parts):
            def f(e, ka=ka, kb=kb):
                ins = None
                for kc in range(ka, kb):
                    ins = e.matmul(PS[bank][:rows, :], lhsT=hsrc[:, kc0 + kc, col0:col0 + rows],
                                   rhs=WB[slot][:, kc, :], start=(start and kc == 0), stop=(stop and kc == nkc - 1),
                                   skip_group_check=True)
                return ins

            P.pe(f, reads=[("WB", slot)] + list(hreads), writes=[("ps", bank)])
            if mid is not None and pi_ == 0:
                mid()

    ep_n = {"n": 0}
    late = []
    bg = []

    def flush_late(keep=0):
        while len(late) > keep:
            late.pop(0)()

    bgn = {"n": 0}

    def pump(n, banks=(7,)):
        for _ in range(n):
            if bg:
                bk = banks[bgn["n"] % len(banks)]
                bgn["n"] += 1
                bg.pop(0)(bk)

    def retention_epilogue(h, rows, col0, o_ap, o_key, sg_ap, sg_key):
        col = rstd_col(o_ap, rows, DV, [o_key], junkB[:rows, :], ("junkB",))
        gs = ep_n["n"] % 2
        ep_n["n"] += 1
        P.dve(lambda e: e.scalar_tensor_tensor(out=gotm[:rows, gs, :], in0=o_ap, scalar=st[:rows, col:col + 1],
                                               in1=sg_ap, op0=ALU.mult, op1=ALU.mult),
              reads=[o_key, ("st", col), sg_key], writes=[("gotm", gs)])

        def part2():
            pv = psb(6)

            def tr(e):
                ins = None
                for ec in range(4):
                    ins = e.transpose(pv[:, ec, 0:rows], gotm[:rows, gs, ec * 128:(ec + 1) * 128], ident[:rows, :rows])
                return ins

            P.pe(tr, reads=[("gotm", gs)], writes=[("ps", 6)])
            P.act(lambda e: e.activation(out=goT[:, h * 4:(h + 1) * 4, col0:col0 + rows], in_=pv[:, 0:4, 0:rows], func=AF.Copy),
                  reads=[("ps", 6)], writes=[("goT", h, col0)])

        late.append(part2)

    def sample_retention(h, hh, ci, TG, cx):
        g4 = math.exp(4.0 * LOGG[h])
        while bg:
            pump(1)
            flush_late()

        def sc_mm(e):
            ins = None
            for half in range(2):
                ins = e.matmul(PS[4][:64, 0:64], lhsT=kT[:, hh, half, TG:TG + 64], rhs=qT[:, hh, half, TG:TG + 64],
                               start=(half == 0), stop=(half == 1))
            return ins

        P.pe(sc_mm, reads=[("kT", hh, ci), ("qT", hh)], writes=[("ps", 4)])
        P.dve(lambda e: e.tensor_tensor(out=PT[:64, 0, 0:64], in0=PS[4][:64, 0:64], in1=bmask[:, :], op=ALU.mult),
              reads=[("ps", 4)], writes=[("PT", 0)])
        P.act(lambda e: e.mul(out=khat[:64, cx, :], in_=kTM[:64, ci, hh, :], mul=g4), reads=[("kTM", ci, hh)], writes=[("khat", cx)])
        P.act(lambda e: e.activation(out=qTs[:, cx], in_=qT[:, hh, :, TG:TG + 64], func=AF.Copy), reads=[("qT", hh)], writes=[("qTs", cx)])
        P.pe(lambda e: e.matmul(PS[7][:64, :], lhsT=PT[:64, 0, 0:64], rhs=vs[:64, cx, :], start=True, stop=True),
             reads=[("PT", 0), ("vs", cx)], writes=[("ps", 7)])
        P.act(lambda e: e.activation(out=osum[:64, cx, :], in_=PS[7][:64, :], func=AF.Copy), reads=[("ps", 7)], writes=[("osum", cx)])

        def load(b):
            P.dma("sp", Sf[:, b % 4], sret[b, h].rearrange("(a p) e -> p a e", p=128), ("sfi", b % 4), writes=[("Sf", b % 4)])

        for b in range(3):
            load(b)

        def cast(b):
            P.act(lambda e: e.activation(out=Sbf[:, b % 2], in_=Sf[:, b % 4], func=AF.Copy), reads=[("Sf", b % 4)], writes=[("Sbf", b % 2)])

        cast(0)

        def mk_cross(b):
            def f(bk):
                s4, s2 = b % 4, b % 2
                P.act(lambda e: e.mul(out=khmb[:64, s2, :], in_=khat[:64, cx, :], mul=bmT[:, b:b + 1]),
                      reads=[("khat", cx)], writes=[("khmb", s2)])

                def mm(e):
                    ins = None
                    for half in range(2):
                        ins = e.matmul(PS[bk][:64, :], lhsT=qTs[:, cx, half, :], rhs=Sbf[:, s2, half, :], start=(half == 0), stop=(half == 1))
                    return ins

                P.pe(mm, reads=[("qTs", cx), ("Sbf", s2)], writes=[("ps", bk)])
                P.dve(lambda e: e.scalar_tensor_tensor(out=osum[:64, cx, :], in0=PS[bk][:64, :], scalar=bmT[:, b:b + 1], in1=osum[:64, cx, :],
                                                       op0=ALU.mult, op1=ALU.add),
                      reads=[("ps", bk), ("osum", cx)], writes=[("osum", cx)])
                if b + 1 < NB:
                    cast(b + 1)
            return f

        def mk_state(b, half):
            def f(bk):
                s4, s2 = b % 4, b % 2
                P.pe(lambda e: e.matmul(PS[bk][:, :], lhsT=khmb[:64, s2, half * 128:(half + 1) * 128], rhs=vs[:64, cx, :], start=True, stop=True),
                     reads=[("khmb", s2), ("vs", cx)], writes=[("ps", bk)])
                P.dve(lambda e: e.scalar_tensor_tensor(out=Sf[:, s4, half, :], in0=Sf[:, s4, half, :], scalar=g4, in1=PS[bk][:, :],
                                                       op0=ALU.mult, op1=ALU.add),
                      reads=[("ps", bk), ("Sf", s4)], writes=[("Sf", s4)])
                if half == 1:
                    P.dma("sp", rets[b, h].rearrange("(a p) e -> p a e", p=128), Sf[:, s4], ("sfo", s4), reads=[("Sf", s4)])
                    if b + 3 < NB:
                        load(b + 3)
            return f

        for b in range(NB):
            bg.append(mk_cross(b))
            bg.append(mk_state(b, 0))
            bg.append(mk_state(b, 1))

        def fin(bk):
            retention_epilogue(h, 64, TG, osum[:64, cx, :], ("osum", cx), sgs[:64, cx, :], ("sgs", cx))
        bg.append(fin)

    ostage = r_f32(R4, 0, [DFF])
    on = {"n": 0}

    def out_rows(src_fn, nf, rows, dst):
        k = on["n"]
        on["n"] += 1
        stg = ostage if k % 2 == 0 else r_f32(R2, 8192, [2048])
        for g0 in range(0, nf, 4):
            bank = (g0 // 4) % 4

            def f(e, g0=g0, bank=bank):
                ins = None
                for j in range(4):
                    ins = e.transpose(PS[bank][:rows, j * 128:(j + 1) * 128], src_fn(g0 + j), identf[:, :])
                return ins

            P.pe(f, writes=[("ps", bank)])
            P.act(lambda e, bank=bank, g0=g0, stg=stg: e.activation(out=stg[:rows, g0 * 128:(g0 + 4) * 128], in_=PS[bank][:rows, :], func=AF.Copy),
                  reads=[("ps", bank)], writes=[("ostg", k % 2)])
        P.dma("sp", dst[:, :], stg[:rows, 0:nf * 128], ("ostg_o", k % 2), reads=[("ostg", k % 2)])

    def enqueue_phaseA(p):
        ps_ = PASSES[p]
        nch_ = ps_["nch"]
        src_ = xm if ps_["src"] == "xm" else xp
        chunks_ = [(128, c * 128) for c in range(nch_)] + ([(NSAMP, nch_ * 128)] if ps_["sample"] else [])

        def load(ci):
            slot = ci % 2
            rows = chunks_[ci][0]
            srcrows = src_[ps_["row0"] + ci * 128: ps_["row0"] + (ci + 1) * 128, :] if ci < nch_ else xs[:, :]
            P.dma("sp", XTh[slot][:rows, :], srcrows, ("xth", slot), writes=[("XTh", slot)])
            return XTh[slot], ("XTh", slot)

        for (s1, s2) in norm_stages(HT, 0, chunks_, load, BGB):
            bgA.append(s1)
            bgA.append(s2)

    blocks = []
    for ps_ in PASSES:
        kv_ = ps_["kv"]
        for hp in range(4):
            if not kv_:
                blocks.append((w_in, 0, C_Q + 512 * hp))
            blocks.append((w_in, 0, C_K + 512 * hp))
            for hh in range(2):
                h = 2 * hp + hh
                if not kv_:
                    blocks.append((w_in, 0, C_G + 512 * h))
                blocks.append((w_in, 0, C_V + 512 * h))
        if not kv_:
            for fg in range(4):
                blocks += [(w_in, 0, C_GC + 512 * fg), (w_in, 0, C_HC + 512 * fg), (w_in, 0, C_GB + 512 * fg)]
            for fg in range(4):
                blocks += [(p_ret, 0, 512 * fg), (p_ret, 2048, 512 * fg), (w_in, 0, C_GA + 512 * fg), (p_conv, 0, 512 * fg),
                           (w_in, 0, C_GBT + 512 * fg)]
            for nb in range(4):
                blocks += [(w_o, 0, 512 * nb)]
            for fg in range(11):
                blocks += [(w_up, 0, 512 * fg), (w_gate, 0, 512 * fg)]
            for nb in range(4):
                for sub in range(3):
                    blocks += [(w_down, sub * 2048, 512 * nb, 16 if sub < 2 else 12)]
    ws = WStream(blocks)
    bi = 0
    ws.get(0)

    for pi, ps in enumerate(PASSES):
        nch, kv, has_s = ps["nch"], ps["kv"], ps["sample"]
        TG = nch * 128
        TT = TG + (NSAMP if has_s else 0)
        src = xm if ps["src"] == "xm" else xp
        hT = hT0 if kv else HT
        chunks = [(128, c * 128) for c in range(nch)] + ([(NSAMP, TG)] if has_s else [])
        nchk = len(chunks)
        tiles = []
        ntl = (TT + 511) // 512
        tw = ((TT + ntl - 1) // ntl + 31) // 32 * 32
        c0 = 0
        while c0 < TT:
            n = min(tw, TT - c0)
            segs = []
            if c0 < TG:
                segs.append((0, min(n, TG - c0), False, c0))
            if c0 + n > TG:
                a = max(c0, TG)
                segs.append((a - c0, c0 + n - a, True, a))
            tiles.append((c0, n, segs))
            c0 += n

        def xrows(ci, src=src, ps=ps, nch=nch):
            if ci < nch:
                return src[ps["row0"] + ci * 128: ps["row0"] + (ci + 1) * 128, :]
            return xs[:, :]

        def load_x(ci, chunks=chunks, xrows=xrows):
            slot = ci % 4
            rows = chunks[ci][0]
            P.dma("sp", XT[slot][:rows, :], xrows(ci), ("xt", slot), writes=[("XT", slot)])
            return XT[slot], ("XT", slot)

        if pi == 0:
            norm_transpose(hT, 0, chunks, load_x)
            P.barrier()
            enqueue_phaseA(1)
        HR = []
        P.dma("sp", ckt[:, 0:7, :], ck_d[pi], "ckt", writes=[("ckt",)])
        P.dma("sp", skt[:, 0:7, :], sk_d[pi], "skt", writes=[("skt",)])
        NPUMP = 4

        for hp in range(4):
            if not kv:
                qn = 0
                for hh in range(2):
                    h = 2 * hp + hh
                    P.dma("sp", cqt[:, 0:640], cq_d[pi - 1, h], "cqt", writes=[("cqt",)])
                    P.dma("sp", sqt[:, 0:640], sq_d[pi - 1, h], "sqt", writes=[("sqt",)])
                    for ti, (t0, tn, _s) in enumerate(tiles):
                        b0_, b1_ = (0, 1) if qn % 2 == 0 else (2, 3)
                        ta, tb2 = (rt1, rt2) if qn % 2 == 0 else (rt3, rt4)
                        qn += 1
                        proj_fm(ws, bi, 2 * hh, hT, HR, t0, tn, b0_, mid=lambda: pump(1, (7, 6, 4, 5)))
                        pump(1, (7, 6, 4, 5))
                        proj_fm(ws, bi, 2 * hh + 1, hT, HR, t0, tn, b1_, mid=lambda: pump(1, (7, 6, 4, 5)))
                        flush_late()
                        pump(1, (7, 6, 4, 5))
                        x1, x2 = PS[b0_][:, 0:tn], PS[b1_][:, 0:tn]
                        cs, sn = cqt[:, t0:t0 + tn], sqt[:, t0:t0 + tn]
                        ka, kb = ("rt", id(ta)), ("rt", id(tb2))
                        P.dve(lambda e, x1=x1, cs=cs, tn=tn, ta=ta: e.tensor_tensor(out=ta[:, 0:tn], in0=x1, in1=cs, op=ALU.mult),
                              reads=[("ps", b0_), ("cqt",)], writes=[ka])
                        P.dve(lambda e, x2=x2, sn=sn, tn=tn, tb2=tb2: e.tensor_tensor(out=tb2[:, 0:tn], in0=x2, in1=sn, op=ALU.mult),
                              reads=[("ps", b1_), ("sqt",)], writes=[kb])
                        P.pool(lambda e, hh=hh, t0=t0, tn=tn, ta=ta, tb2=tb2: e.tensor_tensor(out=qT[:, hh, 0, t0:t0 + tn], in0=ta[:, 0:tn],
                                                                                           in1=tb2[:, 0:tn], op=ALU.subtract),
                               reads=[ka, kb], writes=[("qT", hh)])
                        P.dve(lambda e, x2=x2, cs=cs, tn=tn, ta=ta: e.tensor_tensor(out=ta[:, 0:tn], in0=x2, in1=cs, op=ALU.mult),
                              reads=[("ps", b1_), ("cqt",)], writes=[ka])
                        P.dve(lambda e, x1=x1, sn=sn, tn=tn, tb2=tb2: e.tensor_tensor(out=tb2[:, 0:tn], in0=x1, in1=sn, op=ALU.mult),
                              reads=[("ps", b0_), ("sqt",)], writes=[kb])
                        P.pool(lambda e, hh=hh, t0=t0, tn=tn, ta=ta, tb2=tb2: e.tensor_tensor(out=qT[:, hh, 1, t0:t0 + tn], in0=ta[:, 0:tn],
                                                                                           in1=tb2[:, 0:tn], op=ALU.add),
                               reads=[ka, kb], writes=[("qT", hh)])
                bi += 1
            for ci, (rows, col0) in enumerate(chunks):
                bank = 2 + (ci % 2)
                proj_tm(ws, bi, hT, HR, col0, rows, bank, mid=lambda: pump(1, (7, 6, 4, 5)))
                flush_late()
                pump(1, (7, 6, 4, 5))
                if kv:
                    pumpA(1, every=4)
                for hh in range(2):
                    h = 2 * hp + hh
                    x1 = PS[bank][:rows, hh * 256: hh * 256 + 128]
                    x2 = PS[bank][:rows, hh * 256 + 128: hh * 256 + 256]
                    cs, sn = ckt[:rows, ci, :], skt[:rows, ci, :]
                    sc = ksc[:rows, pi, ci, h:h + 1]
                    rk = [("ps", bank), ("ckt",), ("skt",)]
                    ta, tb2 = (rt1, rt2) if hh == 0 else (rt3, rt4)
                    ka, kb = ("rt", id(ta)), ("rt", id(tb2))

                    def stt(e, o_, a_, b_, sc=sc):
                        return e.scalar_tensor_tensor(out=o_, in0=a_, scalar=sc, in1=b_, op0=ALU.mult, op1=ALU.mult)

                    P.dve(lambda e, x1=x1, cs=cs, rows=rows, stt=stt, ta=ta: stt(e, ta[:rows, 0:128], x1, cs), reads=rk, writes=[ka])
                    P.dve(lambda e, x2=x2, sn=sn, rows=rows, stt=stt, tb2=tb2: stt(e, tb2[:rows, 0:128], x2, sn), reads=rk, writes=[kb])
                    P.dve(lambda e, x2=x2, cs=cs, rows=rows, stt=stt, ta=ta: stt(e, ta[:rows, 128:256], x2, cs), reads=rk, writes=[ka])
                    P.dve(lambda e, x1=x1, sn=sn, rows=rows, stt=stt, tb2=tb2: stt(e, tb2[:rows, 128:256], x1, sn), reads=rk, writes=[kb])
                    P.pool(lambda e, rows=rows, ci=ci, hh=hh, ta=ta, tb2=tb2: e.tensor_tensor(out=kTM[:rows, ci, hh, 0:128], in0=ta[:rows, 0:128],
                                                                                           in1=tb2[:rows, 0:128], op=ALU.subtract),
                           reads=[ka, kb], writes=[("kTM", ci, hh)])
                    P.pool(lambda e, rows=rows, ci=ci, hh=hh, ta=ta, tb2=tb2: e.tensor_tensor(out=kTM[:rows, ci, hh, 128:256], in0=ta[:rows, 128:256],
                                                                                           in1=tb2[:rows, 128:256], op=ALU.add),
                           reads=[ka, kb], writes=[("kTM", ci, hh)])
                    if not kv:
                        def part2(rows=rows, ci=ci, hh=hh, col0=col0):
                            pv = psb(6)

                            def trk(e):
                                ins = None
                                for half in range(2):
                                    ins = e.transpose(pv[:, half, 0:rows], kTM[:rows, ci, hh, half * 128:(half + 1) * 128], ident[:rows, :rows])
                                return ins

                            P.pe(trk, reads=[("kTM", ci, hh)], writes=[("ps", 6)])
                            P.act(lambda e: e.activation(out=kT[:, hh, :, col0:col0 + rows], in_=pv[:, 0:2, 0:rows], func=AF.Copy),
                                  reads=[("ps", 6)], writes=[("kT", hh, ci)])

                        late.append(part2)
            bi += 1
            for hh in range(2):
                h = 2 * hp + hh
                cx = h % 2
                if not kv:
                    for ci, (rows, col0) in enumerate(chunks):
                        bank = 2 + (ci % 2)
                        proj_tm(ws, bi, hT, HR, col0, rows, bank, mid=lambda: pump(2, (7, 6, 4, 5, 0, 1)))
                        flush_late()
                        pump(2, (7, 6, 4, 5, 0, 1))
                        if ci < nch:
                            P.act(lambda e, rows=rows, ci=ci, bank=bank: e.activation(out=sg[:rows, ci, :], in_=PS[bank][:rows, :], func=AF.Silu),
                                  reads=[("ps", bank)], writes=[("sg", ci)])
                        else:
                            P.act(lambda e, rows=rows, cx=cx, bank=bank: e.activation(out=sgs[:rows, cx, :], in_=PS[bank][:rows, :], func=AF.Silu),
                                  reads=[("ps", bank)], writes=[("sgs", cx)])
                    bi += 1
                if pi == 0:
                    P.dve(lambda e: e.memset(base[:], 0.0), writes=[("base",)])
                else:
                    P.dma("sp", base[:].rearrange("p a b -> p (a b)"), sbase_d[h], "base", writes=[("base",)])
                if not kv:
                    P.act(lambda e: e.activation(out=Sb[:, 0], in_=base[:], func=AF.Copy), reads=[("base",)], writes=[("Sb", 0)])
                sbs = 0

                def v_proj(ci, hh=hh, cx=cx):
                    rows, col0 = chunks[ci]
                    bank = 2 + (ci % 2)
                    proj_tm(ws, bi, hT, HR, col0, rows, bank, mid=lambda: pump(1))
                    if ci >= nch:
                        P.act(lambda e: e.activation(out=vs[:rows, cx, :], in_=PS[bank][:rows, :], func=AF.Copy),
                              reads=[("ps", bank)], writes=[("vs", cx)])
                    else:
                        P.act(lambda e: e.activation(out=vS[:rows, ci % 2, :], in_=PS[bank][:rows, :], func=AF.Copy),
                              reads=[("ps", bank)], writes=[("vS", ci % 2)])

                def scores(ci, hh=hh):
                    col0 = chunks[ci][1]
                    pslot = ci % 2

                    def sc_mm(e):
                        ins = None
                        for half in range(2):
                            ins = e.matmul(PS[4][:, 0:128], lhsT=kT[:, hh, half, col0:col0 + 128], rhs=qT[:, hh, half, col0:col0 + 128],
                                           start=(half == 0), stop=(half == 1))
                        return ins

                    P.pe(sc_mm, reads=[("kT", hh, ci), ("qT", hh)], writes=[("ps", 4)])
                    P.dve(lambda e: e.tensor_tensor(out=PT[:, pslot, :], in0=PS[4][:, 0:128], in1=cmask[:, :], op=ALU.mult),
                          reads=[("ps", 4)], writes=[("PT", pslot)])

                v_proj(0)
                if not kv:
                    scores(0)
                for ci in range(nch):
                    rows, col0 = chunks[ci]
                    vslot = ci % 2
                    if ci + 1 < nchk:
                        v_proj(ci + 1)
                    flush_late(1)
                    pump(1)
                    if kv:
                        pumpA(1, every=4)
                    if not kv:
                        def o_mm(e, hh=hh, col0=col0, vslot=vslot, sbs=sbs, ci=ci):
                            e.matmul(PS[5][:, :], lhsT=PT[:, ci % 2, :], rhs=vS[:, vslot, :], start=True, stop=False)
                            ins = None
                            for half in range(2):
                                ins = e.matmul(PS[5][:, :], lhsT=qT[:, hh, half, col0:col0 + 128], rhs=Sb[:, sbs, half, :],
                                               start=False, stop=(half == 1))
                            return ins

                        P.pe(o_mm, reads=[("PT", ci % 2), ("vS", vslot), ("qT", hh), ("Sb", sbs)], writes=[("ps", 5)])

                    def s_mm(e, ci=ci, hh=hh, vslot=vslot):
                        ins = None
                        for half in range(2):
                            ins = e.matmul(PS[half][:, :], lhsT=kTM[:, ci, hh, half * 128:(half + 1) * 128], rhs=vS[:, vslot, :],
                                           start=(ci == 0), stop=True, skip_group_check=True)
                        return ins

                    P.pe(s_mm, reads=[("kTM", ci, hh), ("vS", vslot)], writes=[("ps", 0), ("ps", 1)])
                    pump(1)
                    if not kv and ci + 1 < nch:
                        scores(ci + 1)
                    last_prompt = (ci == nch - 1)
                    if not kv and not last_prompt:
                        nsb = 1 - sbs
                        for half in range(2):
                            P.dve(lambda e, half=half, nsb=nsb: e.tensor_tensor(out=Sb[:, nsb, half, :], in0=PS[half][:, :], in1=base[:, half, :],
                                                                                 op=ALU.add),
                                  reads=[("ps", half), ("base",)], writes=[("Sb", nsb)])
                        sbs = nsb
                    if last_prompt:
                        for half in range(2):
                            P.dve(lambda e, half=half: e.tensor_tensor(out=base[:, half, :], in0=PS[half][:, :], in1=base[:, half, :], op=ALU.add),
                                  reads=[("ps", half), ("base",)], writes=[("base",)])
                        if pi < 2:
                            P.dma("sp", sbase_d[h], base[:].rearrange("p a b -> p (a b)"), "base_o", reads=[("base",)])
                        else:
                            fs = math.exp(LOGG[h] * (2047.0 - TOFF))
                            P.act(lambda e, fs=fs: e.mul(out=base[:], in_=base[:], mul=fs), reads=[("base",)],
                                  writes=[("base",)])
                            P.dma("sp", retp[h].rearrange("(a p) e -> p a e", p=128), base[:], "base_o", reads=[("base",)])
                    if not kv:
                        retention_epilogue(h, 128, col0, PS[5][:, :], ("ps", 5), sg[:, ci, :], ("sg", ci))
                if has_s:
                    sample_retention(h, hh, nch, TG, cx)
                bi += 1
        flush_late()
        while bg:
            pump(1)
            flush_late()
        pumpA(1000, every=1)
        P.barrier()
        if kv:
            continue

        def conv3(dstP, dstS, up, upkey, dkey, f, wt, bias, TG=TG, has_s=has_s):
            upP = up[:, 0:2 + TG]
            segs = [(dstP, lambda k, upP=upP, TG=TG: upP[:, k:k + TG])]
            if has_s:
                upS = up[:, 642:738].rearrange("p (b t) -> p b t", t=6)
                segs.append((dstS, lambda k, upS=upS: upS[:, :, k:k + 4]))
            for (dd, sl) in segs:
                if bias is None:
                    P.dve(lambda e, dd=dd, sl=sl: e.tensor_scalar(out=dd, in0=sl(0), scalar1=wt[:, f, 0:1], scalar2=None, op0=ALU.mult),
                           reads=[upkey], writes=[dkey])
                else:
                    P.dve(lambda e, dd=dd, sl=sl: e.tensor_scalar(out=dd, in0=sl(0), scalar1=wt[:, f, 0:1], scalar2=bias, op0=ALU.mult,
                                                                   op1=ALU.add),
                           reads=[upkey], writes=[dkey])
                for k in (1, 2):
                    P.dve(lambda e, dd=dd, sl=sl, k=k: e.scalar_tensor_tensor(out=dd, in0=sl(k), scalar=wt[:, f, k:k + 1], in1=dd,
                                                                               op0=ALU.mult, op1=ALU.add),
                           reads=[upkey, dkey], writes=[dkey])

        for fg in range(4):
            for fc in range(4):
                for ti, (t0, tn, _s) in enumerate(tiles):
                    bank = (fc * 2 + ti) % 4
                    proj_fm(ws, bi, fc, hT, HR, t0, tn, bank)
                    P.act(lambda e, fc=fc, t0=t0, tn=tn, bank=bank: e.activation(out=gcs[:, fc, t0:t0 + tn], in_=PS[bank][:, 0:tn], func=AF.Copy),
                          reads=[("ps", bank)], writes=[("gcs", fc)])
            bi += 1
            for fc in range(4):
                f = fg * 4 + fc
                up = UPc[:, fc, :]
                upkey = ("upc", fc)
                P.dve(lambda e, up=up, f=f: e.tensor_copy(out=up[:, 0:2], in_=halo_c[:, f, :]), reads=[("halo_c", f)], writes=[upkey])
                if has_s:
                    P.dve(lambda e, up=up, f=f: e.tensor_copy(out=up[:, 642:738].rearrange("p (b t) -> p b t", t=6)[:, :, 0:2],
                                                              in_=halo_cs[:, f, :, :]), reads=[("halo_cs", f)], writes=[upkey])
                for ti, (t0, tn, segs) in enumerate(tiles):
                    bank = (fc * 2 + ti) % 4
                    proj_fm(ws, bi, fc, hT, HR, t0, tn, bank)
                    for (so, sn, is_s, sc0) in segs:
                        if not is_s:
                            P.dve(lambda e, up=up, fc=fc, so=so, sn=sn, sc0=sc0, bank=bank: e.tensor_tensor(
                                out=up[:, 2 + sc0:2 + sc0 + sn], in0=PS[bank][:, so:so + sn], in1=gcs[:, fc, sc0:sc0 + sn], op=ALU.mult),
                                reads=[("ps", bank), ("gcs", fc)], writes=[upkey])
                        else:
                            P.dve(lambda e, up=up, fc=fc, so=so, sn=sn, sc0=sc0, bank=bank: e.tensor_tensor(
                                out=up[:, 642:738].rearrange("p (b t) -> p b t", t=6)[:, :, 2:6],
                                in0=PS[bank][:, so:so + sn].rearrange("p (b t) -> p b t", t=4),
                                in1=gcs[:, fc, sc0:sc0 + sn].rearrange("p (b t) -> p b t", t=4), op=ALU.mult),
                                reads=[("ps", bank), ("gcs", fc)], writes=[upkey])
                P.dve(lambda e, up=up, f=f, TG=TG: e.tensor_copy(out=halo_c[:, f, :], in_=up[:, TG:TG + 2]), reads=[upkey],
                      writes=[("halo_c", f)])
                if has_s:
                    P.dve(lambda e, up=up, f=f: e.tensor_copy(out=halo_cs[:, f, :, :],
                                                              in_=up[:, 642:738].rearrange("p (b t) -> p b t", t=6)[:, :, 4:6]),
                          reads=[upkey], writes=[("halo_cs", f)])
                conv3(cv[:, fc, 0:TG], cv[:, fc, TG:TG + 64].rearrange("p (b t) -> p b t", t=4) if has_s else None, up, upkey, ("cv", fc), f, cw, None)
            bi += 1
            for fc in range(4):
                f = fg * 4 + fc
                for ti, (t0, tn, _s) in enumerate(tiles):
                    bank = (fc * 2 + ti) % 4
                    proj_fm(ws, bi, fc, hT, HR, t0, tn, bank)
                    P.dve(lambda e, f=f, fc=fc, t0=t0, tn=tn, bank=bank: e.tensor_tensor(
                        out=gbcT[:, f, t0:t0 + tn], in0=PS[bank][:, 0:tn], in1=cv[:, fc, t0:t0 + tn], op=ALU.mult),
                        reads=[("ps", bank), ("cv", fc)], writes=[("gbcT", f)])
            bi += 1
        P.barrier()

        for fg in range(4):
            for (kc0, first) in ((0, True), (16, False)):
                for fc in range(4):
                    for ti, (t0, tn, _s) in enumerate(tiles):
                        bank = (fc * 2 + ti) % 4
                        proj_fm(ws, bi, fc, goT, [], t0, tn, bank, kc0=kc0)
                        if first:
                            P.act(lambda e, fc=fc, t0=t0, tn=tn, bank=bank: e.activation(out=ya[:, fc, t0:t0 + tn], in_=PS[bank][:, 0:tn],
                                                                                        func=AF.Copy),
                                  reads=[("ps", bank)], writes=[("ya", fc)])
                        else:
                            P.dve(lambda e, fc=fc, t0=t0, tn=tn, bank=bank: e.tensor_tensor(out=ya[:, fc, t0:t0 + tn], in0=PS[bank][:, 0:tn],
                                                                                            in1=ya[:, fc, t0:t0 + tn], op=ALU.add),
                                  reads=[("ps", bank), ("ya", fc)], writes=[("ya", fc)])
                bi += 1
            for fc in range(4):
                for ti, (t0, tn, _s) in enumerate(tiles):
                    bank = (fc * 2 + ti) % 4
                    proj_fm(ws, bi, fc, hT, [], t0, tn, bank)
                    P.act(lambda e, fc=fc, t0=t0, tn=tn, bank=bank: e.activation(out=sgm[:, fc, t0:t0 + tn], in_=PS[bank][:, 0:tn],
                                                                                func=AF.Sigmoid),
                          reads=[("ps", bank)], writes=[("sgm", fc)])
                    P.pool(lambda e, fc=fc, t0=t0, tn=tn: e.tensor_tensor(out=ya[:, fc, t0:t0 + tn], in0=ya[:, fc, t0:t0 + tn],
                                                                          in1=sgm[:, fc, t0:t0 + tn], op=ALU.mult),
                           reads=[("ya", fc), ("sgm", fc)], writes=[("ya", fc)])
            bi += 1
            for fc in range(4):
                for ti, (t0, tn, _s) in enumerate(tiles):
                    bank = (fc * 2 + ti) % 4
                    proj_fm(ws, bi, fc, gbcT, [], t0, tn, bank)
                    P.act(lambda e, fc=fc, t0=t0, tn=tn, bank=bank: e.activation(out=yb[:, fc, t0:t0 + tn], in_=PS[bank][:, 0:tn], func=AF.Copy),
                          reads=[("ps", bank)], writes=[("yb", fc)])
            bi += 1
            for fc in range(4):
                f = fg * 4 + fc
                for ti, (t0, tn, _s) in enumerate(tiles):
                    bank = (fc * 2 + ti) % 4
                    proj_fm(ws, bi, fc, hT, [], t0, tn, bank)
                    P.act(lambda e, fc=fc, t0=t0, tn=tn, bank=bank: e.activation(out=sgm[:, fc, t0:t0 + tn], in_=PS[bank][:, 0:tn],
                                                                                func=AF.Sigmoid),
                          reads=[("ps", bank)], writes=[("sgm", fc)])
                    P.pool(lambda e, fc=fc, t0=t0, tn=tn: e.tensor_tensor(out=yb[:, fc, t0:t0 + tn], in0=yb[:, fc, t0:t0 + tn],
                                                                          in1=sgm[:, fc, t0:t0 + tn], op=ALU.mult),
                           reads=[("yb", fc), ("sgm", fc)], writes=[("yb", fc)])
                    P.pool(lambda e, f=f, fc=fc, t0=t0, tn=tn: e.tensor_tensor(out=mT[:, f, t0:t0 + tn], in0=ya[:, fc, t0:t0 + tn],
                                                                               in1=yb[:, fc, t0:t0 + tn], op=ALU.add),
                           reads=[("ya", fc), ("yb", fc)], writes=[("mT", f)])
            bi += 1
        P.barrier()

        P.dma("sp", GP[:, :], gpost_d[0], "gp", writes=[("GP",)])
        for nb in range(4):
            for ci, (rows, col0) in enumerate(chunks):
                bank = ci % 4
                proj_tm(ws, bi, mT, [], col0, rows, bank)
                P.act(lambda e, rows=rows, ci=ci, nb=nb, bank=bank: e.activation(out=XF[:rows, ci, nb * 512:(nb + 1) * 512], in_=PS[bank][:rows, :],
                                                                                func=AF.Copy),
                      reads=[("ps", bank)], writes=[("XF", ci)])
            bi += 1
        for ci, (rows, col0) in enumerate(chunks):
            col = rstd_col(XF[:rows, ci, :], rows, D, [("XF", ci)], dumpA[:rows, :], None)
            P.dve(lambda e, rows=rows, ci=ci, col=col: e.scalar_tensor_tensor(out=XF[:rows, ci, :], in0=XF[:rows, ci, :],
                                                                             scalar=st[:rows, col:col + 1], in1=GP[:rows, :],
                                                                             op0=ALU.mult, op1=ALU.mult),
                  reads=[("XF", ci), ("st", col), ("GP",)], writes=[("XF", ci)])
            P.dma("pool", XF[:rows, ci, :], xrows(ci), ("xacc", ci), reads=[("XF", ci)], writes=[("XF", ci)], accum=True)
            P.dma("sp", xmid_d[ci, 0:rows, :], XF[:rows, ci, :], ("xmo", ci), reads=[("XF", ci)])
        norm_transpose(HT, 1, chunks, lambda ci: (XF[:, ci, :], ("XF", ci)))
        P.barrier()

        for fg in range(11):
            for fc in range(4):
                f = fg * 4 + fc
                up = UPf[:, fc % 2, :]
                upkey = ("upf", fc % 2)
                P.dve(lambda e, up=up, f=f: e.tensor_copy(out=up[:, 0:2], in_=halo_f[:, f, :]), reads=[("halo_f", f)], writes=[upkey])
                if has_s:
                    P.dve(lambda e, up=up, f=f: e.tensor_copy(out=up[:, 642:738].rearrange("p (b t) -> p b t", t=6)[:, :, 0:2],
                                                              in_=halo_fs[:, f, :, :]), reads=[("halo_fs", f)], writes=[upkey])
                for ti, (t0, tn, segs) in enumerate(tiles):
                    bank = (fc * 2 + ti) % 4
                    proj_fm(ws, bi, fc, HT, [], t0, tn, bank)
                    for (so, sn, is_s, sc0) in segs:
                        if not is_s:
                            P.act(lambda e, up=up, so=so, sn=sn, sc0=sc0, bank=bank: e.activation(out=up[:, 2 + sc0:2 + sc0 + sn],
                                                                                                 in_=PS[bank][:, so:so + sn], func=AF.Copy),
                                  reads=[("ps", bank)], writes=[upkey])
                        else:
                            P.act(lambda e, up=up, so=so, sn=sn, bank=bank: e.activation(
                                out=up[:, 642:738].rearrange("p (b t) -> p b t", t=6)[:, :, 2:6],
                                in_=PS[bank][:, so:so + sn].rearrange("p (b t) -> p b t", t=4), func=AF.Copy),
                                reads=[("ps", bank)], writes=[upkey])
                P.dve(lambda e, up=up, f=f, TG=TG: e.tensor_copy(out=halo_f[:, f, :], in_=up[:, TG:TG + 2]), reads=[upkey],
                      writes=[("halo_f", f)])
                if has_s:
                    P.dve(lambda e, up=up, f=f: e.tensor_copy(out=halo_fs[:, f, :, :],
                                                              in_=up[:, 642:738].rearrange("p (b t) -> p b t", t=6)[:, :, 4:6]),
                          reads=[upkey], writes=[("halo_fs", f)])
                conv3(gaB[:, fc, 0:TG], gaB[:, fc, TG:TG + 64].rearrange("p (b t) -> p b t", t=4) if has_s else None, up, upkey,
                      ("ga", fc), f, fcw, fcb[:, f:f + 1])
                P.act(lambda e, fc=fc, TT=TT: e.activation(out=gaB[:, fc, 0:TT], in_=gaB[:, fc, 0:TT], func=AF.Gelu_apprx_tanh),
                      reads=[("ga", fc)], writes=[("ga", fc)])
            bi += 1
            for fc in range(4):
                f = fg * 4 + fc
                for ti, (t0, tn, _s) in enumerate(tiles):
                    bank = (fc * 2 + ti) % 4
                    proj_fm(ws, bi, fc, HT, [], t0, tn, bank)
                    P.dve(lambda e, f=f, fc=fc, t0=t0, tn=tn, bank=bank: e.tensor_tensor(
                        out=aT[:, f, t0:t0 + tn], in0=PS[bank][:, 0:tn], in1=gaB[:, fc, t0:t0 + tn], op=ALU.mult),
                        reads=[("ps", bank), ("ga", fc)], writes=[("aT", f)])
            bi += 1
        P.barrier()

        if pi + 1 < len(PASSES):
            enqueue_phaseA(pi + 1)
        e2n = 0
        for nb in range(4):
            for sub in range(3):
                nkc = 16 if sub < 2 else 12
                for ci, (rows, col0) in enumerate(chunks):
                    proj_tm(ws, bi, aT, [], col0, rows, ci, nkc=nkc, kc0=sub * 16, start=(sub == 0), stop=(sub == 2))
                    pumpA(1, every=5)
                    if sub == 2:
                        P.act(lambda e, rows=rows, ci=ci, nb=nb: e.activation(out=XF[:rows, ci, nb * 512:(nb + 1) * 512], in_=PS[ci][:rows, :],
                                                                             func=AF.Copy),
                              reads=[("ps", ci)], writes=[("XF", ci)])
                bi += 1
        pumpA(1000, every=1)
        P.barrier()
        if pi == len(PASSES) - 1:
            out_rows(lambda f: halo_f[:, f, :], NFF, 2, ffnp)
            out_rows(lambda f: halo_c[:, f, :], 16, 2, convp)
            out_rows(lambda f: halo_fs[:, f, :, :].rearrange("p b t -> p (b t)"), NFF, 32, ffns)
            out_rows(lambda f: halo_cs[:, f, :, :].rearrange("p b t -> p (b t)"), 16, 32, convs)
        P.dma("sp", GP[:, :], gpost_d[1], "gp", writes=[("GP",)])
        def ld_xmid(ci):
            P.dma("sp", XT[ci % 4][:chunks[ci][0], :], xmid_d[ci, 0:chunks[ci][0], :], ("xt", ci % 4), writes=[("XT", ci % 4)])

        for ci in range(min(4, nchk)):
            ld_xmid(ci)
        for ci, (rows, col0) in enumerate(chunks):
            slot = ci % 4
            col = rstd_col(XF[:rows, ci, :], rows, D, [("XF", ci)], junks[1][:rows, :], None)
            P.dve(lambda e, rows=rows, ci=ci, col=col: e.scalar_tensor_tensor(out=XF[:rows, ci, :], in0=XF[:rows, ci, :],
                                                                             scalar=st[:rows, col:col + 1], in1=GP[:rows, :],
                                                                             op0=ALU.mult, op1=ALU.mult),
                  reads=[("XF", ci), ("st", col), ("GP",)], writes=[("XF", ci)])
            P.dve(lambda e, rows=rows, ci=ci, slot=slot: e.tensor_tensor(out=XF[:rows, ci, :], in0=XF[:rows, ci, :], in1=XT[slot][:rows, :],
                                                                         op=ALU.add),
                  reads=[("XF", ci), ("XT", slot)], writes=[("XF", ci)])
            if ci + 4 < nchk:
                ld_xmid(ci + 4)
            if ci < nch:
                dst = ym[ps["out0"] + ci * 128: ps["out0"] + (ci + 1) * 128, :]
            else:
                dst = ys[:, :]
            P.dma("sp", dst, XF[:rows, ci, :], ("yo", ci), reads=[("XF", ci)])
        P.barrier()

    P.emit()
    return nc


def _tables(core):
    half = core % 2
    f32 = np.float32
    inv = (f32(10000.0) ** (-(np.arange(128, dtype=f32) / f32(128)))).astype(f32)

    def cs(pos):
        ang = (np.asarray(pos, dtype=f32)[:, None] * inv[None, :]).astype(f32)
        return np.cos(ang).astype(np.float64), np.sin(ang).astype(np.float64)

    logg = np.array(LOGG, dtype=np.float64)
    m = np.arange(NMAIN)
    pos_main = np.maximum(half * 1024 - 128 + m, 0)
    t_main = 896 + m
    pos_pre = np.arange(NPRE)
    t_pre = np.arange(NPRE)
    tau = np.arange(NSAMP) % 4
    pos_s = 16384 + tau
    cm, sm = cs(pos_main)
    cp, sp_ = cs(pos_pre)
    c_s, s_s = cs(pos_s)

    cq = np.zeros((2, NH, 128, 640), np.float64)
    sq = np.zeros((2, NH, 128, 640), np.float64)
    ck = np.zeros((3, 128, 7, 128), np.float64)
    sk = np.zeros((3, 128, 7, 128), np.float64)
    ksc = np.zeros((128, 3, 7, NH), np.float64)
    ck[0] = cp.reshape(7, 128, 128).transpose(1, 0, 2)
    sk[0] = sp_.reshape(7, 128, 128).transpose(1, 0, 2)
    for h in range(NH):
        ksc[:, 0, :, h] = (np.exp(-logg[h] * (t_pre - TOFF)) / 16.0).reshape(7, 128).T
    for p, (row0, nch) in ((1, (0, 5)), (2, (640, 4))):
        rows = slice(row0, row0 + nch * 128)
        ck[p, :, :nch] = cm[rows].reshape(nch, 128, 128).transpose(1, 0, 2)
        sk[p, :, :nch] = sm[rows].reshape(nch, 128, 128).transpose(1, 0, 2)
        for h in range(NH):
            dq = np.exp(logg[h] * (t_main[rows] - TOFF))
            cq[p - 1, h, :, :nch * 128] = (cm[rows] * dq[:, None]).T
            sq[p - 1, h, :, :nch * 128] = (sm[rows] * dq[:, None]).T
            ksc[:, p, :nch, h] = (np.exp(-logg[h] * (t_main[rows] - TOFF)) / 16.0).reshape(nch, 128).T
    ck[2, :64, 4] = c_s
    sk[2, :64, 4] = s_s
    for h in range(NH):
        dq = np.exp(logg[h] * (tau + 1.0))
        cq[1, h, :, 512:576] = (c_s * dq[:, None]).T
        sq[1, h, :, 512:576] = (s_s * dq[:, None]).T
        ksc[:64, 2, 4, h] = np.exp(-logg[h] * (tau + 1.0)) / 16.0
    i = np.arange(128)
    cmask = (i[None, :] >= i[:, None]).astype(f32)
    j64 = np.arange(64)
    bmask = ((j64[None, :] // 4 == j64[:, None] // 4) & (j64[None, :] >= j64[:, None])).astype(f32)
    bm16 = np.broadcast_to((j64[None, :] // 4 == np.arange(NB)[:, None]).astype(f32)[None], (128, NB, 64)).copy()
    bmT = (j64[:, None] // 4 == np.arange(NB)[None, :]).astype(f32)
    return dict(cq=cq.astype(f32), sq=sq.astype(f32), ck=ck.astype(f32), sk=sk.astype(f32), ksc=ksc.astype(f32),
                cmask=cmask, bmask=bmask, bm16=bm16, bmT=bmT, idf=np.eye(128, dtype=f32))


def _in_map(core, x_prompt, x_sample, state_ret, state_conv, state_ffn, g_pre_mix, w_in, conv_w, p_ret, p_conv, w_o, g_post_mix,
            g_pre_ffn, w_up, w_gate, ffn_conv_w, ffn_conv_b, w_down, g_post_ffn):
    f32 = np.float32
    b, half = core // 2, core % 2
    xm = np.zeros((NMAIN, D), f32)
    xp = np.zeros((NPRE, D), f32)
    if half == 0:
        xm[128:] = x_prompt[b, 0:1024]
    else:
        xm[:] = x_prompt[b, 896:2048]
        xp[:] = x_prompt[b, 0:896]
    sl = slice(core * NB, (core + 1) * NB)
    mp = dict(
        xm=xm, xp=xp, xs=np.ascontiguousarray(x_sample[sl].reshape(NSAMP, D)),
        sret=np.ascontiguousarray(state_ret[0, sl]),
        sconv=np.ascontiguousarray(state_conv[0, sl].reshape(2 * NB, D)),
        sffn=np.ascontiguousarray(state_ffn[0, sl].reshape(2 * NB, DFF)),
        w_in=w_in[0], p_ret=p_ret[0], p_conv=p_conv[0], w_o=w_o[0], w_up=w_up[0], w_gate=w_gate[0], w_down=w_down[0],
        gcol=np.ascontiguousarray(np.stack([g_pre_mix[0].reshape(16, 128).T, g_pre_ffn[0].reshape(16, 128).T], axis=1)),
        gpost=np.ascontiguousarray(np.stack([np.broadcast_to(g_post_mix[0], (128, D)), np.broadcast_to(g_post_ffn[0], (128, D))])),
        cw=np.ascontiguousarray(conv_w[0].T.reshape(16, 128, 3).transpose(1, 0, 2)),
        fcw=np.ascontiguousarray(ffn_conv_w[0].T.reshape(NFF, 128, 3).transpose(1, 0, 2)),
        fcb=np.ascontiguousarray(ffn_conv_b[0].reshape(NFF, 128).T),
    )
    mp.update(_tables(core))
    return {k: np.ascontiguousarray(v, dtype=f32) for k, v in mp.items()}


_NC_CACHE = {}


def _run(inputs, cores):
    if "nc" not in _NC_CACHE:
        _NC_CACHE["nc"] = build_program()
    nc = _NC_CACHE["nc"]
    inputs = {k: np.asarray(v) for k, v in inputs.items()}
    in_maps = [_in_map(c, **inputs) for c in cores]
    res = run_bass_kernel_spmd(nc, in_maps, core_ids=list(range(len(cores))))
    return res.results


def kernel(**inputs):
    f32 = np.float32
    results = _run(inputs, list(range(NCORES)))
    B, S = 4, 2048
    y_prompt = np.zeros((B, S, D), f32)
    y_sample = np.zeros((128, 4, D), f32)
    ret_prompt = np.zeros((1, B, NH, DK, DV), f32)
    conv_prompt = np.zeros((1, B, 2, D), f32)
    ffn_prompt = np.zeros((1, B, 2, DFF), f32)
    ret_sample = np.zeros((1, 128, NH, DK, DV), f32)
    conv_sample = np.zeros((1, 128, 2, D), f32)
    ffn_sample = np.zeros((1, 128, 2, DFF), f32)
    for c, r in enumerate(results):
        b, half = c // 2, c % 2
        y_prompt[b, half * 1024:(half + 1) * 1024] = r["ym"][128:]
        sl = slice(c * NB, (c + 1) * NB)
        y_sample[sl] = r["ys"].reshape(NB, 4, D)
        ret_sample[0, sl] = r["rets"]
        conv_sample[0, sl] = r["convs"].reshape(NB, 2, D)
        ffn_sample[0, sl] = r["ffns"].reshape(NB, 2, DFF)
        if half == 1:
            ret_prompt[0, b] = r["retp"]
            conv_prompt[0, b] = r["convp"]
            ffn_prompt[0, b] = r["ffnp"]
    return (y_prompt, y_sample, ret_prompt, conv_prompt, ffn_prompt, ret_sample, conv_sample, ffn_sample)
```

```python
import math
import numpy as np
import concourse.bass as bass
import concourse.mybir as mybir
from concourse.bass_utils import run_bass_kernel_spmd

F32 = mybir.dt.float32
BF16 = mybir.dt.bfloat16
ALU = mybir.AluOpType
AF = mybir.ActivationFunctionType

D = 2048
NH = 8
DK = 256
DV = 512
DFF = 5632
NFF = DFF // 128
EPS = 1e-6
NCORES = 8
NHALO = 4
NPRE = 1024 - NHALO
NMAIN = 1024 + NHALO
NSAMP = 64
NB = 16
TOFF = 1024.0
LOGG = [math.log1p(-2.0 ** (-5 - h)) for h in range(NH)]
C_Q, C_K, C_V, C_G, C_GB, C_GC, C_HC, C_GA, C_GBT = 0, 2048, 4096, 8192, 12288, 14336, 16384, 18432, 20480

PASSES = [
    dict(src="xp", row0=0, rows=[128] * 7 + [124], kv=True, sample=False, out=None),
    dict(src="xm", row0=0, rows=[NHALO, 128, 128, 128, 128], kv=False, sample=False, out=[None, 0, 128, 256, 384]),
    dict(src="xm", row0=NHALO + 512, rows=[128] * 4, kv=False, sample=True, out=[512, 640, 768, 896]),
]
for _p in PASSES:
    _p["nch"] = len(_p["rows"])
    _p["r0"] = [sum(_p["rows"][:i]) for i in range(len(_p["rows"]))]


class _Op:
    __slots__ = ("eng", "fn", "deps", "signal", "ev", "dkey")


class Prog:
    def __init__(self, nc):
        self.nc = nc
        self.ops = []
        self.last_w = {}
        self.readers = {}
        self.dkeys = []

    def add(self, eng, fn, reads=(), writes=(), dkey=None):
        op = _Op()
        op.eng, op.fn, op.signal, op.ev, op.dkey = eng, fn, False, None, dkey
        deps = set()
        for r in reads:
            w = self.last_w.get(r)
            if w is not None:
                deps.add(w)
        for wr in writes:
            w = self.last_w.get(wr)
            if w is not None:
                deps.add(w)
            deps.update(self.readers.get(wr, ()))
        idx = len(self.ops)
        for r in reads:
            self.readers.setdefault(r, []).append(idx)
        for wr in writes:
            self.last_w[wr] = idx
            self.readers[wr] = []
        op.deps = deps
        self.ops.append(op)
        if dkey is not None and dkey not in self.dkeys:
            self.dkeys.append(dkey)
        return idx

    def pe(self, fn, reads=(), writes=()):
        return self.add("pe", fn, reads, writes)

    def act(self, fn, reads=(), writes=()):
        return self.add("act", fn, reads, writes)

    def dve(self, fn, reads=(), writes=()):
        return self.add("dve", fn, reads, writes)

    def pool(self, fn, reads=(), writes=()):
        return self.add("pool", fn, reads, writes)

    def dma(self, q, out, in_, key, reads=(), writes=(), accum=False):
        if accum:
            return self.add(q, lambda e: e.dma_start(out=out, in_=in_, accum_op=ALU.add), reads, writes, dkey=key)
        return self.add(q, lambda e: e.dma_start(out=out, in_=in_), reads, writes, dkey=key)

    def barrier(self):
        last = {}
        for i, op in enumerate(self.ops):
            if op.fn is not None:
                last[("e", op.eng) if op.dkey is None else ("d", op.dkey)] = i
        deps = set(last.values())
        for eng in ("pe", "act", "dve", "pool", "sp"):
            op = _Op()
            op.eng, op.fn, op.signal, op.ev, op.dkey = eng, None, False, None, None
            op.deps = set(deps)
            self.ops.append(op)
        self.last_w = {}
        self.readers = {}

    def emit(self):
        nc = self.nc
        engs = {"pe": nc.tensor, "act": nc.scalar, "dve": nc.vector, "pool": nc.gpsimd, "sp": nc.sync}
        for op in self.ops:
            for d in op.deps:
                if self.ops[d].dkey is None:
                    self.ops[d].signal = True
        esem = {e: nc.alloc_semaphore("es_" + e) for e in engs}
        dsem = {k: nc.alloc_semaphore("ds_%d" % i) for i, k in enumerate(self.dkeys)}
        cnt = {e: 0 for e in engs}
        dcnt = {k: 0 for k in self.dkeys}
        waited = {e: {} for e in engs}
        for op in self.ops:
            E = engs[op.eng]
            wl = {}
            for d in op.deps:
                p = self.ops[d]
                if p.fn is None:
                    continue
                if op.eng == "pe" and p.eng == "pe" and p.dkey is None:
                    continue
                name, sem, val = p.ev
                if wl.get(name, (None, 0))[1] < val:
                    wl[name] = (sem, val)
            for name, (sem, val) in wl.items():
                if waited[op.eng].get(name, 0) >= val:
                    continue
                E.wait_ge(sem, val)
                waited[op.eng][name] = val
            if op.fn is None:
                continue
            ins = op.fn(E)
            if op.dkey is not None:
                dcnt[op.dkey] += 16
                ins.then_inc(dsem[op.dkey], 16)
                op.ev = (("d", op.dkey), dsem[op.dkey], dcnt[op.dkey])
            elif op.signal:
                cnt[op.eng] += 1
                ins.then_inc(esem[op.eng], 1)
                op.ev = (("e", op.eng), esem[op.eng], cnt[op.eng])
        for k in self.dkeys:
            if dcnt[k]:
                nc.sync.wait_ge(dsem[k], dcnt[k])
        for e in engs:
            if cnt[e] and e != "sp":
                nc.sync.wait_ge(esem[e], cnt[e])


def build_program():
    nc = bass.Bass("TRN2", target_bir_lowering=False)
    P = Prog(nc)

    def din(name, shape):
        return nc.dram_tensor(name, list(shape), F32, kind="ExternalInput").ap()

    def dout(name, shape):
        return nc.dram_tensor(name, list(shape), F32, kind="ExternalOutput").ap()

    xm = din("xm", [NMAIN, D])
    xp = din("xp", [NPRE, D])
    xs = din("xs", [NSAMP, D])
    sret = din("sret", [NB, NH, DK, DV])
    sconv = din("sconv", [2 * NB, D])
    sffn = din("sffn", [2 * NB, DFF])
    w_in = din("w_in", [D, 22528])
    p_ret = din("p_ret", [4096, D])
    p_conv = din("p_conv", [D, D])
    w_o = din("w_o", [D, D])
    w_up = din("w_up", [D, DFF])
    w_gate = din("w_gate", [D, DFF])
    w_down = din("w_down", [DFF, D])
    gcol_d = din("gcol", [128, 2, 16])
    gpost_d = din("gpost", [2, 128, D])
    cw_d = din("cw", [128, 16, 3])
    fcw_d = din("fcw", [128, NFF, 3])
    fcb_d = din("fcb", [128, NFF])
    cq_d = din("cq", [2, NH, 128, 640])
    sq_d = din("sq", [2, NH, 128, 640])
    ck_d = din("ck", [3, 128, 8, 128])
    sk_d = din("sk", [3, 128, 8, 128])
    ksc_d = din("ksc", [128, 3, 8, NH])
    cmask_d = din("cmask", [128, 128])
    bmask_d = din("bmask", [64, 64])
    bm16_d = din("bm16", [128, NB, 64])
    bmT_d = din("bmT", [64, NB])
    idf_d = din("idf", [128, 128])

    ym = dout("ym", [1024, D])
    ys = dout("ys", [NSAMP, D])
    retp = dout("retp", [NH, DK, DV])
    convp = dout("convp", [2, D])
    ffnp = dout("ffnp", [2, DFF])
    rets = dout("rets", [NB, NH, DK, DV])
    convs = dout("convs", [2 * NB, D])
    ffns = dout("ffns", [2 * NB, DFF])

    xmid_d = nc.dram_tensor("xmid_scr", [6, 128, D], F32, kind="Internal").ap()
    sbase_d = nc.dram_tensor("sbase_scr", [NH, 128, 2 * DV], F32, kind="Internal").ap()

    def sb(name, shape, dt):
        return nc.alloc_sbuf_tensor("s_" + name, list(shape), dt)

    HT = sb("HT", [128, 16, 640], BF16)
    R1 = sb("R1", [128, 10240], F32)
    R2 = sb("R2", [128, 5120], F32)
    R4 = sb("R4", [128, 16384], F32)
    WB = [sb("WB%d" % i, [128, 16, 512], BF16) for i in range(3)]
    ident = sb("ident", [128, 128], BF16)
    identf = sb("identf", [128, 128], F32)
    cmask = sb("cmask", [128, 128], F32)
    bmask = sb("bmask", [64, 64], F32)
    bm16 = sb("bm16", [128, NB, 64], BF16)
    bmT = sb("bmT", [64, NB], F32)
    gcol = sb("gcol", [128, 2, 16], F32)
    cw = sb("cw", [128, 16, 3], F32)
    fcw = sb("fcw", [128, NFF, 3], F32)
    fcb = sb("fcb", [128, NFF], F32)
    ksc = sb("ksc", [128, 3, 8, NH], F32)
    halo_c = sb("halo_c", [128, 16, 2], F32)
    halo_f = sb("halo_f", [128, NFF, 2], F32)
    halo_cs = sb("halo_cs", [128, 16, NB, 2], F32)
    halo_fs = sb("halo_fs", [128, NFF, NB, 2], F32)
    st = sb("st", [128, 64], F32)
    otm = sb("otm", [32, 512], F32)

    def _view(reg, off_b, shape, bf):
        n = int(np.prod(shape))
        if bf:
            v = reg[:, off_b // 4:(off_b + 2 * n + 3) // 4].bitcast(BF16)
        else:
            v = reg[:, off_b // 4:off_b // 4 + n]
        if len(shape) == 1:
            return v
        names = " ".join("d%d" % i for i in range(len(shape)))
        kw = {"d%d" % i: s for i, s in enumerate(shape)}
        return v.rearrange("p (%s) -> p %s" % (names, names), **kw)

    def r_bf(reg, off_b, shape):
        return _view(reg, off_b, shape, True)

    def r_f32(reg, off_b, shape):
        return _view(reg, off_b, shape, False)

    goT = r_bf(R1, 0, [32, 640])
    XF = r_f32(R1, 0, [5, 2048])
    hT0 = r_bf(R1, 0, [16, 1024])
    gbcT = r_bf(R2, 0, [16, 640])
    Sf = r_f32(R2, 0, [4, 2, 512])
    Sbf = r_bf(R2, 16384, [2, 2, 512])
    GP = r_f32(R2, 0, [2048])
    mT = r_bf(R4, 0, [16, 640])
    aT = r_bf(R4, 0, [NFF, 640])
    junk = r_bf(R4, 40960, [2048])
    XT = [r_f32(R4, 49152, [2048]), r_f32(R4, 57344, [2048]), r_f32(R4, 24576, [2048]), r_f32(R4, 32768, [2048])]
    dumpA = r_bf(R4, 20480, [2048])
    XTh = [r_f32(R2, 0, [2048]), r_f32(R2, 8192, [2048])]
    junkh = r_bf(R2, 16384, [2048])
    off = {"o": 0}

    def tb(shape, bf):
        n = int(np.prod(shape)) * (2 if bf else 4)
        n = (n + 3) // 4 * 4
        v = _view(R4, off["o"], shape, bf)
        off["o"] += n
        return v

    qT = tb([2, 2, 640], True)
    kTM = tb([8, 2, 256], True)
    kT = tb([2, 2, 640], True)
    vS = tb([2, 512], True)
    sg = tb([5, 512], True)
    cqt = tb([576], False)
    sqt = tb([576], False)
    ckt = tb([8, 128], False)
    skt = tb([8, 128], False)
    Sb = tb([2, 2, 512], True)
    base = tb([2, 512], False)
    rt1 = tb([288], False)
    rt2 = tb([288], False)
    PT = tb([2, 128], True)
    gotm = tb([2, 512], True)
    qTs = tb([2, 2, 64], True)
    khat = tb([2, 256], True)
    vs = tb([2, 512], True)
    osum = tb([2, 512], False)
    sgs = tb([2, 512], True)
    khmb = tb([2, 256], True)
    rt3 = tb([288], False)
    rt4 = tb([288], False)
    junkB = tb([512], True)
    assert off["o"] <= 65536, off["o"]
    gcs = r_f32(R4, 20480, [4, 640])
    UPc = r_f32(R4, 20480 + 10240, [4, 740])
    cv = r_f32(R4, 20480 + 10240 + 11840, [4, 640])
    ya = r_f32(R4, 20480, [4, 640])
    yb = r_f32(R4, 20480 + 10240, [4, 640])
    sgm = r_f32(R4, 20480 + 20480, [4, 640])
    UPf = r_f32(R2, 0, [2, 740])
    gaB = r_f32(R2, 5920, [4, 640])

    PS = [nc.alloc_psum_tensor("ps%d" % i, [128, 512], F32) for i in range(8)]

    def psb(i):
        return PS[i][:, :].bitcast(BF16).rearrange("p (a b) -> p a b", b=128)

    def cload(dst, src, key, q="sp"):
        P.dma(q, dst, src, key, writes=[("c", key)])

    cload(ident[:], idf_d[:, :], "c_id", q="pool")
    cload(identf[:], idf_d[:, :], "c_idf")
    cload(cmask[:], cmask_d[:, :], "c_cm")
    cload(bmask[:], bmask_d[:, :], "c_bm")
    cload(bm16[:], bm16_d[:, :, :], "c_bm16", q="pool")
    cload(bmT[:], bmT_d[:, :], "c_bmT")
    cload(gcol[:], gcol_d[:, :, :], "c_gcol")
    cload(cw[:], cw_d[:, :, :], "c_cw")
    cload(fcw[:], fcw_d[:, :, :], "c_fcw")
    cload(fcb[:], fcb_d[:, :], "c_fcb")
    cload(ksc[:], ksc_d[:, :, :, :], "c_ksc")
    P.dve(lambda e: e.memset(halo_c[:], 0.0), writes=[("halo_c",)])
    P.dve(lambda e: e.memset(halo_f[:], 0.0), writes=[("halo_f",)])
    tmpc = r_f32(R1, 0, [2048])
    tmpf = r_f32(R1, 8192, [DFF])
    P.dma("sp", tmpc[:32, :], sconv[:, :], "c_tc", writes=[("tmpc",)])
    P.dma("sp", tmpf[:32, :], sffn[:, :], "c_tf", writes=[("tmpf",)])
    P.barrier()

    def halo_init(tmp, dst, nf):
        for g0 in range(0, nf, 16):
            n = min(16, nf - g0)
            bank = (g0 // 16) % 2

            def f(e, g0=g0, n=n, bank=bank):
                ins = None
                for j in range(n):
                    ins = e.transpose(PS[bank][:, j * 32:(j + 1) * 32], tmp[:32, (g0 + j) * 128:(g0 + j + 1) * 128], identf[:32, :32])
                return ins

            P.pe(f, writes=[("ps", bank)])
            P.act(lambda e, g0=g0, n=n, bank=bank: e.activation(
                out=dst[:, g0:g0 + n, :, :].rearrange("p f b t -> p f (b t)"),
                in_=PS[bank][:, 0:n * 32].rearrange("p (f x) -> p f x", x=32), func=AF.Copy),
                reads=[("ps", bank)], writes=[("halo_s", g0)])

    halo_init(tmpc, halo_cs, 16)
    halo_init(tmpf, halo_fs, NFF)
    P.barrier()

    wstate = {"n": 0}

    def wload(w2d, r0, c0, nkc=16, ncol=512):
        slot = wstate["n"] % 3
        wstate["n"] += 1
        src = w2d[r0:r0 + nkc * 128, c0:c0 + ncol].rearrange("(kc p) n -> p kc n", p=128)
        P.dma("pool", WB[slot][:, 0:nkc, 0:ncol], src, ("wb", slot), writes=[("WB", slot)])
        return slot

    class WStream:
        def __init__(self, blocks):
            self.blocks = blocks
            self.issued = []

        def get(self, i):
            while len(self.issued) < min(len(self.blocks), i + 3):
                b = self.blocks[len(self.issued)]
                self.issued.append(wload(*b))
            return self.issued[i]

    st_n = {"n": 0}

    def stcol():
        i = st_n["n"] % 64
        st_n["n"] += 1
        return i

    def rstd_col(src_ap, rows, n, reads, junk_ap, junk_key):
        col = stcol()
        P.act(lambda e: e.activation(out=junk_ap, in_=src_ap, func=AF.Square, accum_out=st[:rows, col:col + 1]),
              reads=list(reads), writes=[("st", col)] + ([junk_key] if junk_key is not None else []))
        P.act(lambda e: e.activation(out=st[:rows, col:col + 1], in_=st[:rows, col:col + 1], func=AF.Sqrt, bias=EPS, scale=1.0 / n),
              reads=[("st", col)], writes=[("st", col)])
        P.dve(lambda e: e.reciprocal(out=st[:rows, col:col + 1], in_=st[:rows, col:col + 1]), reads=[("st", col)], writes=[("st", col)])
        return col

    junks = [junk, r_bf(R4, 45056, [2048])]
    FG = dict(junks=junks, jkeys=[("junk", 0), ("junk", 1)], dump=dumpA)
    BGB = dict(junks=[junkh], jkeys=[("junkh",)], dump=None)

    def norm_stages(dst, gi, chunks, load_fn, bufs):
        out = []
        nj = len(bufs["junks"])
        for ci, (rows, col0) in enumerate(chunks):
            jb = bufs["junks"][ci % nj]
            jk = bufs["jkeys"][ci % nj]

            def s1(ci=ci, rows=rows, jb=jb, jk=jk):
                src, skey = load_fn(ci)
                if bufs["dump"] is not None:
                    col = rstd_col(src[:rows, :], rows, D, [skey], bufs["dump"][:rows, :], None)
                else:
                    col = rstd_col(src[:rows, :], rows, D, [skey], jb[:rows, :], jk)
                P.dve(lambda e: e.tensor_scalar(out=jb[:rows, :], in0=src[:rows, :], scalar1=st[:rows, col:col + 1], scalar2=None,
                                                op0=ALU.mult),
                      reads=[skey, ("st", col)], writes=[jk])

            def s2(ci=ci, rows=rows, col0=col0, jb=jb, jk=jk):
                for g4 in range(4):
                    bank = 6 + (g4 % 2)
                    pv = psb(bank)

                    def tr(e, g4=g4, pv=pv):
                        ins = None
                        for j in range(4):
                            kc = g4 * 4 + j
                            ins = e.transpose(pv[:, j, 0:rows], jb[:rows, kc * 128:(kc + 1) * 128], ident[:rows, :rows])
                        return ins

                    P.pe(tr, reads=[jk], writes=[("ps", bank)])
                    P.dve(lambda e, g4=g4, pv=pv: e.tensor_tensor(
                        out=dst[:, g4 * 4:(g4 + 1) * 4, col0:col0 + rows], in0=pv[:, 0:4, 0:rows],
                        in1=gcol[:, gi, g4 * 4:(g4 + 1) * 4].unsqueeze(2).to_broadcast([128, 4, rows]), op=ALU.mult),
                        reads=[("ps", bank)], writes=[("hT", g4, ci)])

            out.append((s1, s2))
        return out

    def norm_transpose(dst, gi, chunks, load_fn):
        stg = norm_stages(dst, gi, chunks, load_fn, FG)
        stg[0][0]()
        for ci in range(len(stg)):
            if ci + 1 < len(stg):
                stg[ci + 1][0]()
            stg[ci][1]()

    bgA = []

    pa = {"n": 0}

    def pumpA(n=1, every=1):
        pa["n"] += 1
        if pa["n"] % every != 0:
            return
        for _ in range(n):
            if bgA:
                bgA.pop(0)()

    def hT_reads(nchunks):
        return [("hT", g4, ci) for g4 in range(4) for ci in range(nchunks)]

    def proj_fm(ws, bi, fc, hsrc, hreads, col0, ncols, bank, nkc=16, kc0=0, mid=None):
        slot = ws.get(bi)
        parts = [(0, nkc)] if mid is None else [(0, nkc // 2), (nkc // 2, nkc)]
        for pi_, (ka, kb) in enumerate(parts):
            def f(e, ka=ka, kb=kb):
                ins = None
                for kc in range(ka, kb):
                    ins = e.matmul(PS[bank][:, 0:ncols], lhsT=WB[slot][:, kc, fc * 128:(fc + 1) * 128],
                                   rhs=hsrc[:, kc0 + kc, col0:col0 + ncols], start=(kc == 0), stop=(kc == nkc - 1), skip_group_check=True)
                return ins

            P.pe(f, reads=[("WB", slot)] + list(hreads), writes=[("ps", bank)])
            if mid is not None and pi_ == 0:
                mid()

    def proj_tm(ws, bi, hsrc, hreads, col0, rows, bank, nkc=16, kc0=0, start=True, stop=True, mid=None):
        slot = ws.get(bi)
        parts = [(0, nkc)] if mid is None else [(0, nkc // 2), (nkc // 2, nkc)]
        for pi_, (ka, kb) in enumerate(parts):
            def f(e, ka=ka, kb=kb):
                ins = None
                for kc in range(ka, kb):
                    ins = e.matmul(PS[bank][:rows, :], lhsT=hsrc[:, kc0 + kc, col0:col0 + rows],
                                   rhs=WB[slot][:, kc, :], start=(start and kc == 0), stop=(stop and kc == nkc - 1),
                                   skip_group_check=True)
                return ins

            P.pe(f, reads=[("WB", slot)] + list(hreads), writes=[("ps", bank)])
            if mid is not None and pi_ == 0:
                mid()

    ep_n = {"n": 0}
    late = []
    bg = []

    def flush_late(keep=0):
        while len(late) > keep:
            late.pop(0)()

    bgn = {"n": 0}

    def pump(n, banks=(7,)):
        for _ in range(n):
            if bg:
                bk = banks[bgn["n"] % len(banks)]
                bgn["n"] += 1
                bg.pop(0)(bk)

    def retention_epilogue(h, rows, col0, o_ap, o_key, sg_ap, sg_key):
        col = rstd_col(o_ap, rows, DV, [o_key], junkB[:rows, :], ("junkB",))
        gs = ep_n["n"] % 2
        ep_n["n"] += 1
        P.dve(lambda e: e.scalar_tensor_tensor(out=gotm[:rows, gs, :], in0=o_ap, scalar=st[:rows, col:col + 1],
                                               in1=sg_ap, op0=ALU.mult, op1=ALU.mult),
              reads=[o_key, ("st", col), sg_key], writes=[("gotm", gs)])

        def part2():
            pv = psb(6)

            def tr(e):
                ins = None
                for ec in range(4):
                    ins = e.transpose(pv[:, ec, 0:rows], gotm[:rows, gs, ec * 128:(ec + 1) * 128], ident[:rows, :rows])
                return ins

            P.pe(tr, reads=[("gotm", gs)], writes=[("ps", 6)])
            P.act(lambda e: e.activation(out=goT[:, h * 4:(h + 1) * 4, col0:col0 + rows], in_=pv[:, 0:4, 0:rows], func=AF.Copy),
                  reads=[("ps", 6)], writes=[("goT", h, col0)])

        late.append(part2)

    def sample_retention(h, hh, ci, TG, cx):
        g4 = math.exp(4.0 * LOGG[h])
        while bg:
            pump(1)
            flush_late()

        def sc_mm(e):
            ins = None
            for half in range(2):
                ins = e.matmul(PS[4][:64, 0:64], lhsT=kT[:, hh, half, TG:TG + 64], rhs=qT[:, hh, half, TG:TG + 64],
                               start=(half == 0), stop=(half == 1))
            return ins

        P.pe(sc_mm, reads=[("kT", hh, ci), ("qT", hh)], writes=[("ps", 4)])
        P.dve(lambda e: e.tensor_tensor(out=PT[:64, 0, 0:64], in0=PS[4][:64, 0:64], in1=bmask[:, :], op=ALU.mult),
              reads=[("ps", 4)], writes=[("PT", 0)])
        P.act(lambda e: e.mul(out=khat[:64, cx, :], in_=kTM[:64, ci, hh, :], mul=g4), reads=[("kTM", ci, hh)], writes=[("khat", cx)])
        P.act(lambda e: e.activation(out=qTs[:, cx], in_=qT[:, hh, :, TG:TG + 64], func=AF.Copy), reads=[("qT", hh)], writes=[("qTs", cx)])
        P.pe(lambda e: e.matmul(PS[7][:64, :], lhsT=PT[:64, 0, 0:64], rhs=vs[:64, cx, :], start=True, stop=True),
             reads=[("PT", 0), ("vs", cx)], writes=[("ps", 7)])
        P.act(lambda e: e.activation(out=osum[:64, cx, :], in_=PS[7][:64, :], func=AF.Copy), reads=[("ps", 7)], writes=[("osum", cx)])

        def load(b):
            P.dma("sp", Sf[:, b % 4], sret[b, h].rearrange("(a p) e -> p a e", p=128), ("sfi", b % 4), writes=[("Sf", b % 4)])

        for b in range(3):
            load(b)

        def cast(b):
            P.act(lambda e: e.activation(out=Sbf[:, b % 2], in_=Sf[:, b % 4], func=AF.Copy), reads=[("Sf", b % 4)], writes=[("Sbf", b % 2)])

        cast(0)

        def mk_cross(b):
            def f(bk):
                s4, s2 = b % 4, b % 2
                P.act(lambda e: e.mul(out=khmb[:64, s2, :], in_=khat[:64, cx, :], mul=bmT[:, b:b + 1]),
                      reads=[("khat", cx)], writes=[("khmb", s2)])

                def mm(e):
                    ins = None
                    for half in range(2):
                        ins = e.matmul(PS[bk][:64, :], lhsT=qTs[:, cx, half, :], rhs=Sbf[:, s2, half, :], start=(half == 0), stop=(half == 1))
                    return ins

                P.pe(mm, reads=[("qTs", cx), ("Sbf", s2)], writes=[("ps", bk)])
                P.dve(lambda e: e.scalar_tensor_tensor(out=osum[:64, cx, :], in0=PS[bk][:64, :], scalar=bmT[:, b:b + 1], in1=osum[:64, cx, :],
                                                       op0=ALU.mult, op1=ALU.add),
                      reads=[("ps", bk), ("osum", cx)], writes=[("osum", cx)])
                if b + 1 < NB:
                    cast(b + 1)
            return f

        def mk_state(b, half):
            def f(bk):
                s4, s2 = b % 4, b % 2
                P.pe(lambda e: e.matmul(PS[bk][:, :], lhsT=khmb[:64, s2, half * 128:(half + 1) * 128], rhs=vs[:64, cx, :], start=True, stop=True),
                     reads=[("khmb", s2), ("vs", cx)], writes=[("ps", bk)])
                P.dve(lambda e: e.scalar_tensor_tensor(out=Sf[:, s4, half, :], in0=Sf[:, s4, half, :], scalar=g4, in1=PS[bk][:, :],
                                                       op0=ALU.mult, op1=ALU.add),
                      reads=[("ps", bk), ("Sf", s4)], writes=[("Sf", s4)])
                if half == 1:
                    P.dma("sp", rets[b, h].rearrange("(a p) e -> p a e", p=128), Sf[:, s4], ("sfo", s4), reads=[("Sf", s4)])
                    if b + 3 < NB:
                        load(b + 3)
            return f

        for b in range(NB):
            bg.append(mk_cross(b))
            bg.append(mk_state(b, 0))
            bg.append(mk_state(b, 1))

        def fin(bk):
            retention_epilogue(h, 64, TG, osum[:64, cx, :], ("osum", cx), sgs[:64, cx, :], ("sgs", cx))
        bg.append(fin)

    ostage = r_f32(R4, 0, [DFF])
    on = {"n": 0}

    def out_rows(src_fn, nf, rows, dst):
        k = on["n"]
        on["n"] += 1
        stg = ostage if k % 2 == 0 else r_f32(R2, 8192, [2048])
        for g0 in range(0, nf, 4):
            bank = (g0 // 4) % 4

            def f(e, g0=g0, bank=bank):
                ins = None
                for j in range(4):
                    ins = e.transpose(PS[bank][:rows, j * 128:(j + 1) * 128], src_fn(g0 + j), identf[:, :])
                return ins

            P.pe(f, writes=[("ps", bank)])
            P.act(lambda e, bank=bank, g0=g0, stg=stg: e.activation(out=stg[:rows, g0 * 128:(g0 + 4) * 128], in_=PS[bank][:rows, :], func=AF.Copy),
                  reads=[("ps", bank)], writes=[("ostg", k % 2)])
        P.dma("sp", dst[:, :], stg[:rows, 0:nf * 128], ("ostg_o", k % 2), reads=[("ostg", k % 2)])

    def enqueue_phaseA(p):
        ps_ = PASSES[p]
        nch_ = ps_["nch"]
        src_ = xm if ps_["src"] == "xm" else xp
        chunks_ = [(ps_["rows"][c], ps_["r0"][c]) for c in range(nch_)] + ([(NSAMP, sum(ps_["rows"]))] if ps_["sample"] else [])

        def load(ci):
            slot = ci % 2
            rows = chunks_[ci][0]
            srcrows = src_[ps_["row0"] + ps_["r0"][ci]: ps_["row0"] + ps_["r0"][ci] + ps_["rows"][ci], :] if ci < nch_ else xs[:, :]
            P.dma("sp", XTh[slot][:rows, :], srcrows, ("xth", slot), writes=[("XTh", slot)])
            return XTh[slot], ("XTh", slot)

        for (s1, s2) in norm_stages(HT, 0, chunks_, load, BGB):
            bgA.append(s1)
            bgA.append(s2)

    blocks = []
    for ps_ in PASSES:
        kv_ = ps_["kv"]
        for hp in range(4):
            if not kv_:
                blocks.append((w_in, 0, C_Q + 512 * hp))
            blocks.append((w_in, 0, C_K + 512 * hp))
            for hh in range(2):
                h = 2 * hp + hh
                if not kv_:
                    blocks.append((w_in, 0, C_G + 512 * h))
                blocks.append((w_in, 0, C_V + 512 * h))
        if not kv_:
            for fg in range(4):
                blocks += [(w_in, 0, C_GC + 512 * fg), (w_in, 0, C_HC + 512 * fg), (w_in, 0, C_GB + 512 * fg)]
            for fg in range(4):
                blocks += [(p_ret, 0, 512 * fg), (p_ret, 2048, 512 * fg), (w_in, 0, C_GA + 512 * fg), (p_conv, 0, 512 * fg),
                           (w_in, 0, C_GBT + 512 * fg)]
            for nb in range(4):
                blocks += [(w_o, 0, 512 * nb)]
            for fg in range(11):
                blocks += [(w_up, 0, 512 * fg), (w_gate, 0, 512 * fg)]
            for nb in range(4):
                for sub in range(3):
                    blocks += [(w_down, sub * 2048, 512 * nb, 16 if sub < 2 else 12)]
    ws = WStream(blocks)
    bi = 0
    ws.get(0)

    for pi, ps in enumerate(PASSES):
        nch, kv, has_s = ps["nch"], ps["kv"], ps["sample"]
        TG = sum(ps["rows"])
        TT = TG + (NSAMP if has_s else 0)
        src = xm if ps["src"] == "xm" else xp
        hT = hT0 if kv else HT
        chunks = [(ps["rows"][c], ps["r0"][c]) for c in range(nch)] + ([(NSAMP, TG)] if has_s else [])
        nchk = len(chunks)
        tiles = []
        ntl = (TT + 511) // 512
        tw = ((TT + ntl - 1) // ntl + 31) // 32 * 32
        c0 = 0
        while c0 < TT:
            n = min(tw, TT - c0)
            segs = []
            if c0 < TG:
                segs.append((0, min(n, TG - c0), False, c0))
            if c0 + n > TG:
                a = max(c0, TG)
                segs.append((a - c0, c0 + n - a, True, a))
            tiles.append((c0, n, segs))
            c0 += n

        def xrows(ci, src=src, ps=ps, nch=nch):
            if ci < nch:
                return src[ps["row0"] + ps["r0"][ci]: ps["row0"] + ps["r0"][ci] + ps["rows"][ci], :]
            return xs[:, :]

        def load_x(ci, chunks=chunks, xrows=xrows):
            slot = ci % 4
            rows = chunks[ci][0]
            P.dma("sp", XT[slot][:rows, :], xrows(ci), ("xt", slot), writes=[("XT", slot)])
            return XT[slot], ("XT", slot)

        if pi == 0:
            norm_transpose(hT, 0, chunks, load_x)
            P.barrier()
            enqueue_phaseA(1)
        HR = []
        P.dma("sp", ckt[:, 0:8, :], ck_d[pi], "ckt", writes=[("ckt",)])
        P.dma("sp", skt[:, 0:8, :], sk_d[pi], "skt", writes=[("skt",)])
        NPUMP = 4

        for hp in range(4):
            if not kv:
                qn = 0
                for hh in range(2):
                    h = 2 * hp + hh
                    P.dma("sp", cqt[:, 0:576], cq_d[pi - 1, h][:, 0:576], "cqt", writes=[("cqt",)])
                    P.dma("sp", sqt[:, 0:576], sq_d[pi - 1, h][:, 0:576], "sqt", writes=[("sqt",)])
                    for ti, (t0, tn, _s) in enumerate(tiles):
                        b0_, b1_ = (0, 1) if qn % 2 == 0 else (2, 3)
                        ta, tb2 = (rt1, rt2) if qn % 2 == 0 else (rt3, rt4)
                        qn += 1
                        proj_fm(ws, bi, 2 * hh, hT, HR, t0, tn, b0_, mid=lambda: pump(1, (7, 6, 4, 5)))
                        pump(1, (7, 6, 4, 5))
                        proj_fm(ws, bi, 2 * hh + 1, hT, HR, t0, tn, b1_, mid=lambda: pump(1, (7, 6, 4, 5)))
                        flush_late()
                        pump(1, (7, 6, 4, 5))
                        x1, x2 = PS[b0_][:, 0:tn], PS[b1_][:, 0:tn]
                        cs, sn = cqt[:, t0:t0 + tn], sqt[:, t0:t0 + tn]
                        ka, kb = ("rt", id(ta)), ("rt", id(tb2))
                        P.dve(lambda e, x1=x1, cs=cs, tn=tn, ta=ta: e.tensor_tensor(out=ta[:, 0:tn], in0=x1, in1=cs, op=ALU.mult),
                              reads=[("ps", b0_), ("cqt",)], writes=[ka])
                        P.dve(lambda e, x2=x2, sn=sn, tn=tn, tb2=tb2: e.tensor_tensor(out=tb2[:, 0:tn], in0=x2, in1=sn, op=ALU.mult),
                              reads=[("ps", b1_), ("sqt",)], writes=[kb])
                        P.pool(lambda e, hh=hh, t0=t0, tn=tn, ta=ta, tb2=tb2: e.tensor_tensor(out=qT[:, hh, 0, t0:t0 + tn], in0=ta[:, 0:tn],
                                                                                           in1=tb2[:, 0:tn], op=ALU.subtract),
                               reads=[ka, kb], writes=[("qT", hh)])
                        P.dve(lambda e, x2=x2, cs=cs, tn=tn, ta=ta: e.tensor_tensor(out=ta[:, 0:tn], in0=x2, in1=cs, op=ALU.mult),
                              reads=[("ps", b1_), ("cqt",)], writes=[ka])
                        P.dve(lambda e, x1=x1, sn=sn, tn=tn, tb2=tb2: e.tensor_tensor(out=tb2[:, 0:tn], in0=x1, in1=sn, op=ALU.mult),
                              reads=[("ps", b0_), ("sqt",)], writes=[kb])
                        P.pool(lambda e, hh=hh, t0=t0, tn=tn, ta=ta, tb2=tb2: e.tensor_tensor(out=qT[:, hh, 1, t0:t0 + tn], in0=ta[:, 0:tn],
                                                                                           in1=tb2[:, 0:tn], op=ALU.add),
                               reads=[ka, kb], writes=[("qT", hh)])
                bi += 1
            for ci, (rows, col0) in enumerate(chunks):
                bank = 2 + (ci % 2)
                proj_tm(ws, bi, hT, HR, col0, rows, bank, mid=lambda: pump(1, (7, 6, 4, 5)))
                flush_late()
                pump(1, (7, 6, 4, 5))
                if kv:
                    pumpA(1, every=4)
                for hh in range(2):
                    h = 2 * hp + hh
                    x1 = PS[bank][:rows, hh * 256: hh * 256 + 128]
                    x2 = PS[bank][:rows, hh * 256 + 128: hh * 256 + 256]
                    cs, sn = ckt[:rows, ci, :], skt[:rows, ci, :]
                    sc = ksc[:rows, pi, ci, h:h + 1]
                    rk = [("ps", bank), ("ckt",), ("skt",)]
                    ta, tb2 = (rt1, rt2) if hh == 0 else (rt3, rt4)
                    ka, kb = ("rt", id(ta)), ("rt", id(tb2))

                    def stt(e, o_, a_, b_, sc=sc):
                        return e.scalar_tensor_tensor(out=o_, in0=a_, scalar=sc, in1=b_, op0=ALU.mult, op1=ALU.mult)

                    P.dve(lambda e, x1=x1, cs=cs, rows=rows, stt=stt, ta=ta: stt(e, ta[:rows, 0:128], x1, cs), reads=rk, writes=[ka])
                    P.dve(lambda e, x2=x2, sn=sn, rows=rows, stt=stt, tb2=tb2: stt(e, tb2[:rows, 0:128], x2, sn), reads=rk, writes=[kb])
                    P.dve(lambda e, x2=x2, cs=cs, rows=rows, stt=stt, ta=ta: stt(e, ta[:rows, 128:256], x2, cs), reads=rk, writes=[ka])
                    P.dve(lambda e, x1=x1, sn=sn, rows=rows, stt=stt, tb2=tb2: stt(e, tb2[:rows, 128:256], x1, sn), reads=rk, writes=[kb])
                    P.pool(lambda e, rows=rows, ci=ci, hh=hh, ta=ta, tb2=tb2: e.tensor_tensor(out=kTM[:rows, ci, hh, 0:128], in0=ta[:rows, 0:128],
                                                                                           in1=tb2[:rows, 0:128], op=ALU.subtract),
                           reads=[ka, kb], writes=[("kTM", ci, hh)])
                    P.pool(lambda e, rows=rows, ci=ci, hh=hh, ta=ta, tb2=tb2: e.tensor_tensor(out=kTM[:rows, ci, hh, 128:256], in0=ta[:rows, 128:256],
                                                                                           in1=tb2[:rows, 128:256], op=ALU.add),
                           reads=[ka, kb], writes=[("kTM", ci, hh)])
                    if not kv:
                        def part2(rows=rows, ci=ci, hh=hh, col0=col0):
                            pv = psb(6)

                            def trk(e):
                                ins = None
                                for half in range(2):
                                    ins = e.transpose(pv[:, half, 0:rows], kTM[:rows, ci, hh, half * 128:(half + 1) * 128], ident[:rows, :rows])
                                return ins

                            P.pe(trk, reads=[("kTM", ci, hh)], writes=[("ps", 6)])
                            P.act(lambda e: e.activation(out=kT[:, hh, :, col0:col0 + rows], in_=pv[:, 0:2, 0:rows], func=AF.Copy),
                                  reads=[("ps", 6)], writes=[("kT", hh, ci)])

                        late.append(part2)
            bi += 1
            for hh in range(2):
                h = 2 * hp + hh
                cx = h % 2
                if not kv:
                    for ci, (rows, col0) in enumerate(chunks):
                        bank = 2 + (ci % 2)
                        proj_tm(ws, bi, hT, HR, col0, rows, bank, mid=lambda: pump(2, (7, 6, 4, 5, 0, 1)))
                        flush_late()
                        pump(2, (7, 6, 4, 5, 0, 1))
                        if ci < nch:
                            P.act(lambda e, rows=rows, ci=ci, bank=bank: e.activation(out=sg[:rows, ci, :], in_=PS[bank][:rows, :], func=AF.Silu),
                                  reads=[("ps", bank)], writes=[("sg", ci)])
                        else:
                            P.act(lambda e, rows=rows, cx=cx, bank=bank: e.activation(out=sgs[:rows, cx, :], in_=PS[bank][:rows, :], func=AF.Silu),
                                  reads=[("ps", bank)], writes=[("sgs", cx)])
                    bi += 1
                if pi == 0:
                    P.dve(lambda e: e.memset(base[:], 0.0), writes=[("base",)])
                else:
                    P.dma("sp", base[:].rearrange("p a b -> p (a b)"), sbase_d[h], "base", writes=[("base",)])
                if not kv:
                    P.act(lambda e: e.activation(out=Sb[:, 0], in_=base[:], func=AF.Copy), reads=[("base",)], writes=[("Sb", 0)])
                sbs = 0

                def v_proj(ci, hh=hh, cx=cx):
                    rows, col0 = chunks[ci]
                    bank = 2 + (ci % 2)
                    proj_tm(ws, bi, hT, HR, col0, rows, bank, mid=lambda: pump(1))
                    if ci >= nch:
                        P.act(lambda e: e.activation(out=vs[:rows, cx, :], in_=PS[bank][:rows, :], func=AF.Copy),
                              reads=[("ps", bank)], writes=[("vs", cx)])
                    else:
                        P.act(lambda e: e.activation(out=vS[:rows, ci % 2, :], in_=PS[bank][:rows, :], func=AF.Copy),
                              reads=[("ps", bank)], writes=[("vS", ci % 2)])

                def scores(ci, hh=hh):
                    rw, col0 = chunks[ci]
                    pslot = ci % 2

                    def sc_mm(e):
                        ins = None
                        for half in range(2):
                            ins = e.matmul(PS[4][:rw, 0:rw], lhsT=kT[:, hh, half, col0:col0 + rw], rhs=qT[:, hh, half, col0:col0 + rw],
                                           start=(half == 0), stop=(half == 1))
                        return ins

                    P.pe(sc_mm, reads=[("kT", hh, ci), ("qT", hh)], writes=[("ps", 4)])
                    P.dve(lambda e: e.tensor_tensor(out=PT[:rw, pslot, 0:rw], in0=PS[4][:rw, 0:rw], in1=cmask[:rw, :rw], op=ALU.mult),
                          reads=[("ps", 4)], writes=[("PT", pslot)])

                v_proj(0)
                if not kv:
                    scores(0)
                for ci in range(nch):
                    rows, col0 = chunks[ci]
                    vslot = ci % 2
                    if ci + 1 < nchk:
                        v_proj(ci + 1)
                    flush_late(1)
                    pump(1)
                    if kv:
                        pumpA(1, every=4)
                    if not kv:
                        def o_mm(e, hh=hh, col0=col0, vslot=vslot, sbs=sbs, ci=ci, rows=rows):
                            e.matmul(PS[5][:rows, :], lhsT=PT[:rows, ci % 2, 0:rows], rhs=vS[:rows, vslot, :], start=True, stop=False)
                            ins = None
                            for half in range(2):
                                ins = e.matmul(PS[5][:rows, :], lhsT=qT[:, hh, half, col0:col0 + rows], rhs=Sb[:, sbs, half, :],
                                               start=False, stop=(half == 1))
                            return ins

                        P.pe(o_mm, reads=[("PT", ci % 2), ("vS", vslot), ("qT", hh), ("Sb", sbs)], writes=[("ps", 5)])

                    def s_mm(e, ci=ci, hh=hh, vslot=vslot, rows=rows):
                        ins = None
                        for half in range(2):
                            ins = e.matmul(PS[half][:, :], lhsT=kTM[:rows, ci, hh, half * 128:(half + 1) * 128], rhs=vS[:rows, vslot, :],
                                           start=(ci == 0), stop=True, skip_group_check=True)
                        return ins

                    P.pe(s_mm, reads=[("kTM", ci, hh), ("vS", vslot)], writes=[("ps", 0), ("ps", 1)])
                    pump(1)
                    if not kv and ci + 1 < nch:
                        scores(ci + 1)
                    last_prompt = (ci == nch - 1)
                    if not kv and not last_prompt:
                        nsb = 1 - sbs
                        for half in range(2):
                            P.dve(lambda e, half=half, nsb=nsb: e.tensor_tensor(out=Sb[:, nsb, half, :], in0=PS[half][:, :], in1=base[:, half, :],
                                                                                 op=ALU.add),
                                  reads=[("ps", half), ("base",)], writes=[("Sb", nsb)])
                        sbs = nsb
                    if last_prompt:
                        for half in range(2):
                            P.dve(lambda e, half=half: e.tensor_tensor(out=base[:, half, :], in0=PS[half][:, :], in1=base[:, half, :], op=ALU.add),
                                  reads=[("ps", half), ("base",)], writes=[("base",)])
                        if pi < 2:
                            P.dma("sp", sbase_d[h], base[:].rearrange("p a b -> p (a b)"), "base_o", reads=[("base",)])
                        else:
                            fs = math.exp(LOGG[h] * (2047.0 - TOFF))
                            P.act(lambda e, fs=fs: e.mul(out=base[:], in_=base[:], mul=fs), reads=[("base",)],
                                  writes=[("base",)])
                            P.dma("sp", retp[h].rearrange("(a p) e -> p a e", p=128), base[:], "base_o", reads=[("base",)])
                    if not kv:
                        retention_epilogue(h, rows, col0, PS[5][:rows, :], ("ps", 5), sg[:rows, ci, :], ("sg", ci))
                if has_s:
                    sample_retention(h, hh, nch, TG, cx)
                bi += 1
        flush_late()
        while bg:
            pump(1)
            flush_late()
        pumpA(1000, every=1)
        P.barrier()
        if kv:
            continue

        def conv3(dstP, dstS, up, upkey, dkey, f, wt, bias, TG=TG, has_s=has_s):
            upP = up[:, 0:2 + TG]
            segs = [(dstP, lambda k, upP=upP, TG=TG: upP[:, k:k + TG])]
            if has_s:
                upS = up[:, 642:738].rearrange("p (b t) -> p b t", t=6)
                segs.append((dstS, lambda k, upS=upS: upS[:, :, k:k + 4]))
            for (dd, sl) in segs:
                if bias is None:
                    P.dve(lambda e, dd=dd, sl=sl: e.tensor_scalar(out=dd, in0=sl(0), scalar1=wt[:, f, 0:1], scalar2=None, op0=ALU.mult),
                           reads=[upkey], writes=[dkey])
                else:
                    P.dve(lambda e, dd=dd, sl=sl: e.tensor_scalar(out=dd, in0=sl(0), scalar1=wt[:, f, 0:1], scalar2=bias, op0=ALU.mult,
                                                                   op1=ALU.add),
                           reads=[upkey], writes=[dkey])
                for k in (1, 2):
                    P.dve(lambda e, dd=dd, sl=sl, k=k: e.scalar_tensor_tensor(out=dd, in0=sl(k), scalar=wt[:, f, k:k + 1], in1=dd,
                                                                               op0=ALU.mult, op1=ALU.add),
                           reads=[upkey, dkey], writes=[dkey])

        for fg in range(4):
            for fc in range(4):
                for ti, (t0, tn, _s) in enumerate(tiles):
                    bank = (fc * 2 + ti) % 4
                    proj_fm(ws, bi, fc, hT, HR, t0, tn, bank)
                    P.act(lambda e, fc=fc, t0=t0, tn=tn, bank=bank: e.activation(out=gcs[:, fc, t0:t0 + tn], in_=PS[bank][:, 0:tn], func=AF.Copy),
                          reads=[("ps", bank)], writes=[("gcs", fc)])
            bi += 1
            for fc in range(4):
                f = fg * 4 + fc
                up = UPc[:, fc, :]
                upkey = ("upc", fc)
                P.dve(lambda e, up=up, f=f: e.tensor_copy(out=up[:, 0:2], in_=halo_c[:, f, :]), reads=[("halo_c", f)], writes=[upkey])
                if has_s:
                    P.dve(lambda e, up=up, f=f: e.tensor_copy(out=up[:, 642:738].rearrange("p (b t) -> p b t", t=6)[:, :, 0:2],
                                                              in_=halo_cs[:, f, :, :]), reads=[("halo_cs", f)], writes=[upkey])
                for ti, (t0, tn, segs) in enumerate(tiles):
                    bank = (fc * 2 + ti) % 4
                    proj_fm(ws, bi, fc, hT, HR, t0, tn, bank)
                    for (so, sn, is_s, sc0) in segs:
                        if not is_s:
                            P.dve(lambda e, up=up, fc=fc, so=so, sn=sn, sc0=sc0, bank=bank: e.tensor_tensor(
                                out=up[:, 2 + sc0:2 + sc0 + sn], in0=PS[bank][:, so:so + sn], in1=gcs[:, fc, sc0:sc0 + sn], op=ALU.mult),
                                reads=[("ps", bank), ("gcs", fc)], writes=[upkey])
                        else:
                            P.dve(lambda e, up=up, fc=fc, so=so, sn=sn, sc0=sc0, bank=bank: e.tensor_tensor(
                                out=up[:, 642:738].rearrange("p (b t) -> p b t", t=6)[:, :, 2:6],
                                in0=PS[bank][:, so:so + sn].rearrange("p (b t) -> p b t", t=4),
                                in1=gcs[:, fc, sc0:sc0 + sn].rearrange("p (b t) -> p b t", t=4), op=ALU.mult),
                                reads=[("ps", bank), ("gcs", fc)], writes=[upkey])
                P.dve(lambda e, up=up, f=f, TG=TG: e.tensor_copy(out=halo_c[:, f, :], in_=up[:, TG:TG + 2]), reads=[upkey],
                      writes=[("halo_c", f)])
                if has_s:
                    P.dve(lambda e, up=up, f=f: e.tensor_copy(out=halo_cs[:, f, :, :],
                                                              in_=up[:, 642:738].rearrange("p (b t) -> p b t", t=6)[:, :, 4:6]),
                          reads=[upkey], writes=[("halo_cs", f)])
                conv3(cv[:, fc, 0:TG], cv[:, fc, TG:TG + 64].rearrange("p (b t) -> p b t", t=4) if has_s else None, up, upkey, ("cv", fc), f, cw, None)
            bi += 1
            for fc in range(4):
                f = fg * 4 + fc
                for ti, (t0, tn, _s) in enumerate(tiles):
                    bank = (fc * 2 + ti) % 4
                    proj_fm(ws, bi, fc, hT, HR, t0, tn, bank)
                    P.dve(lambda e, f=f, fc=fc, t0=t0, tn=tn, bank=bank: e.tensor_tensor(
                        out=gbcT[:, f, t0:t0 + tn], in0=PS[bank][:, 0:tn], in1=cv[:, fc, t0:t0 + tn], op=ALU.mult),
                        reads=[("ps", bank), ("cv", fc)], writes=[("gbcT", f)])
            bi += 1
        P.barrier()

        for fg in range(4):
            for (kc0, first) in ((0, True), (16, False)):
                for fc in range(4):
                    for ti, (t0, tn, _s) in enumerate(tiles):
                        bank = (fc * 2 + ti) % 4
                        proj_fm(ws, bi, fc, goT, [], t0, tn, bank, kc0=kc0)
                        if first:
                            P.act(lambda e, fc=fc, t0=t0, tn=tn, bank=bank: e.activation(out=ya[:, fc, t0:t0 + tn], in_=PS[bank][:, 0:tn],
                                                                                        func=AF.Copy),
                                  reads=[("ps", bank)], writes=[("ya", fc)])
                        else:
                            P.dve(lambda e, fc=fc, t0=t0, tn=tn, bank=bank: e.tensor_tensor(out=ya[:, fc, t0:t0 + tn], in0=PS[bank][:, 0:tn],
                                                                                            in1=ya[:, fc, t0:t0 + tn], op=ALU.add),
                                  reads=[("ps", bank), ("ya", fc)], writes=[("ya", fc)])
                bi += 1
            for fc in range(4):
                for ti, (t0, tn, _s) in enumerate(tiles):
                    bank = (fc * 2 + ti) % 4
                    proj_fm(ws, bi, fc, hT, [], t0, tn, bank)
                    P.act(lambda e, fc=fc, t0=t0, tn=tn, bank=bank: e.activation(out=sgm[:, fc, t0:t0 + tn], in_=PS[bank][:, 0:tn],
                                                                                func=AF.Sigmoid),
                          reads=[("ps", bank)], writes=[("sgm", fc)])
                    P.pool(lambda e, fc=fc, t0=t0, tn=tn: e.tensor_tensor(out=ya[:, fc, t0:t0 + tn], in0=ya[:, fc, t0:t0 + tn],
                                                                          in1=sgm[:, fc, t0:t0 + tn], op=ALU.mult),
                           reads=[("ya", fc), ("sgm", fc)], writes=[("ya", fc)])
            bi += 1
            for fc in range(4):
                for ti, (t0, tn, _s) in enumerate(tiles):
                    bank = (fc * 2 + ti) % 4
                    proj_fm(ws, bi, fc, gbcT, [], t0, tn, bank)
                    P.act(lambda e, fc=fc, t0=t0, tn=tn, bank=bank: e.activation(out=yb[:, fc, t0:t0 + tn], in_=PS[bank][:, 0:tn], func=AF.Copy),
                          reads=[("ps", bank)], writes=[("yb", fc)])
            bi += 1
            for fc in range(4):
                f = fg * 4 + fc
                for ti, (t0, tn, _s) in enumerate(tiles):
                    bank = (fc * 2 + ti) % 4
                    proj_fm(ws, bi, fc, hT, [], t0, tn, bank)
                    P.act(lambda e, fc=fc, t0=t0, tn=tn, bank=bank: e.activation(out=sgm[:, fc, t0:t0 + tn], in_=PS[bank][:, 0:tn],
                                                                                func=AF.Sigmoid),
                          reads=[("ps", bank)], writes=[("sgm", fc)])
                    P.pool(lambda e, fc=fc, t0=t0, tn=tn: e.tensor_tensor(out=yb[:, fc, t0:t0 + tn], in0=yb[:, fc, t0:t0 + tn],
                                                                          in1=sgm[:, fc, t0:t0 + tn], op=ALU.mult),
                           reads=[("yb", fc), ("sgm", fc)], writes=[("yb", fc)])
                    P.pool(lambda e, f=f, fc=fc, t0=t0, tn=tn: e.tensor_tensor(out=mT[:, f, t0:t0 + tn], in0=ya[:, fc, t0:t0 + tn],
                                                                               in1=yb[:, fc, t0:t0 + tn], op=ALU.add),
                           reads=[("ya", fc), ("yb", fc)], writes=[("mT", f)])
            bi += 1
        P.barrier()

        P.dma("sp", GP[:, :], gpost_d[0], "gp", writes=[("GP",)])
        for nb in range(4):
            for ci, (rows, col0) in enumerate(chunks):
                bank = ci % 4
                proj_tm(ws, bi, mT, [], col0, rows, bank)
                P.act(lambda e, rows=rows, ci=ci, nb=nb, bank=bank: e.activation(out=XF[:rows, ci, nb * 512:(nb + 1) * 512], in_=PS[bank][:rows, :],
                                                                                func=AF.Copy),
                      reads=[("ps", bank)], writes=[("XF", ci)])
            bi += 1
        for ci, (rows, col0) in enumerate(chunks):
            col = rstd_col(XF[:rows, ci, :], rows, D, [("XF", ci)], dumpA[:rows, :], None)
            P.dve(lambda e, rows=rows, ci=ci, col=col: e.scalar_tensor_tensor(out=XF[:rows, ci, :], in0=XF[:rows, ci, :],
                                                                             scalar=st[:rows, col:col + 1], in1=GP[:rows, :],
                                                                             op0=ALU.mult, op1=ALU.mult),
                  reads=[("XF", ci), ("st", col), ("GP",)], writes=[("XF", ci)])
            P.dma("pool", XF[:rows, ci, :], xrows(ci), ("xacc", ci), reads=[("XF", ci)], writes=[("XF", ci)], accum=True)
            P.dma("sp", xmid_d[ci, 0:rows, :], XF[:rows, ci, :], ("xmo", ci), reads=[("XF", ci)])
        norm_transpose(HT, 1, chunks, lambda ci: (XF[:, ci, :], ("XF", ci)))
        P.barrier()

        for fg in range(11):
            for fc in range(4):
                f = fg * 4 + fc
                up = UPf[:, fc % 2, :]
                upkey = ("upf", fc % 2)
                P.dve(lambda e, up=up, f=f: e.tensor_copy(out=up[:, 0:2], in_=halo_f[:, f, :]), reads=[("halo_f", f)], writes=[upkey])
                if has_s:
                    P.dve(lambda e, up=up, f=f: e.tensor_copy(out=up[:, 642:738].rearrange("p (b t) -> p b t", t=6)[:, :, 0:2],
                                                              in_=halo_fs[:, f, :, :]), reads=[("halo_fs", f)], writes=[upkey])
                for ti, (t0, tn, segs) in enumerate(tiles):
                    bank = (fc * 2 + ti) % 4
                    proj_fm(ws, bi, fc, HT, [], t0, tn, bank)
                    for (so, sn, is_s, sc0) in segs:
                        if not is_s:
                            P.act(lambda e, up=up, so=so, sn=sn, sc0=sc0, bank=bank: e.activation(out=up[:, 2 + sc0:2 + sc0 + sn],
                                                                                                 in_=PS[bank][:, so:so + sn], func=AF.Copy),
                                  reads=[("ps", bank)], writes=[upkey])
                        else:
                            P.act(lambda e, up=up, so=so, sn=sn, bank=bank: e.activation(
                                out=up[:, 642:738].rearrange("p (b t) -> p b t", t=6)[:, :, 2:6],
                                in_=PS[bank][:, so:so + sn].rearrange("p (b t) -> p b t", t=4), func=AF.Copy),
                                reads=[("ps", bank)], writes=[upkey])
                P.dve(lambda e, up=up, f=f, TG=TG: e.tensor_copy(out=halo_f[:, f, :], in_=up[:, TG:TG + 2]), reads=[upkey],
                      writes=[("halo_f", f)])
                if has_s:
                    P.dve(lambda e, up=up, f=f: e.tensor_copy(out=halo_fs[:, f, :, :],
                                                              in_=up[:, 642:738].rearrange("p (b t) -> p b t", t=6)[:, :, 4:6]),
                          reads=[upkey], writes=[("halo_fs", f)])
                conv3(gaB[:, fc, 0:TG], gaB[:, fc, TG:TG + 64].rearrange("p (b t) -> p b t", t=4) if has_s else None, up, upkey,
                      ("ga", fc), f, fcw, fcb[:, f:f + 1])
                P.act(lambda e, fc=fc, TT=TT: e.activation(out=gaB[:, fc, 0:TT], in_=gaB[:, fc, 0:TT], func=AF.Gelu_apprx_tanh),
                      reads=[("ga", fc)], writes=[("ga", fc)])
            bi += 1
            for fc in range(4):
                f = fg * 4 + fc
                for ti, (t0, tn, _s) in enumerate(tiles):
                    bank = (fc * 2 + ti) % 4
                    proj_fm(ws, bi, fc, HT, [], t0, tn, bank)
                    P.dve(lambda e, f=f, fc=fc, t0=t0, tn=tn, bank=bank: e.tensor_tensor(
                        out=aT[:, f, t0:t0 + tn], in0=PS[bank][:, 0:tn], in1=gaB[:, fc, t0:t0 + tn], op=ALU.mult),
                        reads=[("ps", bank), ("ga", fc)], writes=[("aT", f)])
            bi += 1
        P.barrier()

        if pi + 1 < len(PASSES):
            enqueue_phaseA(pi + 1)
        e2n = 0
        for nb in range(4):
            for sub in range(3):
                nkc = 16 if sub < 2 else 12
                for ci, (rows, col0) in enumerate(chunks):
                    proj_tm(ws, bi, aT, [], col0, rows, ci, nkc=nkc, kc0=sub * 16, start=(sub == 0), stop=(sub == 2))
                    pumpA(1, every=5)
                    if sub == 2:
                        P.act(lambda e, rows=rows, ci=ci, nb=nb: e.activation(out=XF[:rows, ci, nb * 512:(nb + 1) * 512], in_=PS[ci][:rows, :],
                                                                             func=AF.Copy),
                              reads=[("ps", ci)], writes=[("XF", ci)])
                bi += 1
        pumpA(1000, every=1)
        P.barrier()
        if pi == len(PASSES) - 1:
            out_rows(lambda f: halo_f[:, f, :], NFF, 2, ffnp)
            out_rows(lambda f: halo_c[:, f, :], 16, 2, convp)
            out_rows(lambda f: halo_fs[:, f, :, :].rearrange("p b t -> p (b t)"), NFF, 32, ffns)
            out_rows(lambda f: halo_cs[:, f, :, :].rearrange("p b t -> p (b t)"), 16, 32, convs)
        P.dma("sp", GP[:, :], gpost_d[1], "gp", writes=[("GP",)])
        def ld_xmid(ci):
            P.dma("sp", XT[ci % 4][:chunks[ci][0], :], xmid_d[ci, 0:chunks[ci][0], :], ("xt", ci % 4), writes=[("XT", ci % 4)])

        for ci in range(min(4, nchk)):
            ld_xmid(ci)
        for ci, (rows, col0) in enumerate(chunks):
            slot = ci % 4
            col = rstd_col(XF[:rows, ci, :], rows, D, [("XF", ci)], junks[1][:rows, :], None)
            P.dve(lambda e, rows=rows, ci=ci, col=col: e.scalar_tensor_tensor(out=XF[:rows, ci, :], in0=XF[:rows, ci, :],
                                                                             scalar=st[:rows, col:col + 1], in1=GP[:rows, :],
                                                                             op0=ALU.mult, op1=ALU.mult),
                  reads=[("XF", ci), ("st", col), ("GP",)], writes=[("XF", ci)])
            P.dve(lambda e, rows=rows, ci=ci, slot=slot: e.tensor_tensor(out=XF[:rows, ci, :], in0=XF[:rows, ci, :], in1=XT[slot][:rows, :],
                                                                         op=ALU.add),
                  reads=[("XF", ci), ("XT", slot)], writes=[("XF", ci)])
            if ci + 4 < nchk:
                ld_xmid(ci + 4)
            if ci < nch:
                dst = None if ps["out"][ci] is None else ym[ps["out"][ci]: ps["out"][ci] + rows, :]
            else:
                dst = ys[:, :]
            if dst is not None:
                P.dma("sp", dst, XF[:rows, ci, :], ("yo", ci), reads=[("XF", ci)])
        P.barrier()

    P.emit()
    return nc


def _tables(core):
    half = core % 2
    f32 = np.float32
    inv = (f32(10000.0) ** (-(np.arange(128, dtype=f32) / f32(128)))).astype(f32)

    def cs(pos):
        ang = (np.asarray(pos, dtype=f32)[:, None] * inv[None, :]).astype(f32)
        return np.cos(ang).astype(np.float64), np.sin(ang).astype(np.float64)

    logg = np.array(LOGG, dtype=np.float64)
    m = np.arange(NMAIN)
    pos_main = np.maximum(half * 1024 - NHALO + m, 0)
    t_main = NPRE + m
    pos_pre = np.arange(NPRE)
    t_pre = np.arange(NPRE)
    tau = np.arange(NSAMP) % 4
    pos_s = 16384 + tau
    cm, sm = cs(pos_main)
    cp, sp_ = cs(pos_pre)
    c_s, s_s = cs(pos_s)

    cq = np.zeros((2, NH, 128, 640), np.float64)
    sq = np.zeros((2, NH, 128, 640), np.float64)
    ck = np.zeros((3, 128, 8, 128), np.float64)
    sk = np.zeros((3, 128, 8, 128), np.float64)
    ksc = np.zeros((128, 3, 8, NH), np.float64)
    for p, ps in enumerate(PASSES):
        csrc, ssrc, tsrc = (cp, sp_, t_pre) if p == 0 else (cm, sm, t_main)
        for ci, (rw, r0) in enumerate(zip(ps["rows"], ps["r0"])):
            a = ps["row0"] + r0
            ck[p, :rw, ci] = csrc[a:a + rw]
            sk[p, :rw, ci] = ssrc[a:a + rw]
            for h in range(NH):
                ksc[:rw, p, ci, h] = np.exp(-logg[h] * (tsrc[a:a + rw] - TOFF)) / 16.0
                if p > 0:
                    dq = np.exp(logg[h] * (tsrc[a:a + rw] - TOFF))
                    cq[p - 1, h, :, r0:r0 + rw] = (csrc[a:a + rw] * dq[:, None]).T
                    sq[p - 1, h, :, r0:r0 + rw] = (ssrc[a:a + rw] * dq[:, None]).T
    ck[2, :64, 4] = c_s
    sk[2, :64, 4] = s_s
    for h in range(NH):
        dq = np.exp(logg[h] * (tau + 1.0))
        cq[1, h, :, 512:576] = (c_s * dq[:, None]).T
        sq[1, h, :, 512:576] = (s_s * dq[:, None]).T
        ksc[:64, 2, 4, h] = np.exp(-logg[h] * (tau + 1.0)) / 16.0
    i = np.arange(128)
    cmask = (i[None, :] >= i[:, None]).astype(f32)
    j64 = np.arange(64)
    bmask = ((j64[None, :] // 4 == j64[:, None] // 4) & (j64[None, :] >= j64[:, None])).astype(f32)
    bm16 = np.broadcast_to((j64[None, :] // 4 == np.arange(NB)[:, None]).astype(f32)[None], (128, NB, 64)).copy()
    bmT = (j64[:, None] // 4 == np.arange(NB)[None, :]).astype(f32)
    return dict(cq=cq.astype(f32), sq=sq.astype(f32), ck=ck.astype(f32), sk=sk.astype(f32), ksc=ksc.astype(f32),
                cmask=cmask, bmask=bmask, bm16=bm16, bmT=bmT, idf=np.eye(128, dtype=f32))


def _in_map(core, x_prompt, x_sample, state_ret, state_conv, state_ffn, g_pre_mix, w_in, conv_w, p_ret, p_conv, w_o, g_post_mix,
            g_pre_ffn, w_up, w_gate, ffn_conv_w, ffn_conv_b, w_down, g_post_ffn):
    f32 = np.float32
    b, half = core // 2, core % 2
    xm = np.zeros((NMAIN, D), f32)
    xp = np.zeros((NPRE, D), f32)
    if half == 0:
        xm[NHALO:] = x_prompt[b, 0:1024]
    else:
        xm[:] = x_prompt[b, 1024 - NHALO:2048]
        xp[:] = x_prompt[b, 0:NPRE]
    sl = slice(core * NB, (core + 1) * NB)
    mp = dict(
        xm=xm, xp=xp, xs=np.ascontiguousarray(x_sample[sl].reshape(NSAMP, D)),
        sret=np.ascontiguousarray(state_ret[0, sl]),
        sconv=np.ascontiguousarray(state_conv[0, sl].reshape(2 * NB, D)),
        sffn=np.ascontiguousarray(state_ffn[0, sl].reshape(2 * NB, DFF)),
        w_in=w_in[0], p_ret=p_ret[0], p_conv=p_conv[0], w_o=w_o[0], w_up=w_up[0], w_gate=w_gate[0], w_down=w_down[0],
        gcol=np.ascontiguousarray(np.stack([g_pre_mix[0].reshape(16, 128).T, g_pre_ffn[0].reshape(16, 128).T], axis=1)),
        gpost=np.ascontiguousarray(np.stack([np.broadcast_to(g_post_mix[0], (128, D)), np.broadcast_to(g_post_ffn[0], (128, D))])),
        cw=np.ascontiguousarray(conv_w[0].T.reshape(16, 128, 3).transpose(1, 0, 2)),
        fcw=np.ascontiguousarray(ffn_conv_w[0].T.reshape(NFF, 128, 3).transpose(1, 0, 2)),
        fcb=np.ascontiguousarray(ffn_conv_b[0].reshape(NFF, 128).T),
    )
    mp.update(_tables(core))
    return {k: np.ascontiguousarray(v, dtype=f32) for k, v in mp.items()}


_NC_CACHE = {}


def _run(inputs, cores):
    if "nc" not in _NC_CACHE:
        _NC_CACHE["nc"] = build_program()
    nc = _NC_CACHE["nc"]
    inputs = {k: np.asarray(v) for k, v in inputs.items()}
    in_maps = [_in_map(c, **inputs) for c in cores]
    res = run_bass_kernel_spmd(nc, in_maps, core_ids=list(range(len(cores))))
    return res.results


def kernel(**inputs):
    f32 = np.float32
    results = _run(inputs, list(range(NCORES)))
    B, S = 4, 2048
    y_prompt = np.zeros((B, S, D), f32)
    y_sample = np.zeros((128, 4, D), f32)
    ret_prompt = np.zeros((1, B, NH, DK, DV), f32)
    conv_prompt = np.zeros((1, B, 2, D), f32)
    ffn_prompt = np.zeros((1, B, 2, DFF), f32)
    ret_sample = np.zeros((1, 128, NH, DK, DV), f32)
    conv_sample = np.zeros((1, 128, 2, D), f32)
    ffn_sample = np.zeros((1, 128, 2, DFF), f32)
    for c, r in enumerate(results):
        b, half = c // 2, c % 2
        y_prompt[b, half * 1024:(half + 1) * 1024] = r["ym"]
        sl = slice(c * NB, (c + 1) * NB)
        y_sample[sl] = r["ys"].reshape(NB, 4, D)
        ret_sample[0, sl] = r["rets"]
        conv_sample[0, sl] = r["convs"].reshape(NB, 2, D)
        ffn_sample[0, sl] = r["ffns"].reshape(NB, 2, DFF)
        if half == 1:
            ret_prompt[0, b] = r["retp"]
            conv_prompt[0, b] = r["convp"]
            ffn_prompt[0, b] = r["ffnp"]
    return (y_prompt, y_sample, ret_prompt, conv_prompt, ffn_prompt, ret_sample, conv_sample, ffn_sample)
```

```python
import math
import numpy as np
import concourse.bass as bass
import concourse.mybir as mybir
from concourse.bass_utils import run_bass_kernel_spmd

F32 = mybir.dt.float32
BF16 = mybir.dt.bfloat16
ALU = mybir.AluOpType
AF = mybir.ActivationFunctionType

D = 2048
NH = 8
DK = 256
DV = 512
DFF = 5632
NFF = DFF // 128
EPS = 1e-6
NCORES = 8
NHALO = 4
NPRE = 1024 - NHALO
NMAIN = 1024 + NHALO
NSAMP = 64
NB = 16
TOFF = 1024.0
LOGG = [math.log1p(-2.0 ** (-5 - h)) for h in range(NH)]
C_Q, C_K, C_V, C_G, C_GB, C_GC, C_HC, C_GA, C_GBT = 0, 2048, 4096, 8192, 12288, 14336, 16384, 18432, 20480

PASSES = [
    dict(src="xp", row0=0, rows=[128] * 7 + [124], kv=True, sample=False, out=None),
    dict(src="xm", row0=0, rows=[NHALO, 128, 128, 128, 128], kv=False, sample=False, out=[None, 0, 128, 256, 384]),
    dict(src="xm", row0=NHALO + 512, rows=[128] * 4, kv=False, sample=True, out=[512, 640, 768, 896]),
]
for _p in PASSES:
    _p["nch"] = len(_p["rows"])
    _p["r0"] = [sum(_p["rows"][:i]) for i in range(len(_p["rows"]))]


class _Op:
    __slots__ = ("eng", "fn", "deps", "signal", "ev", "dkey")


class Prog:
    def __init__(self, nc):
        self.nc = nc
        self.ops = []
        self.last_w = {}
        self.readers = {}
        self.dkeys = []

    def add(self, eng, fn, reads=(), writes=(), dkey=None):
        op = _Op()
        op.eng, op.fn, op.signal, op.ev, op.dkey = eng, fn, False, None, dkey
        deps = set()
        for r in reads:
            w = self.last_w.get(r)
            if w is not None:
                deps.add(w)
        for wr in writes:
            w = self.last_w.get(wr)
            if w is not None:
                deps.add(w)
            deps.update(self.readers.get(wr, ()))
        idx = len(self.ops)
        for r in reads:
            self.readers.setdefault(r, []).append(idx)
        for wr in writes:
            self.last_w[wr] = idx
            self.readers[wr] = []
        op.deps = deps
        self.ops.append(op)
        if dkey is not None and dkey not in self.dkeys:
            self.dkeys.append(dkey)
        return idx

    def pe(self, fn, reads=(), writes=()):
        return self.add("pe", fn, reads, writes)

    def act(self, fn, reads=(), writes=()):
        return self.add("act", fn, reads, writes)

    def dve(self, fn, reads=(), writes=()):
        return self.add("dve", fn, reads, writes)

    def pool(self, fn, reads=(), writes=()):
        return self.add("pool", fn, reads, writes)

    def dma(self, q, out, in_, key, reads=(), writes=(), accum=False):
        if accum:
            return self.add(q, lambda e: e.dma_start(out=out, in_=in_, accum_op=ALU.add), reads, writes, dkey=key)
        return self.add(q, lambda e: e.dma_start(out=out, in_=in_), reads, writes, dkey=key)

    def barrier(self):
        last = {}
        for i, op in enumerate(self.ops):
            if op.fn is not None:
                last[("e", op.eng) if op.dkey is None else ("d", op.dkey)] = i
        deps = set(last.values())
        for eng in ("pe", "act", "dve", "pool", "sp"):
            op = _Op()
            op.eng, op.fn, op.signal, op.ev, op.dkey = eng, None, False, None, None
            op.deps = set(deps)
            self.ops.append(op)
        self.last_w = {}
        self.readers = {}

    def emit(self):
        nc = self.nc
        engs = {"pe": nc.tensor, "act": nc.scalar, "dve": nc.vector, "pool": nc.gpsimd, "sp": nc.sync}
        for op in self.ops:
            for d in op.deps:
                if self.ops[d].dkey is None:
                    self.ops[d].signal = True
        esem = {e: nc.alloc_semaphore("es_" + e) for e in engs}
        dsem = {k: nc.alloc_semaphore("ds_%d" % i) for i, k in enumerate(self.dkeys)}
        cnt = {e: 0 for e in engs}
        dcnt = {k: 0 for k in self.dkeys}
        waited = {e: {} for e in engs}
        for op in self.ops:
            E = engs[op.eng]
            wl = {}
            for d in op.deps:
                p = self.ops[d]
                if p.fn is None:
                    continue
                if op.eng == "pe" and p.eng == "pe" and p.dkey is None:
                    continue
                name, sem, val = p.ev
                if wl.get(name, (None, 0))[1] < val:
                    wl[name] = (sem, val)
            for name, (sem, val) in wl.items():
                if waited[op.eng].get(name, 0) >= val:
                    continue
                E.wait_ge(sem, val)
                waited[op.eng][name] = val
            if op.fn is None:
                continue
            ins = op.fn(E)
            if op.dkey is not None:
                dcnt[op.dkey] += 16
                ins.then_inc(dsem[op.dkey], 16)
                op.ev = (("d", op.dkey), dsem[op.dkey], dcnt[op.dkey])
            elif op.signal:
                cnt[op.eng] += 1
                ins.then_inc(esem[op.eng], 1)
                op.ev = (("e", op.eng), esem[op.eng], cnt[op.eng])
        for k in self.dkeys:
            if dcnt[k]:
                nc.sync.wait_ge(dsem[k], dcnt[k])
        for e in engs:
            if cnt[e] and e != "sp":
                nc.sync.wait_ge(esem[e], cnt[e])


def build_program():
    nc = bass.Bass("TRN2", target_bir_lowering=False)
    P = Prog(nc)

    def din(name, shape):
        return nc.dram_tensor(name, list(shape), F32, kind="ExternalInput").ap()

    def dout(name, shape):
        return nc.dram_tensor(name, list(shape), F32, kind="ExternalOutput").ap()

    xm = din("xm", [NMAIN, D])
    xp = din("xp", [NPRE, D])
    xs = din("xs", [NSAMP, D])
    sret = din("sret", [NB, NH, DK, DV])
    sconv = din("sconv", [2 * NB, D])
    sffn = din("sffn", [2 * NB, DFF])
    w_in = din("w_in", [D, 22528])
    p_ret = din("p_ret", [4096, D])
    p_conv = din("p_conv", [D, D])
    w_o = din("w_o", [D, D])
    w_up = din("w_up", [D, DFF])
    w_gate = din("w_gate", [D, DFF])
    w_down = din("w_down", [DFF, D])
    gcol_d = din("gcol", [128, 2, 16])
    gpost_d = din("gpost", [2, 128, D])
    cw_d = din("cw", [128, 16, 3])
    fcw_d = din("fcw", [128, NFF, 3])
    fcb_d = din("fcb", [128, NFF])
    cq_d = din("cq", [2, NH, 128, 640])
    sq_d = din("sq", [2, NH, 128, 640])
    ck_d = din("ck", [3, 128, 8, 128])
    sk_d = din("sk", [3, 128, 8, 128])
    ksc_d = din("ksc", [128, 3, 8, NH])
    cmask_d = din("cmask", [128, 128])
    bmask_d = din("bmask", [64, 64])
    bm16_d = din("bm16", [128, NB, 64])
    bmT_d = din("bmT", [64, NB])
    idf_d = din("idf", [128, 128])

    ym = dout("ym", [1024, D])
    ys = dout("ys", [NSAMP, D])
    retp = dout("retp", [NH, DK, DV])
    convp = dout("convp", [2, D])
    ffnp = dout("ffnp", [2, DFF])
    rets = dout("rets", [NB, NH, DK, DV])
    convs = dout("convs", [2 * NB, D])
    ffns = dout("ffns", [2 * NB, DFF])

    xmid_d = nc.dram_tensor("xmid_scr", [6, 128, D], F32, kind="Internal").ap()
    sbase_d = nc.dram_tensor("sbase_scr", [NH, 128, 2 * DV], F32, kind="Internal").ap()

    def sb(name, shape, dt):
        return nc.alloc_sbuf_tensor("s_" + name, list(shape), dt)

    HT = sb("HT", [128, 16, 640], BF16)
    R1 = sb("R1", [128, 10240], F32)
    R2 = sb("R2", [128, 5120], F32)
    R4 = sb("R4", [128, 16384], F32)
    WB = [sb("WB%d" % i, [128, 16, 512], BF16) for i in range(3)]
    ident = sb("ident", [128, 128], BF16)
    identf = sb("identf", [128, 128], F32)
    cmask = sb("cmask", [128, 128], F32)
    bmask = sb("bmask", [64, 64], F32)
    bm16 = sb("bm16", [128, NB, 64], BF16)
    bmT = sb("bmT", [64, NB], F32)
    gcol = sb("gcol", [128, 2, 16], F32)
    cw = sb("cw", [128, 16, 3], F32)
    fcw = sb("fcw", [128, NFF, 3], F32)
    fcb = sb("fcb", [128, NFF], F32)
    ksc = sb("ksc", [128, 3, 8, NH], F32)
    halo_c = sb("halo_c", [128, 16, 2], F32)
    halo_f = sb("halo_f", [128, NFF, 2], F32)
    halo_cs = sb("halo_cs", [128, 16, NB, 2], F32)
    halo_fs = sb("halo_fs", [128, NFF, NB, 2], F32)
    st = sb("st", [128, 64], F32)
    otm = sb("otm", [32, 512], F32)

    def _view(reg, off_b, shape, bf):
        n = int(np.prod(shape))
        if bf:
            v = reg[:, off_b // 4:(off_b + 2 * n + 3) // 4].bitcast(BF16)
        else:
            v = reg[:, off_b // 4:off_b // 4 + n]
        if len(shape) == 1:
            return v
        names = " ".join("d%d" % i for i in range(len(shape)))
        kw = {"d%d" % i: s for i, s in enumerate(shape)}
        return v.rearrange("p (%s) -> p %s" % (names, names), **kw)

    def r_bf(reg, off_b, shape):
        return _view(reg, off_b, shape, True)

    def r_f32(reg, off_b, shape):
        return _view(reg, off_b, shape, False)

    goT = r_bf(R1, 0, [32, 640])
    XF = r_f32(R1, 0, [5, 2048])
    hT0 = r_bf(R1, 0, [16, 1024])
    gbcT = r_bf(R2, 0, [16, 640])
    Sf = r_f32(R2, 0, [4, 2, 512])
    Sbf = r_bf(R2, 16384, [2, 2, 512])
    GP = r_f32(R2, 0, [2048])
    mT = r_bf(R4, 0, [16, 640])
    aT = r_bf(R4, 0, [NFF, 640])
    junk = r_bf(R4, 40960, [2048])
    XT = [r_f32(R4, 49152, [2048]), r_f32(R4, 57344, [2048]), r_f32(R4, 24576, [2048]), r_f32(R4, 32768, [2048])]
    dumpA = r_bf(R4, 20480, [2048])
    XTh = [r_f32(R2, 0, [2048]), r_f32(R2, 8192, [2048])]
    junkh = r_bf(R2, 16384, [2048])
    off = {"o": 0}

    def tb(shape, bf):
        n = int(np.prod(shape)) * (2 if bf else 4)
        n = (n + 3) // 4 * 4
        v = _view(R4, off["o"], shape, bf)
        off["o"] += n
        return v

    qT = tb([2, 2, 640], True)
    kTM = tb([8, 2, 256], True)
    kT = tb([2, 2, 640], True)
    vS = tb([2, 512], True)
    sg = tb([5, 512], True)
    cqt = tb([576], False)
    sqt = tb([576], False)
    ckt = tb([8, 128], False)
    skt = tb([8, 128], False)
    Sb = tb([2, 2, 512], True)
    base = tb([2, 512], False)
    rt1 = tb([288], False)
    rt2 = tb([288], False)
    PT = tb([2, 128], True)
    gotm = tb([2, 512], True)
    qTs = tb([2, 2, 64], True)
    khat = tb([2, 256], True)
    vs = tb([2, 512], True)
    osum = tb([2, 512], False)
    sgs = tb([2, 512], True)
    khmb = tb([2, 256], True)
    rt3 = tb([288], False)
    rt4 = tb([288], False)
    junkB = tb([512], True)
    assert off["o"] <= 65536, off["o"]
    gcs = r_f32(R4, 20480, [4, 640])
    UPc = r_f32(R4, 20480 + 10240, [4, 740])
    cv = r_f32(R4, 20480 + 10240 + 11840, [4, 640])
    ya = r_f32(R4, 20480, [4, 640])
    yb = r_f32(R4, 20480 + 10240, [4, 640])
    sgm = r_f32(R4, 20480 + 20480, [4, 640])
    UPf = r_f32(R2, 0, [2, 740])
    gaB = r_f32(R2, 5920, [4, 640])

    PS = [nc.alloc_psum_tensor("ps%d" % i, [128, 512], F32) for i in range(8)]

    def psb(i):
        return PS[i][:, :].bitcast(BF16).rearrange("p (a b) -> p a b", b=128)

    def cload(dst, src, key, q="sp"):
        P.dma(q, dst, src, key, writes=[("c", key)])

    cload(ident[:], idf_d[:, :], "c_id", q="pool")
    cload(identf[:], idf_d[:, :], "c_idf")
    cload(cmask[:], cmask_d[:, :], "c_cm")
    cload(bmask[:], bmask_d[:, :], "c_bm")
    cload(bm16[:], bm16_d[:, :, :], "c_bm16", q="pool")
    cload(bmT[:], bmT_d[:, :], "c_bmT")
    cload(gcol[:], gcol_d[:, :, :], "c_gcol")
    cload(cw[:], cw_d[:, :, :], "c_cw")
    cload(fcw[:], fcw_d[:, :, :], "c_fcw")
    cload(fcb[:], fcb_d[:, :], "c_fcb")
    cload(ksc[:], ksc_d[:, :, :, :], "c_ksc")
    P.dve(lambda e: e.memset(halo_c[:], 0.0), writes=[("halo_c",)])
    P.dve(lambda e: e.memset(halo_f[:], 0.0), writes=[("halo_f",)])
    tmpc = r_f32(R1, 0, [2048])
    tmpf = r_f32(R1, 8192, [DFF])
    P.dma("sp", tmpc[:32, :], sconv[:, :], "c_tc", writes=[("tmpc",)])
    P.dma("sp", tmpf[:32, :], sffn[:, :], "c_tf", writes=[("tmpf",)])
    P.barrier()

    def halo_init(tmp, dst, nf):
        for g0 in range(0, nf, 16):
            n = min(16, nf - g0)
            bank = (g0 // 16) % 2

            def f(e, g0=g0, n=n, bank=bank):
                ins = None
                for j in range(n):
                    ins = e.transpose(PS[bank][:, j * 32:(j + 1) * 32], tmp[:32, (g0 + j) * 128:(g0 + j + 1) * 128], identf[:32, :32])
                return ins

            P.pe(f, writes=[("ps", bank)])
            P.act(lambda e, g0=g0, n=n, bank=bank: e.activation(
                out=dst[:, g0:g0 + n, :, :].rearrange("p f b t -> p f (b t)"),
                in_=PS[bank][:, 0:n * 32].rearrange("p (f x) -> p f x", x=32), func=AF.Copy),
                reads=[("ps", bank)], writes=[("halo_s", g0)])

    halo_init(tmpc, halo_cs, 16)
    halo_init(tmpf, halo_fs, NFF)
    P.barrier()

    wstate = {"n": 0}

    def wload(w2d, r0, c0, nkc=16, ncol=512):
        slot = wstate["n"] % 3
        wstate["n"] += 1
        src = w2d[r0:r0 + nkc * 128, c0:c0 + ncol].rearrange("(kc p) n -> p kc n", p=128)
        P.dma("pool", WB[slot][:, 0:nkc, 0:ncol], src, ("wb", slot), writes=[("WB", slot)])
        return slot

    class WStream:
        def __init__(self, blocks):
            self.blocks = blocks
            self.issued = []

        def get(self, i):
            while len(self.issued) < min(len(self.blocks), i + 3):
                b = self.blocks[len(self.issued)]
                self.issued.append(wload(*b))
            return self.issued[i]

    st_n = {"n": 0}

    def stcol():
        i = st_n["n"] % 64
        st_n["n"] += 1
        return i

    def rstd_col(src_ap, rows, n, reads, junk_ap, junk_key):
        col = stcol()
        P.act(lambda e: e.activation(out=junk_ap, in_=src_ap, func=AF.Square, accum_out=st[:rows, col:col + 1]),
              reads=list(reads), writes=[("st", col)] + ([junk_key] if junk_key is not None else []))
        P.act(lambda e: e.activation(out=st[:rows, col:col + 1], in_=st[:rows, col:col + 1], func=AF.Sqrt, bias=EPS, scale=1.0 / n),
              reads=[("st", col)], writes=[("st", col)])
        P.dve(lambda e: e.reciprocal(out=st[:rows, col:col + 1], in_=st[:rows, col:col + 1]), reads=[("st", col)], writes=[("st", col)])
        return col

    junks = [junk, r_bf(R4, 45056, [2048])]
    FG = dict(junks=junks, jkeys=[("junk", 0), ("junk", 1)], dump=dumpA)
    BGB = dict(junks=[junkh], jkeys=[("junkh",)], dump=None)

    def norm_stages(dst, gi, chunks, load_fn, bufs):
        out = []
        nj = len(bufs["junks"])
        for ci, (rows, col0) in enumerate(chunks):
            jb = bufs["junks"][ci % nj]
            jk = bufs["jkeys"][ci % nj]

            def s1(ci=ci, rows=rows, jb=jb, jk=jk):
                src, skey = load_fn(ci)
                if bufs["dump"] is not None:
                    col = rstd_col(src[:rows, :], rows, D, [skey], bufs["dump"][:rows, :], None)
                else:
                    col = rstd_col(src[:rows, :], rows, D, [skey], jb[:rows, :], jk)
                P.dve(lambda e: e.tensor_scalar(out=jb[:rows, :], in0=src[:rows, :], scalar1=st[:rows, col:col + 1], scalar2=None,
                                                op0=ALU.mult),
                      reads=[skey, ("st", col)], writes=[jk])

            def s2(ci=ci, rows=rows, col0=col0, jb=jb, jk=jk):
                for g4 in range(4):
                    bank = 6 + (g4 % 2)
                    pv = psb(bank)

                    def tr(e, g4=g4, pv=pv):
                        ins = None
                        for j in range(4):
                            kc = g4 * 4 + j
                            ins = e.transpose(pv[:, j, 0:rows], jb[:rows, kc * 128:(kc + 1) * 128], ident[:rows, :rows])
                        return ins

                    P.pe(tr, reads=[jk], writes=[("ps", bank)])
                    P.dve(lambda e, g4=g4, pv=pv: e.tensor_tensor(
                        out=dst[:, g4 * 4:(g4 + 1) * 4, col0:col0 + rows], in0=pv[:, 0:4, 0:rows],
                        in1=gcol[:, gi, g4 * 4:(g4 + 1) * 4].unsqueeze(2).to_broadcast([128, 4, rows]), op=ALU.mult),
                        reads=[("ps", bank)], writes=[("hT", g4, ci)])

            out.append((s1, s2))
        return out

    def norm_transpose(dst, gi, chunks, load_fn, after2=None):
        stg = norm_stages(dst, gi, chunks, load_fn, FG)
        stg[0][0]()
        for ci in range(len(stg)):
            if ci + 1 < len(stg):
                stg[ci + 1][0]()
            if ci == 1 and after2 is not None:
                after2()
            stg[ci][1]()

    bgA = []

    pa = {"n": 0}

    def pumpA(n=1, every=1):
        pa["n"] += 1
        if pa["n"] % every != 0:
            return
        for _ in range(n):
            if bgA:
                bgA.pop(0)()

    def hT_reads(nchunks):
        return [("hT", g4, ci) for g4 in range(4) for ci in range(nchunks)]

    def proj_fm(ws, bi, fc, hsrc, hreads, col0, ncols, bank, nkc=16, kc0=0, mid=None):
        slot = ws.get(bi)
        parts = [(0, nkc)] if mid is None else [(0, nkc // 2), (nkc // 2, nkc)]
        for pi_, (ka, kb) in enumerate(parts):
            def f(e, ka=ka, kb=kb):
                ins = None
                for kc in range(ka, kb):
                    ins = e.matmul(PS[bank][:, 0:ncols], lhsT=WB[slot][:, kc, fc * 128:(fc + 1) * 128],
                                   rhs=hsrc[:, kc0 + kc, col0:col0 + ncols], start=(kc == 0), stop=(kc == nkc - 1), skip_group_check=True)
                return ins

            P.pe(f, reads=[("WB", slot)] + list(hreads), writes=[("ps", bank)])
            if mid is not None and pi_ == 0:
                mid()

    def proj_tm(ws, bi, hsrc, hreads, col0, rows, bank, nkc=16, kc0=0, start=True, stop=True, mid=None):
        slot = ws.get(bi)
        parts = [(0, nkc)] if mid is None else [(0, nkc // 2), (nkc // 2, nkc)]
        for pi_, (ka, kb) in enumerate(parts):
            def f(e, ka=ka, kb=kb):
                ins = None
                for kc in range(ka, kb):
                    ins = e.matmul(PS[bank][:rows, :], lhsT=hsrc[:, kc0 + kc, col0:col0 + rows],
                                   rhs=WB[slot][:, kc, :], start=(start and kc == 0), stop=(stop and kc == nkc - 1),
                                   skip_group_check=True)
                return ins

            P.pe(f, reads=[("WB", slot)] + list(hreads), writes=[("ps", bank)])
            if mid is not None and pi_ == 0:
                mid()

    ep_n = {"n": 0}
    late = []
    bg = []

    def flush_late(keep=0):
        while len(late) > keep:
            late.pop(0)()

    bgn = {"n": 0}

    def pump(n, banks=(7,)):
        for _ in range(n):
            if bg:
                bk = banks[bgn["n"] % len(banks)]
                bgn["n"] += 1
                bg.pop(0)(bk)

    def retention_epilogue(h, rows, col0, o_ap, o_key, sg_ap, sg_key):
        col = rstd_col(o_ap, rows, DV, [o_key], junkB[:rows, :], ("junkB",))
        gs = ep_n["n"] % 2
        ep_n["n"] += 1
        P.dve(lambda e: e.scalar_tensor_tensor(out=gotm[:rows, gs, :], in0=o_ap, scalar=st[:rows, col:col + 1],
                                               in1=sg_ap, op0=ALU.mult, op1=ALU.mult),
              reads=[o_key, ("st", col), sg_key], writes=[("gotm", gs)])

        def part2():
            pv = psb(6)

            def tr(e):
                ins = None
                for ec in range(4):
                    ins = e.transpose(pv[:, ec, 0:rows], gotm[:rows, gs, ec * 128:(ec + 1) * 128], ident[:rows, :rows])
                return ins

            P.pe(tr, reads=[("gotm", gs)], writes=[("ps", 6)])
            P.act(lambda e: e.activation(out=goT[:, h * 4:(h + 1) * 4, col0:col0 + rows], in_=pv[:, 0:4, 0:rows], func=AF.Copy),
                  reads=[("ps", 6)], writes=[("goT", h, col0)])

        late.append(part2)

    def sample_retention(h, hh, ci, TG, cx):
        g4 = math.exp(4.0 * LOGG[h])
        while bg:
            pump(1, (7, 6))
            flush_late()

        def sc_mm(e):
            ins = None
            for half in range(2):
                ins = e.matmul(PS[4][:64, 0:64], lhsT=kT[:, hh, half, TG:TG + 64], rhs=qT[:, hh, half, TG:TG + 64],
                               start=(half == 0), stop=(half == 1))
            return ins

        P.pe(sc_mm, reads=[("kT", hh, ci), ("qT", hh)], writes=[("ps", 4)])
        P.dve(lambda e: e.tensor_tensor(out=PT[:64, 0, 0:64], in0=PS[4][:64, 0:64], in1=bmask[:, :], op=ALU.mult),
              reads=[("ps", 4)], writes=[("PT", 0)])
        P.act(lambda e: e.mul(out=khat[:64, cx, :], in_=kTM[:64, ci, hh, :], mul=g4), reads=[("kTM", ci, hh)], writes=[("khat", cx)])
        P.act(lambda e: e.activation(out=qTs[:, cx], in_=qT[:, hh, :, TG:TG + 64], func=AF.Copy), reads=[("qT", hh)], writes=[("qTs", cx)])
        P.pe(lambda e: e.matmul(PS[7][:64, :], lhsT=PT[:64, 0, 0:64], rhs=vs[:64, cx, :], start=True, stop=True),
             reads=[("PT", 0), ("vs", cx)], writes=[("ps", 7)])
        P.act(lambda e: e.activation(out=osum[:64, cx, :], in_=PS[7][:64, :], func=AF.Copy), reads=[("ps", 7)], writes=[("osum", cx)])

        def load(b):
            P.dma("sp", Sf[:, b % 4], sret[b, h].rearrange("(a p) e -> p a e", p=128), ("sfi", b % 4), writes=[("Sf", b % 4)])

        for b in range(3):
            load(b)

        def cast(b):
            P.act(lambda e: e.activation(out=Sbf[:, b % 2], in_=Sf[:, b % 4], func=AF.Copy), reads=[("Sf", b % 4)], writes=[("Sbf", b % 2)])

        cast(0)

        def mk_cross(b):
            def f(bk):
                s4, s2 = b % 4, b % 2
                P.act(lambda e: e.mul(out=khmb[:64, s2, :], in_=khat[:64, cx, :], mul=bmT[:, b:b + 1]),
                      reads=[("khat", cx)], writes=[("khmb", s2)])

                def mm(e):
                    ins = None
                    for half in range(2):
                        ins = e.matmul(PS[bk][:64, :], lhsT=qTs[:, cx, half, :], rhs=Sbf[:, s2, half, :], start=(half == 0), stop=(half == 1))
                    return ins

                P.pe(mm, reads=[("qTs", cx), ("Sbf", s2)], writes=[("ps", bk)])
                P.dve(lambda e: e.scalar_tensor_tensor(out=osum[:64, cx, :], in0=PS[bk][:64, :], scalar=bmT[:, b:b + 1], in1=osum[:64, cx, :],
                                                       op0=ALU.mult, op1=ALU.add),
                      reads=[("ps", bk), ("osum", cx)], writes=[("osum", cx)])
                if b + 1 < NB:
                    cast(b + 1)
            return f

        def mk_state(b, half):
            def f(bk):
                s4, s2 = b % 4, b % 2
                P.pe(lambda e: e.matmul(PS[bk][:, :], lhsT=khmb[:64, s2, half * 128:(half + 1) * 128], rhs=vs[:64, cx, :], start=True, stop=True),
                     reads=[("khmb", s2), ("vs", cx)], writes=[("ps", bk)])
                P.dve(lambda e: e.scalar_tensor_tensor(out=Sf[:, s4, half, :], in0=Sf[:, s4, half, :], scalar=g4, in1=PS[bk][:, :],
                                                       op0=ALU.mult, op1=ALU.add),
                      reads=[("ps", bk), ("Sf", s4)], writes=[("Sf", s4)])
                if half == 1:
                    P.dma("sp", rets[b, h].rearrange("(a p) e -> p a e", p=128), Sf[:, s4], ("sfo", s4), reads=[("Sf", s4)])
                    if b + 3 < NB:
                        load(b + 3)
            return f

        for b in range(NB):
            bg.append(mk_cross(b))
            bg.append(mk_state(b, 0))
            bg.append(mk_state(b, 1))

        def fin(bk):
            retention_epilogue(h, 64, TG, osum[:64, cx, :], ("osum", cx), sgs[:64, cx, :], ("sgs", cx))
        bg.append(fin)

    ostage = r_f32(R4, 0, [DFF])
    on = {"n": 0}

    def out_rows(src_fn, nf, rows, dst):
        k = on["n"]
        on["n"] += 1
        stg = ostage if k % 2 == 0 else r_f32(R2, 8192, [2048])
        for g0 in range(0, nf, 4):
            bank = (g0 // 4) % 4

            def f(e, g0=g0, bank=bank):
                ins = None
                for j in range(4):
                    ins = e.transpose(PS[bank][:rows, j * 128:(j + 1) * 128], src_fn(g0 + j), identf[:, :])
                return ins

            P.pe(f, writes=[("ps", bank)])
            P.act(lambda e, bank=bank, g0=g0, stg=stg: e.activation(out=stg[:rows, g0 * 128:(g0 + 4) * 128], in_=PS[bank][:rows, :], func=AF.Copy),
                  reads=[("ps", bank)], writes=[("ostg", k % 2)])
        P.dma("sp", dst[:, :], stg[:rows, 0:nf * 128], ("ostg_o", k % 2), reads=[("ostg", k % 2)])

    def enqueue_phaseA(p):
        ps_ = PASSES[p]
        nch_ = ps_["nch"]
        src_ = xm if ps_["src"] == "xm" else xp
        chunks_ = [(ps_["rows"][c], ps_["r0"][c]) for c in range(nch_)] + ([(NSAMP, sum(ps_["rows"]))] if ps_["sample"] else [])

        def load(ci):
            slot = ci % 2
            rows = chunks_[ci][0]
            srcrows = src_[ps_["row0"] + ps_["r0"][ci]: ps_["row0"] + ps_["r0"][ci] + ps_["rows"][ci], :] if ci < nch_ else xs[:, :]
            P.dma("sp", XTh[slot][:rows, :], srcrows, ("xth", slot), writes=[("XTh", slot)])
            return XTh[slot], ("XTh", slot)

        for (s1, s2) in norm_stages(HT, 0, chunks_, load, BGB):
            bgA.append(s1)
            bgA.append(s2)

    blocks = []
    for ps_ in PASSES:
        kv_ = ps_["kv"]
        for hp in range(4):
            if not kv_:
                blocks.append((w_in, 0, C_Q + 512 * hp))
            blocks.append((w_in, 0, C_K + 512 * hp))
            for hh in range(2):
                h = 2 * hp + hh
                if not kv_:
                    blocks.append((w_in, 0, C_G + 512 * h))
                blocks.append((w_in, 0, C_V + 512 * h))
        if not kv_:
            for fg in range(4):
                blocks += [(w_in, 0, C_GC + 512 * fg), (w_in, 0, C_HC + 512 * fg), (w_in, 0, C_GB + 512 * fg)]
            for fg in range(4):
                blocks += [(p_ret, 0, 512 * fg), (p_ret, 2048, 512 * fg), (w_in, 0, C_GA + 512 * fg), (p_conv, 0, 512 * fg),
                           (w_in, 0, C_GBT + 512 * fg)]
            for nb in range(4):
                blocks += [(w_o, 0, 512 * nb)]
            for fg in range(11):
                blocks += [(w_up, 0, 512 * fg), (w_gate, 0, 512 * fg)]
            for nb in range(4):
                for sub in range(3):
                    blocks += [(w_down, sub * 2048, 512 * nb, 16 if sub < 2 else 12)]
    ws = WStream(blocks)
    bi = 0

    for pi, ps in enumerate(PASSES):
        nch, kv, has_s = ps["nch"], ps["kv"], ps["sample"]
        TG = sum(ps["rows"])
        TT = TG + (NSAMP if has_s else 0)
        src = xm if ps["src"] == "xm" else xp
        hT = hT0 if kv else HT
        chunks = [(ps["rows"][c], ps["r0"][c]) for c in range(nch)] + ([(NSAMP, TG)] if has_s else [])
        nchk = len(chunks)
        tiles = []
        ntl = (TT + 511) // 512
        tw = ((TT + ntl - 1) // ntl + 31) // 32 * 32
        c0 = 0
        while c0 < TT:
            n = min(tw, TT - c0)
            segs = []
            if c0 < TG:
                segs.append((0, min(n, TG - c0), False, c0))
            if c0 + n > TG:
                a = max(c0, TG)
                segs.append((a - c0, c0 + n - a, True, a))
            tiles.append((c0, n, segs))
            c0 += n

        def xrows(ci, src=src, ps=ps, nch=nch):
            if ci < nch:
                return src[ps["row0"] + ps["r0"][ci]: ps["row0"] + ps["r0"][ci] + ps["rows"][ci], :]
            return xs[:, :]

        def load_x(ci, chunks=chunks, xrows=xrows):
            slot = ci % 4
            rows = chunks[ci][0]
            P.dma("sp", XT[slot][:rows, :], xrows(ci), ("xt", slot), writes=[("XT", slot)])
            return XT[slot], ("XT", slot)

        if pi == 0:
            norm_transpose(hT, 0, chunks, load_x, after2=lambda: ws.get(0))
            P.barrier()
            enqueue_phaseA(1)
        HR = []
        P.dma("sp", ckt[:, 0:8, :], ck_d[pi], "ckt", writes=[("ckt",)])
        P.dma("sp", skt[:, 0:8, :], sk_d[pi], "skt", writes=[("skt",)])
        NPUMP = 4

        for hp in range(4):
            if not kv:
                qn = 0
                for hh in range(2):
                    h = 2 * hp + hh
                    P.dma("sp", cqt[:, 0:576], cq_d[pi - 1, h][:, 0:576], "cqt", writes=[("cqt",)])
                    P.dma("sp", sqt[:, 0:576], sq_d[pi - 1, h][:, 0:576], "sqt", writes=[("sqt",)])
                    for ti, (t0, tn, _s) in enumerate(tiles):
                        b0_, b1_ = (0, 1) if qn % 2 == 0 else (2, 3)
                        ta, tb2 = (rt1, rt2) if qn % 2 == 0 else (rt3, rt4)
                        qn += 1
                        proj_fm(ws, bi, 2 * hh, hT, HR, t0, tn, b0_, mid=lambda: pump(1, (7, 6, 4, 5)))
                        pump(1, (7, 6, 4, 5))
                        proj_fm(ws, bi, 2 * hh + 1, hT, HR, t0, tn, b1_, mid=lambda: pump(1, (7, 6, 4, 5)))
                        flush_late()
                        pump(1, (7, 6, 4, 5))
                        x1, x2 = PS[b0_][:, 0:tn], PS[b1_][:, 0:tn]
                        cs, sn = cqt[:, t0:t0 + tn], sqt[:, t0:t0 + tn]
                        ka, kb = ("rt", id(ta)), ("rt", id(tb2))
                        P.dve(lambda e, x1=x1, cs=cs, tn=tn, ta=ta: e.tensor_tensor(out=ta[:, 0:tn], in0=x1, in1=cs, op=ALU.mult),
                              reads=[("ps", b0_), ("cqt",)], writes=[ka])
                        P.dve(lambda e, x2=x2, sn=sn, tn=tn, tb2=tb2: e.tensor_tensor(out=tb2[:, 0:tn], in0=x2, in1=sn, op=ALU.mult),
                              reads=[("ps", b1_), ("sqt",)], writes=[kb])
                        P.pool(lambda e, hh=hh, t0=t0, tn=tn, ta=ta, tb2=tb2: e.tensor_tensor(out=qT[:, hh, 0, t0:t0 + tn], in0=ta[:, 0:tn],
                                                                                           in1=tb2[:, 0:tn], op=ALU.subtract),
                               reads=[ka, kb], writes=[("qT", hh)])
                        P.dve(lambda e, x2=x2, cs=cs, tn=tn, ta=ta: e.tensor_tensor(out=ta[:, 0:tn], in0=x2, in1=cs, op=ALU.mult),
                              reads=[("ps", b1_), ("cqt",)], writes=[ka])
                        P.dve(lambda e, x1=x1, sn=sn, tn=tn, tb2=tb2: e.tensor_tensor(out=tb2[:, 0:tn], in0=x1, in1=sn, op=ALU.mult),
                              reads=[("ps", b0_), ("sqt",)], writes=[kb])
                        P.pool(lambda e, hh=hh, t0=t0, tn=tn, ta=ta, tb2=tb2: e.tensor_tensor(out=qT[:, hh, 1, t0:t0 + tn], in0=ta[:, 0:tn],
                                                                                           in1=tb2[:, 0:tn], op=ALU.add),
                               reads=[ka, kb], writes=[("qT", hh)])
                bi += 1
            for ci, (rows, col0) in enumerate(chunks):
                bank = 2 + (ci % 2)
                proj_tm(ws, bi, hT, HR, col0, rows, bank, mid=lambda: pump(1, (7, 6, 4, 5)))
                flush_late()
                pump(1, (7, 6, 4, 5))
                if kv:
                    pumpA(1, every=4)
                for hh in range(2):
                    h = 2 * hp + hh
                    x1 = PS[bank][:rows, hh * 256: hh * 256 + 128]
                    x2 = PS[bank][:rows, hh * 256 + 128: hh * 256 + 256]
                    cs, sn = ckt[:rows, ci, :], skt[:rows, ci, :]
                    sc = ksc[:rows, pi, ci, h:h + 1]
                    rk = [("ps", bank), ("ckt",), ("skt",)]
                    ta, tb2 = (rt1, rt2) if hh == 0 else (rt3, rt4)
                    ka, kb = ("rt", id(ta)), ("rt", id(tb2))

                    def stt(e, o_, a_, b_, sc=sc):
                        return e.scalar_tensor_tensor(out=o_, in0=a_, scalar=sc, in1=b_, op0=ALU.mult, op1=ALU.mult)

                    P.dve(lambda e, x1=x1, cs=cs, rows=rows, stt=stt, ta=ta: stt(e, ta[:rows, 0:128], x1, cs), reads=rk, writes=[ka])
                    P.dve(lambda e, x2=x2, sn=sn, rows=rows, stt=stt, tb2=tb2: stt(e, tb2[:rows, 0:128], x2, sn), reads=rk, writes=[kb])
                    P.dve(lambda e, x2=x2, cs=cs, rows=rows, stt=stt, ta=ta: stt(e, ta[:rows, 128:256], x2, cs), reads=rk, writes=[ka])
                    P.dve(lambda e, x1=x1, sn=sn, rows=rows, stt=stt, tb2=tb2: stt(e, tb2[:rows, 128:256], x1, sn), reads=rk, writes=[kb])
                    P.pool(lambda e, rows=rows, ci=ci, hh=hh, ta=ta, tb2=tb2: e.tensor_tensor(out=kTM[:rows, ci, hh, 0:128], in0=ta[:rows, 0:128],
                                                                                           in1=tb2[:rows, 0:128], op=ALU.subtract),
                           reads=[ka, kb], writes=[("kTM", ci, hh)])
                    P.pool(lambda e, rows=rows, ci=ci, hh=hh, ta=ta, tb2=tb2: e.tensor_tensor(out=kTM[:rows, ci, hh, 128:256], in0=ta[:rows, 128:256],
                                                                                           in1=tb2[:rows, 128:256], op=ALU.add),
                           reads=[ka, kb], writes=[("kTM", ci, hh)])
                    if not kv:
                        def part2(rows=rows, ci=ci, hh=hh, col0=col0):
                            pv = psb(6)

                            def trk(e):
                                ins = None
                                for half in range(2):
                                    ins = e.transpose(pv[:, half, 0:rows], kTM[:rows, ci, hh, half * 128:(half + 1) * 128], ident[:rows, :rows])
                                return ins

                            P.pe(trk, reads=[("kTM", ci, hh)], writes=[("ps", 6)])
                            P.act(lambda e: e.activation(out=kT[:, hh, :, col0:col0 + rows], in_=pv[:, 0:2, 0:rows], func=AF.Copy),
                                  reads=[("ps", 6)], writes=[("kT", hh, ci)])

                        late.append(part2)
            bi += 1
            for hh in range(2):
                h = 2 * hp + hh
                cx = h % 2
                if not kv:
                    for ci, (rows, col0) in enumerate(chunks):
                        bank = 2 + (ci % 2)
                        proj_tm(ws, bi, hT, HR, col0, rows, bank, mid=lambda: pump(2, (7, 6, 4, 5, 0, 1)))
                        flush_late()
                        pump(2, (7, 6, 4, 5, 0, 1))
                        if ci < nch:
                            P.act(lambda e, rows=rows, ci=ci, bank=bank: e.activation(out=sg[:rows, ci, :], in_=PS[bank][:rows, :], func=AF.Silu),
                                  reads=[("ps", bank)], writes=[("sg", ci)])
                        else:
                            P.act(lambda e, rows=rows, cx=cx, bank=bank: e.activation(out=sgs[:rows, cx, :], in_=PS[bank][:rows, :], func=AF.Silu),
                                  reads=[("ps", bank)], writes=[("sgs", cx)])
                    bi += 1
                if pi == 0:
                    P.dve(lambda e: e.memset(base[:], 0.0), writes=[("base",)])
                else:
                    P.dma("sp", base[:].rearrange("p a b -> p (a b)"), sbase_d[h], "base", writes=[("base",)])
                if not kv:
                    P.act(lambda e: e.activation(out=Sb[:, 0], in_=base[:], func=AF.Copy), reads=[("base",)], writes=[("Sb", 0)])
                sbs = 0

                def v_proj(ci, hh=hh, cx=cx):
                    rows, col0 = chunks[ci]
                    bank = 2 + (ci % 2)
                    proj_tm(ws, bi, hT, HR, col0, rows, bank, mid=lambda: pump(1))
                    if ci >= nch:
                        P.act(lambda e: e.activation(out=vs[:rows, cx, :], in_=PS[bank][:rows, :], func=AF.Copy),
                              reads=[("ps", bank)], writes=[("vs", cx)])
                    else:
                        P.act(lambda e: e.activation(out=vS[:rows, ci % 2, :], in_=PS[bank][:rows, :], func=AF.Copy),
                              reads=[("ps", bank)], writes=[("vS", ci % 2)])

                def scores(ci, hh=hh):
                    rw, col0 = chunks[ci]
                    pslot = ci % 2

                    def sc_mm(e):
                        ins = None
                        for half in range(2):
                            ins = e.matmul(PS[4][:rw, 0:rw], lhsT=kT[:, hh, half, col0:col0 + rw], rhs=qT[:, hh, half, col0:col0 + rw],
                                           start=(half == 0), stop=(half == 1))
                        return ins

                    P.pe(sc_mm, reads=[("kT", hh, ci), ("qT", hh)], writes=[("ps", 4)])
                    P.dve(lambda e: e.tensor_tensor(out=PT[:rw, pslot, 0:rw], in0=PS[4][:rw, 0:rw], in1=cmask[:rw, :rw], op=ALU.mult),
                          reads=[("ps", 4)], writes=[("PT", pslot)])

                v_proj(0)
                if not kv:
                    scores(0)
                for ci in range(nch):
                    rows, col0 = chunks[ci]
                    vslot = ci % 2
                    if ci + 1 < nchk:
                        v_proj(ci + 1)
                    flush_late(1)
                    pump(1)
                    if kv:
                        pumpA(1, every=4)
                    if not kv:
                        def o_mm(e, hh=hh, col0=col0, vslot=vslot, sbs=sbs, ci=ci, rows=rows):
                            e.matmul(PS[5][:rows, :], lhsT=PT[:rows, ci % 2, 0:rows], rhs=vS[:rows, vslot, :], start=True, stop=False)
                            ins = None
                            for half in range(2):
                                ins = e.matmul(PS[5][:rows, :], lhsT=qT[:, hh, half, col0:col0 + rows], rhs=Sb[:, sbs, half, :],
                                               start=False, stop=(half == 1))
                            return ins

                        P.pe(o_mm, reads=[("PT", ci % 2), ("vS", vslot), ("qT", hh), ("Sb", sbs)], writes=[("ps", 5)])

                    def s_mm(e, ci=ci, hh=hh, vslot=vslot, rows=rows):
                        ins = None
                        for half in range(2):
                            ins = e.matmul(PS[half][:, :], lhsT=kTM[:rows, ci, hh, half * 128:(half + 1) * 128], rhs=vS[:rows, vslot, :],
                                           start=(ci == 0), stop=True, skip_group_check=True)
                        return ins

                    P.pe(s_mm, reads=[("kTM", ci, hh), ("vS", vslot)], writes=[("ps", 0), ("ps", 1)])
                    pump(1)
                    if not kv and ci + 1 < nch:
                        scores(ci + 1)
                    last_prompt = (ci == nch - 1)
                    if not kv and not last_prompt:
                        nsb = 1 - sbs
                        for half in range(2):
                            P.dve(lambda e, half=half, nsb=nsb: e.tensor_tensor(out=Sb[:, nsb, half, :], in0=PS[half][:, :], in1=base[:, half, :],
                                                                                 op=ALU.add),
                                  reads=[("ps", half), ("base",)], writes=[("Sb", nsb)])
                        sbs = nsb
                    if last_prompt:
                        for half in range(2):
                            P.dve(lambda e, half=half: e.tensor_tensor(out=base[:, half, :], in0=PS[half][:, :], in1=base[:, half, :], op=ALU.add),
                                  reads=[("ps", half), ("base",)], writes=[("base",)])
                        if pi < 2:
                            P.dma("sp", sbase_d[h], base[:].rearrange("p a b -> p (a b)"), "base_o", reads=[("base",)])
                        else:
                            fs = math.exp(LOGG[h] * (2047.0 - TOFF))
                            P.act(lambda e, fs=fs: e.mul(out=base[:], in_=base[:], mul=fs), reads=[("base",)],
                                  writes=[("base",)])
                            P.dma("sp", retp[h].rearrange("(a p) e -> p a e", p=128), base[:], "base_o", reads=[("base",)])
                    if not kv:
                        retention_epilogue(h, rows, col0, PS[5][:rows, :], ("ps", 5), sg[:rows, ci, :], ("sg", ci))
                if has_s:
                    sample_retention(h, hh, nch, TG, cx)
                bi += 1
        flush_late()
        while bg:
            pump(1, (7, 6, 4, 5, 0, 1, 2, 3))
            flush_late()
        pumpA(1000, every=1)
        P.barrier()
        if kv:
            continue

        def conv3(dstP, dstS, up, upkey, dkey, f, wt, bias, TG=TG, has_s=has_s):
            upP = up[:, 0:2 + TG]
            segs = [(dstP, lambda k, upP=upP, TG=TG: upP[:, k:k + TG])]
            if has_s:
                upS = up[:, 642:738].rearrange("p (b t) -> p b t", t=6)
                segs.append((dstS, lambda k, upS=upS: upS[:, :, k:k + 4]))
            for (dd, sl) in segs:
                if bias is None:
                    P.dve(lambda e, dd=dd, sl=sl: e.tensor_scalar(out=dd, in0=sl(0), scalar1=wt[:, f, 0:1], scalar2=None, op0=ALU.mult),
                           reads=[upkey], writes=[dkey])
                else:
                    P.dve(lambda e, dd=dd, sl=sl: e.tensor_scalar(out=dd, in0=sl(0), scalar1=wt[:, f, 0:1], scalar2=bias, op0=ALU.mult,
                                                                   op1=ALU.add),
                           reads=[upkey], writes=[dkey])
                for k in (1, 2):
                    P.dve(lambda e, dd=dd, sl=sl, k=k: e.scalar_tensor_tensor(out=dd, in0=sl(k), scalar=wt[:, f, k:k + 1], in1=dd,
                                                                               op0=ALU.mult, op1=ALU.add),
                           reads=[upkey, dkey], writes=[dkey])

        for fg in range(4):
            for fc in range(4):
                for ti, (t0, tn, _s) in enumerate(tiles):
                    bank = (fc * 2 + ti) % 4
                    proj_fm(ws, bi, fc, hT, HR, t0, tn, bank)
                    P.act(lambda e, fc=fc, t0=t0, tn=tn, bank=bank: e.activation(out=gcs[:, fc, t0:t0 + tn], in_=PS[bank][:, 0:tn], func=AF.Copy),
                          reads=[("ps", bank)], writes=[("gcs", fc)])
            bi += 1
            for fc in range(4):
                f = fg * 4 + fc
                up = UPc[:, fc, :]
                upkey = ("upc", fc)
                P.dve(lambda e, up=up, f=f: e.tensor_copy(out=up[:, 0:2], in_=halo_c[:, f, :]), reads=[("halo_c", f)], writes=[upkey])
                if has_s:
                    P.dve(lambda e, up=up, f=f: e.tensor_copy(out=up[:, 642:738].rearrange("p (b t) -> p b t", t=6)[:, :, 0:2],
                                                              in_=halo_cs[:, f, :, :]), reads=[("halo_cs", f)], writes=[upkey])
                for ti, (t0, tn, segs) in enumerate(tiles):
                    bank = (fc * 2 + ti) % 4
                    proj_fm(ws, bi, fc, hT, HR, t0, tn, bank)
                    for (so, sn, is_s, sc0) in segs:
                        if not is_s:
                            P.dve(lambda e, up=up, fc=fc, so=so, sn=sn, sc0=sc0, bank=bank: e.tensor_tensor(
                                out=up[:, 2 + sc0:2 + sc0 + sn], in0=PS[bank][:, so:so + sn], in1=gcs[:, fc, sc0:sc0 + sn], op=ALU.mult),
                                reads=[("ps", bank), ("gcs", fc)], writes=[upkey])
                        else:
                            P.dve(lambda e, up=up, fc=fc, so=so, sn=sn, sc0=sc0, bank=bank: e.tensor_tensor(
                                out=up[:, 642:738].rearrange("p (b t) -> p b t", t=6)[:, :, 2:6],
                                in0=PS[bank][:, so:so + sn].rearrange("p (b t) -> p b t", t=4),
                                in1=gcs[:, fc, sc0:sc0 + sn].rearrange("p (b t) -> p b t", t=4), op=ALU.mult),
                                reads=[("ps", bank), ("gcs", fc)], writes=[upkey])
                P.dve(lambda e, up=up, f=f, TG=TG: e.tensor_copy(out=halo_c[:, f, :], in_=up[:, TG:TG + 2]), reads=[upkey],
                      writes=[("halo_c", f)])
                if has_s:
                    P.dve(lambda e, up=up, f=f: e.tensor_copy(out=halo_cs[:, f, :, :],
                                                              in_=up[:, 642:738].rearrange("p (b t) -> p b t", t=6)[:, :, 4:6]),
                          reads=[upkey], writes=[("halo_cs", f)])
                conv3(cv[:, fc, 0:TG], cv[:, fc, TG:TG + 64].rearrange("p (b t) -> p b t", t=4) if has_s else None, up, upkey, ("cv", fc), f, cw, None)
            bi += 1
            for fc in range(4):
                f = fg * 4 + fc
                for ti, (t0, tn, _s) in enumerate(tiles):
                    bank = (fc * 2 + ti) % 4
                    proj_fm(ws, bi, fc, hT, HR, t0, tn, bank)
                    P.dve(lambda e, f=f, fc=fc, t0=t0, tn=tn, bank=bank: e.tensor_tensor(
                        out=gbcT[:, f, t0:t0 + tn], in0=PS[bank][:, 0:tn], in1=cv[:, fc, t0:t0 + tn], op=ALU.mult),
                        reads=[("ps", bank), ("cv", fc)], writes=[("gbcT", f)])
            bi += 1
        P.barrier()

        for fg in range(4):
            for (kc0, first) in ((0, True), (16, False)):
                for fc in range(4):
                    for ti, (t0, tn, _s) in enumerate(tiles):
                        bank = (fc * 2 + ti) % 4
                        proj_fm(ws, bi, fc, goT, [], t0, tn, bank, kc0=kc0)
                        if first:
                            P.act(lambda e, fc=fc, t0=t0, tn=tn, bank=bank: e.activation(out=ya[:, fc, t0:t0 + tn], in_=PS[bank][:, 0:tn],
                                                                                        func=AF.Copy),
                                  reads=[("ps", bank)], writes=[("ya", fc)])
                        else:
                            P.dve(lambda e, fc=fc, t0=t0, tn=tn, bank=bank: e.tensor_tensor(out=ya[:, fc, t0:t0 + tn], in0=PS[bank][:, 0:tn],
                                                                                            in1=ya[:, fc, t0:t0 + tn], op=ALU.add),
                                  reads=[("ps", bank), ("ya", fc)], writes=[("ya", fc)])
                bi += 1
            for fc in range(4):
                for ti, (t0, tn, _s) in enumerate(tiles):
                    bank = (fc * 2 + ti) % 4
                    proj_fm(ws, bi, fc, hT, [], t0, tn, bank)
                    P.act(lambda e, fc=fc, t0=t0, tn=tn, bank=bank: e.activation(out=sgm[:, fc, t0:t0 + tn], in_=PS[bank][:, 0:tn],
                                                                                func=AF.Sigmoid),
                          reads=[("ps", bank)], writes=[("sgm", fc)])
                    P.pool(lambda e, fc=fc, t0=t0, tn=tn: e.tensor_tensor(out=ya[:, fc, t0:t0 + tn], in0=ya[:, fc, t0:t0 + tn],
                                                                          in1=sgm[:, fc, t0:t0 + tn], op=ALU.mult),
                           reads=[("ya", fc), ("sgm", fc)], writes=[("ya", fc)])
            bi += 1
            for fc in range(4):
                for ti, (t0, tn, _s) in enumerate(tiles):
                    bank = (fc * 2 + ti) % 4
                    proj_fm(ws, bi, fc, gbcT, [], t0, tn, bank)
                    P.act(lambda e, fc=fc, t0=t0, tn=tn, bank=bank: e.activation(out=yb[:, fc, t0:t0 + tn], in_=PS[bank][:, 0:tn], func=AF.Copy),
                          reads=[("ps", bank)], writes=[("yb", fc)])
            bi += 1
            for fc in range(4):
                f = fg * 4 + fc
                for ti, (t0, tn, _s) in enumerate(tiles):
                    bank = (fc * 2 + ti) % 4
                    proj_fm(ws, bi, fc, hT, [], t0, tn, bank)
                    P.act(lambda e, fc=fc, t0=t0, tn=tn, bank=bank: e.activation(out=sgm[:, fc, t0:t0 + tn], in_=PS[bank][:, 0:tn],
                                                                                func=AF.Sigmoid),
                          reads=[("ps", bank)], writes=[("sgm", fc)])
                    P.pool(lambda e, fc=fc, t0=t0, tn=tn: e.tensor_tensor(out=yb[:, fc, t0:t0 + tn], in0=yb[:, fc, t0:t0 + tn],
                                                                          in1=sgm[:, fc, t0:t0 + tn], op=ALU.mult),
                           reads=[("yb", fc), ("sgm", fc)], writes=[("yb", fc)])
                    P.pool(lambda e, f=f, fc=fc, t0=t0, tn=tn: e.tensor_tensor(out=mT[:, f, t0:t0 + tn], in0=ya[:, fc, t0:t0 + tn],
                                                                               in1=yb[:, fc, t0:t0 + tn], op=ALU.add),
                           reads=[("ya", fc), ("yb", fc)], writes=[("mT", f)])
            bi += 1
        P.barrier()

        P.dma("sp", GP[:, :], gpost_d[0], "gp", writes=[("GP",)])
        for nb in range(4):
            for ci, (rows, col0) in enumerate(chunks):
                bank = ci % 4
                proj_tm(ws, bi, mT, [], col0, rows, bank)
                P.act(lambda e, rows=rows, ci=ci, nb=nb, bank=bank: e.activation(out=XF[:rows, ci, nb * 512:(nb + 1) * 512], in_=PS[bank][:rows, :],
                                                                                func=AF.Copy),
                      reads=[("ps", bank)], writes=[("XF", ci)])
            bi += 1
        for ci, (rows, col0) in enumerate(chunks):
            col = rstd_col(XF[:rows, ci, :], rows, D, [("XF", ci)], dumpA[:rows, :], None)
            P.dve(lambda e, rows=rows, ci=ci, col=col: e.scalar_tensor_tensor(out=XF[:rows, ci, :], in0=XF[:rows, ci, :],
                                                                             scalar=st[:rows, col:col + 1], in1=GP[:rows, :],
                                                                             op0=ALU.mult, op1=ALU.mult),
                  reads=[("XF", ci), ("st", col), ("GP",)], writes=[("XF", ci)])
            P.dma("pool", XF[:rows, ci, :], xrows(ci), ("xacc", ci), reads=[("XF", ci)], writes=[("XF", ci)], accum=True)
            P.dma("sp", xmid_d[ci, 0:rows, :], XF[:rows, ci, :], ("xmo", ci), reads=[("XF", ci)])
        norm_transpose(HT, 1, chunks, lambda ci: (XF[:, ci, :], ("XF", ci)))
        P.barrier()

        for fg in range(11):
            for fc in range(4):
                f = fg * 4 + fc
                up = UPf[:, fc % 2, :]
                upkey = ("upf", fc % 2)
                P.dve(lambda e, up=up, f=f: e.tensor_copy(out=up[:, 0:2], in_=halo_f[:, f, :]), reads=[("halo_f", f)], writes=[upkey])
                if has_s:
                    P.dve(lambda e, up=up, f=f: e.tensor_copy(out=up[:, 642:738].rearrange("p (b t) -> p b t", t=6)[:, :, 0:2],
                                                              in_=halo_fs[:, f, :, :]), reads=[("halo_fs", f)], writes=[upkey])
                for ti, (t0, tn, segs) in enumerate(tiles):
                    bank = (fc * 2 + ti) % 4
                    proj_fm(ws, bi, fc, HT, [], t0, tn, bank)
                    for (so, sn, is_s, sc0) in segs:
                        if not is_s:
                            P.act(lambda e, up=up, so=so, sn=sn, sc0=sc0, bank=bank: e.activation(out=up[:, 2 + sc0:2 + sc0 + sn],
                                                                                                 in_=PS[bank][:, so:so + sn], func=AF.Copy),
                                  reads=[("ps", bank)], writes=[upkey])
                        else:
                            P.act(lambda e, up=up, so=so, sn=sn, bank=bank: e.activation(
                                out=up[:, 642:738].rearrange("p (b t) -> p b t", t=6)[:, :, 2:6],
                                in_=PS[bank][:, so:so + sn].rearrange("p (b t) -> p b t", t=4), func=AF.Copy),
                                reads=[("ps", bank)], writes=[upkey])
                P.dve(lambda e, up=up, f=f, TG=TG: e.tensor_copy(out=halo_f[:, f, :], in_=up[:, TG:TG + 2]), reads=[upkey],
                      writes=[("halo_f", f)])
                if has_s:
                    P.dve(lambda e, up=up, f=f: e.tensor_copy(out=halo_fs[:, f, :, :],
                                                              in_=up[:, 642:738].rearrange("p (b t) -> p b t", t=6)[:, :, 4:6]),
                          reads=[upkey], writes=[("halo_fs", f)])
                conv3(gaB[:, fc, 0:TG], gaB[:, fc, TG:TG + 64].rearrange("p (b t) -> p b t", t=4) if has_s else None, up, upkey,
                      ("ga", fc), f, fcw, fcb[:, f:f + 1])
                P.act(lambda e, fc=fc, TT=TT: e.activation(out=gaB[:, fc, 0:TT], in_=gaB[:, fc, 0:TT], func=AF.Gelu_apprx_tanh),
                      reads=[("ga", fc)], writes=[("ga", fc)])
            bi += 1
            for fc in range(4):
                f = fg * 4 + fc
                for ti, (t0, tn, _s) in enumerate(tiles):
                    bank = (fc * 2 + ti) % 4
                    proj_fm(ws, bi, fc, HT, [], t0, tn, bank)
                    P.dve(lambda e, f=f, fc=fc, t0=t0, tn=tn, bank=bank: e.tensor_tensor(
                        out=aT[:, f, t0:t0 + tn], in0=PS[bank][:, 0:tn], in1=gaB[:, fc, t0:t0 + tn], op=ALU.mult),
                        reads=[("ps", bank), ("ga", fc)], writes=[("aT", f)])
            bi += 1
        P.barrier()

        if pi + 1 < len(PASSES):
            enqueue_phaseA(pi + 1)
        e2n = 0
        for nb in range(4):
            for sub in range(3):
                nkc = 16 if sub < 2 else 12
                for ci, (rows, col0) in enumerate(chunks):
                    proj_tm(ws, bi, aT, [], col0, rows, ci, nkc=nkc, kc0=sub * 16, start=(sub == 0), stop=(sub == 2))
                    pumpA(1, every=5)
                    if sub == 2:
                        P.act(lambda e, rows=rows, ci=ci, nb=nb: e.activation(out=XF[:rows, ci, nb * 512:(nb + 1) * 512], in_=PS[ci][:rows, :],
                                                                             func=AF.Copy),
                              reads=[("ps", ci)], writes=[("XF", ci)])
                bi += 1
        pumpA(1000, every=1)
        P.barrier()
        if pi == len(PASSES) - 1:
            out_rows(lambda f: halo_f[:, f, :], NFF, 2, ffnp)
            out_rows(lambda f: halo_c[:, f, :], 16, 2, convp)
            out_rows(lambda f: halo_fs[:, f, :, :].rearrange("p b t -> p (b t)"), NFF, 32, ffns)
            out_rows(lambda f: halo_cs[:, f, :, :].rearrange("p b t -> p (b t)"), 16, 32, convs)
        P.dma("sp", GP[:, :], gpost_d[1], "gp", writes=[("GP",)])
        def ld_xmid(ci):
            P.dma("sp", XT[ci % 4][:chunks[ci][0], :], xmid_d[ci, 0:chunks[ci][0], :], ("xt", ci % 4), writes=[("XT", ci % 4)])

        for ci in range(min(4, nchk)):
            ld_xmid(ci)
        for ci, (rows, col0) in enumerate(chunks):
            slot = ci % 4
            col = rstd_col(XF[:rows, ci, :], rows, D, [("XF", ci)], junks[1][:rows, :], None)
            P.dve(lambda e, rows=rows, ci=ci, col=col: e.scalar_tensor_tensor(out=XF[:rows, ci, :], in0=XF[:rows, ci, :],
                                                                             scalar=st[:rows, col:col + 1], in1=GP[:rows, :],
                                                                             op0=ALU.mult, op1=ALU.mult),
                  reads=[("XF", ci), ("st", col), ("GP",)], writes=[("XF", ci)])
            P.dve(lambda e, rows=rows, ci=ci, slot=slot: e.tensor_tensor(out=XF[:rows, ci, :], in0=XF[:rows, ci, :], in1=XT[slot][:rows, :],
                                                                         op=ALU.add),
                  reads=[("XF", ci), ("XT", slot)], writes=[("XF", ci)])
            if ci + 4 < nchk:
                ld_xmid(ci + 4)
            if ci < nch:
                dst = None if ps["out"][ci] is None else ym[ps["out"][ci]: ps["out"][ci] + rows, :]
            else:
                dst = ys[:, :]
            if dst is not None:
                P.dma("sp", dst, XF[:rows, ci, :], ("yo", ci), reads=[("XF", ci)])
        P.barrier()

    P.emit()
    return nc


def _tables(core):
    half = core % 2
    f32 = np.float32
    inv = (f32(10000.0) ** (-(np.arange(128, dtype=f32) / f32(128)))).astype(f32)

    def cs(pos):
        ang = (np.asarray(pos, dtype=f32)[:, None] * inv[None, :]).astype(f32)
        return np.cos(ang).astype(np.float64), np.sin(ang).astype(np.float64)

    logg = np.array(LOGG, dtype=np.float64)
    m = np.arange(NMAIN)
    pos_main = np.maximum(half * 1024 - NHALO + m, 0)
    t_main = NPRE + m
    pos_pre = np.arange(NPRE)
    t_pre = np.arange(NPRE)
    tau = np.arange(NSAMP) % 4
    pos_s = 16384 + tau
    cm, sm = cs(pos_main)
    cp, sp_ = cs(pos_pre)
    c_s, s_s = cs(pos_s)

    cq = np.zeros((2, NH, 128, 640), np.float64)
    sq = np.zeros((2, NH, 128, 640), np.float64)
    ck = np.zeros((3, 128, 8, 128), np.float64)
    sk = np.zeros((3, 128, 8, 128), np.float64)
    ksc = np.zeros((128, 3, 8, NH), np.float64)
    for p, ps in enumerate(PASSES):
        csrc, ssrc, tsrc = (cp, sp_, t_pre) if p == 0 else (cm, sm, t_main)
        for ci, (rw, r0) in enumerate(zip(ps["rows"], ps["r0"])):
            a = ps["row0"] + r0
            ck[p, :rw, ci] = csrc[a:a + rw]
            sk[p, :rw, ci] = ssrc[a:a + rw]
            for h in range(NH):
                ksc[:rw, p, ci, h] = np.exp(-logg[h] * (tsrc[a:a + rw] - TOFF)) / 16.0
                if p > 0:
                    dq = np.exp(logg[h] * (tsrc[a:a + rw] - TOFF))
                    cq[p - 1, h, :, r0:r0 + rw] = (csrc[a:a + rw] * dq[:, None]).T
                    sq[p - 1, h, :, r0:r0 + rw] = (ssrc[a:a + rw] * dq[:, None]).T
    ck[2, :64, 4] = c_s
    sk[2, :64, 4] = s_s
    for h in range(NH):
        dq = np.exp(logg[h] * (tau + 1.0))
        cq[1, h, :, 512:576] = (c_s * dq[:, None]).T
        sq[1, h, :, 512:576] = (s_s * dq[:, None]).T
        ksc[:64, 2, 4, h] = np.exp(-logg[h] * (tau + 1.0)) / 16.0
    i = np.arange(128)
    cmask = (i[None, :] >= i[:, None]).astype(f32)
    j64 = np.arange(64)
    bmask = ((j64[None, :] // 4 == j64[:, None] // 4) & (j64[None, :] >= j64[:, None])).astype(f32)
    bm16 = np.broadcast_to((j64[None, :] // 4 == np.arange(NB)[:, None]).astype(f32)[None], (128, NB, 64)).copy()
    bmT = (j64[:, None] // 4 == np.arange(NB)[None, :]).astype(f32)
    return dict(cq=cq.astype(f32), sq=sq.astype(f32), ck=ck.astype(f32), sk=sk.astype(f32), ksc=ksc.astype(f32),
                cmask=cmask, bmask=bmask, bm16=bm16, bmT=bmT, idf=np.eye(128, dtype=f32))


def _in_map(core, x_prompt, x_sample, state_ret, state_conv, state_ffn, g_pre_mix, w_in, conv_w, p_ret, p_conv, w_o, g_post_mix,
            g_pre_ffn, w_up, w_gate, ffn_conv_w, ffn_conv_b, w_down, g_post_ffn):
    f32 = np.float32
    b, half = core // 2, core % 2
    xm = np.zeros((NMAIN, D), f32)
    xp = np.zeros((NPRE, D), f32)
    if half == 0:
        xm[NHALO:] = x_prompt[b, 0:1024]
    else:
        xm[:] = x_prompt[b, 1024 - NHALO:2048]
        xp[:] = x_prompt[b, 0:NPRE]
    sl = slice(core * NB, (core + 1) * NB)
    mp = dict(
        xm=xm, xp=xp, xs=np.ascontiguousarray(x_sample[sl].reshape(NSAMP, D)),
        sret=np.ascontiguousarray(state_ret[0, sl]),
        sconv=np.ascontiguousarray(state_conv[0, sl].reshape(2 * NB, D)),
        sffn=np.ascontiguousarray(state_ffn[0, sl].reshape(2 * NB, DFF)),
        w_in=w_in[0], p_ret=p_ret[0], p_conv=p_conv[0], w_o=w_o[0], w_up=w_up[0], w_gate=w_gate[0], w_down=w_down[0],
        gcol=np.ascontiguousarray(np.stack([g_pre_mix[0].reshape(16, 128).T, g_pre_ffn[0].reshape(16, 128).T], axis=1)),
        gpost=np.ascontiguousarray(np.stack([np.broadcast_to(g_post_mix[0], (128, D)), np.broadcast_to(g_post_ffn[0], (128, D))])),
        cw=np.ascontiguousarray(conv_w[0].T.reshape(16, 128, 3).transpose(1, 0, 2)),
        fcw=np.ascontiguousarray(ffn_conv_w[0].T.reshape(NFF, 128, 3).transpose(1, 0, 2)),
        fcb=np.ascontiguousarray(ffn_conv_b[0].reshape(NFF, 128).T),
    )
    mp.update(_tables(core))
    return {k: np.ascontiguousarray(v, dtype=f32) for k, v in mp.items()}


_NC_CACHE = {}


def _run(inputs, cores):
    if "nc" not in _NC_CACHE:
        _NC_CACHE["nc"] = build_program()
    nc = _NC_CACHE["nc"]
    inputs = {k: np.asarray(v) for k, v in inputs.items()}
    in_maps = [_in_map(c, **inputs) for c in cores]
    res = run_bass_kernel_spmd(nc, in_maps, core_ids=list(range(len(cores))))
    return res.results


def kernel(**inputs):
    f32 = np.float32
    results = _run(inputs, list(range(NCORES)))
    B, S = 4, 2048
    y_prompt = np.zeros((B, S, D), f32)
    y_sample = np.zeros((128, 4, D), f32)
    ret_prompt = np.zeros((1, B, NH, DK, DV), f32)
    conv_prompt = np.zeros((1, B, 2, D), f32)
    ffn_prompt = np.zeros((1, B, 2, DFF), f32)
    ret_sample = np.zeros((1, 128, NH, DK, DV), f32)
    conv_sample = np.zeros((1, 128, 2, D), f32)
    ffn_sample = np.zeros((1, 128, 2, DFF), f32)
    for c, r in enumerate(results):
        b, half = c // 2, c % 2
        y_prompt[b, half * 1024:(half + 1) * 1024] = r["ym"]
        sl = slice(c * NB, (c + 1) * NB)
        y_sample[sl] = r["ys"].reshape(NB, 4, D)
        ret_sample[0, sl] = r["rets"]
        conv_sample[0, sl] = r["convs"].reshape(NB, 2, D)
        ffn_sample[0, sl] = r["ffns"].reshape(NB, 2, DFF)
        if half == 1:
            ret_prompt[0, b] = r["retp"]
            conv_prompt[0, b] = r["convp"]
            ffn_prompt[0, b] = r["ffnp"]
    return (y_prompt, y_sample, ret_prompt, conv_prompt, ffn_prompt, ret_sample, conv_sample, ffn_sample)
```
